# Optimizing a Trainium2 kernel written in Bass

```python
import jax, jax.numpy as jnp
from jax import lax
import numpy as np

D_MODEL = 1024
BATCH = 8
SEQ = 2048
DEPTH = 4

N_MIXERS = 2
N_NSA_LAYERS = (DEPTH + 1) // 2
N_RG_LAYERS = DEPTH // 2
EPS = 1e-6

NSA_HEADS = 16
NSA_GROUPS = 4
NSA_HPG = NSA_HEADS // NSA_GROUPS
HEAD_DIM = 64
CMP_STRIDE = 16
CMP_BLOCK = 2 * CMP_STRIDE
CMP_HIDDEN = 256
SEL_BLOCK = 64
SEL_TOP_N = 8
WINDOW = 512
Q_BLOCK = 128
N_BRANCH = 3
Q_DIM = NSA_HEADS * HEAD_DIM
KV_DIM = NSA_GROUPS * HEAD_DIM
NSA_IN = Q_DIM + 6 * KV_DIM + N_BRANCH * NSA_HEADS

RNN_WIDTH = 1408
RNN_BLOCKS = 16
RNN_BLOCK_W = RNN_WIDTH // RNN_BLOCKS
CONV_WIDTH = 4
RG_C = 8.0

FFN_HIDDEN = ((8 * D_MODEL // 3 + 255) // 256) * 256

kernel_name = "hybrid_nsa_rglru_block"


def rms_norm(x, g):
    xf = x.astype(jnp.float32)
    y = xf * lax.rsqrt(jnp.mean(xf * xf, axis=-1, keepdims=True) + EPS)
    return (y * g.astype(jnp.float32)).astype(x.dtype)


def masked_softmax(s, mask):
    s = jnp.where(mask, s.astype(jnp.float32), -jnp.inf)
    m = jnp.max(s, axis=-1, keepdims=True)
    m = jnp.where(jnp.isfinite(m), m, 0.0)
    p = jnp.exp(s - m)
    return p / jnp.maximum(jnp.sum(p, axis=-1, keepdims=True), jnp.finfo(jnp.float32).tiny)


def cmp_to_sel_map(seq):
    n_c = seq // CMP_STRIDE - 1
    n_s = seq // SEL_BLOCK
    tok = np.arange(seq)
    start = np.arange(n_c) * CMP_STRIDE
    cover_c = (tok[None, :] >= start[:, None]) & (tok[None, :] < start[:, None] + CMP_BLOCK)
    cover_s = (tok[:, None] // SEL_BLOCK) == np.arange(n_s)[None, :]
    m = cover_c.astype(np.float32) @ cover_s.astype(np.float32) / np.float32(CMP_BLOCK)
    return jnp.asarray(m, dtype=jnp.float32)


def compress(t, pos, w1, w2):
    B, S, G, DH = t.shape
    ch = t.reshape(B, S // CMP_STRIDE, CMP_STRIDE, G, DH)
    blocks = jnp.concatenate([ch[:, :-1], ch[:, 1:]], axis=2) + pos[:, None, :]
    flat = jnp.transpose(blocks, (0, 1, 3, 2, 4)).reshape(B, S // CMP_STRIDE - 1, G, CMP_BLOCK * DH)
    return jax.nn.silu(flat @ w1) @ w2


def nsa_mixer(h, w_in, w_out, cmp_pos, cmp_w1, cmp_w2, q_gain, k_gain):
    B, S, _ = h.shape
    G, HPG, DH = NSA_GROUPS, NSA_HPG, HEAD_DIM
    n_c = S // CMP_STRIDE - 1
    n_s = S // SEL_BLOCK
    n_qb = S // Q_BLOCK
    top_k = min(SEL_TOP_N, n_s)
    scale = HEAD_DIM ** -0.5

    proj = h @ w_in
    splits = [Q_DIM + j * KV_DIM for j in range(7)]
    q, k_c, v_c, k_s, v_s, k_w, v_w, g = jnp.split(proj, splits, axis=-1)

    q = rms_norm(q.reshape(B, S, NSA_HEADS, DH), q_gain).reshape(B, S, G, HPG, DH)
    kv = lambda t: t.reshape(B, S, G, DH)
    k_c = rms_norm(compress(kv(k_c), cmp_pos[0], cmp_w1[0], cmp_w2[0]), k_gain[0])
    v_c = compress(kv(v_c), cmp_pos[1], cmp_w1[1], cmp_w2[1])
    k_s = rms_norm(kv(k_s), k_gain[1])
    v_s = kv(v_s)
    k_w = rms_norm(kv(k_w), k_gain[2])
    v_w = kv(v_w)
    gates = jax.nn.sigmoid(g.astype(jnp.float32)).reshape(B, S, G, HPG, N_BRANCH)

    to_blocks = lambda t: jnp.transpose(t.reshape(B, n_s, SEL_BLOCK, G, DH), (0, 3, 1, 2, 4))
    k_sb, v_sb = to_blocks(k_s), to_blocks(v_s)
    pad = ((0, 0), (WINDOW, 0), (0, 0), (0, 0))
    k_wp, v_wp = jnp.pad(k_w, pad), jnp.pad(v_w, pad)
    sel_map = cmp_to_sel_map(S)
    cmp_end = jnp.arange(n_c) * CMP_STRIDE + CMP_BLOCK - 1
    blk = jnp.arange(n_s)
    gather = jax.vmap(jax.vmap(lambda blocks, ix: blocks[ix]))

    def block_fn(args):
        qb, gb, i = args
        t = i * Q_BLOCK + jnp.arange(Q_BLOCK)
        s = jnp.einsum('bqgnd,bcgd->bgnqc', qb, k_c) * scale
        p_c = masked_softmax(s, cmp_end[None, :] <= t[:, None])
        o_c = jnp.einsum('bgnqc,bcgd->bqgnd', p_c.astype(v_c.dtype), v_c)
        imp = jnp.einsum('bgnqc,cs->bgqs', p_c, sel_map)
        cur = (t // SEL_BLOCK)[:, None]
        imp = jnp.where(blk[None, :] <= cur, imp, -jnp.inf)
        forced = (blk[None, :] == 0) | (blk[None, :] == cur) | (blk[None, :] == cur - 1)
        imp = jnp.where(forced, jnp.inf, imp)
        _, idx = lax.top_k(imp, top_k)
        k_sel = gather(k_sb, idx).reshape(B, G, Q_BLOCK, top_k * SEL_BLOCK, DH)
        v_sel = gather(v_sb, idx).reshape(B, G, Q_BLOCK, top_k * SEL_BLOCK, DH)
        kpos = (idx[..., None] * SEL_BLOCK + jnp.arange(SEL_BLOCK)).reshape(B, G, Q_BLOCK, top_k * SEL_BLOCK)
        mask_s = (kpos <= t[None, None, :, None])[:, :, None]
        s = jnp.einsum('bqgnd,bgqjd->bgnqj', qb, k_sel) * scale
        p = masked_softmax(s, mask_s)
        o_s = jnp.einsum('bgnqj,bgqjd->bqgnd', p.astype(v_sel.dtype), v_sel)
        kw = lax.dynamic_slice_in_dim(k_wp, i * Q_BLOCK, WINDOW + Q_BLOCK, axis=1)
        vw = lax.dynamic_slice_in_dim(v_wp, i * Q_BLOCK, WINDOW + Q_BLOCK, axis=1)
        wpos = i * Q_BLOCK - WINDOW + jnp.arange(WINDOW + Q_BLOCK)
        mask_w = (wpos[None, :] <= t[:, None]) & (wpos[None, :] > t[:, None] - WINDOW) & (wpos[None, :] >= 0)
        s = jnp.einsum('bqgnd,bkgd->bgnqk', qb, kw) * scale
        p = masked_softmax(s, mask_w)
        o_w = jnp.einsum('bgnqk,bkgd->bqgnd', p.astype(vw.dtype), vw)
        o = gb[..., 0:1] * o_c + gb[..., 1:2] * o_s + gb[..., 2:3] * o_w
        return o.astype(qb.dtype)

    q_blocks = jnp.moveaxis(q.reshape(B, n_qb, Q_BLOCK, G, HPG, DH), 1, 0)
    g_blocks = jnp.moveaxis(gates.reshape(B, n_qb, Q_BLOCK, G, HPG, N_BRANCH), 1, 0)
    o = lax.map(block_fn, (q_blocks, g_blocks, jnp.arange(n_qb)))
    o = jnp.moveaxis(o, 0, 1).reshape(B, S, Q_DIM)
    return o @ w_out


def lin_rec(left, right):
    a_l, b_l = left
    a_r, b_r = right
    return a_l * a_r, a_r * b_l + b_r


def rglru_mixer(h, w_in, conv_w, conv_b, w_a, b_a, w_x, b_x, lam, w_out):
    B, S, _ = h.shape
    gate, u = jnp.split(h @ w_in, 2, axis=-1)
    u = lax.conv_general_dilated(u, conv_w[:, None, :].astype(u.dtype), window_strides=(1,),
                                 padding=[(CONV_WIDTH - 1, 0)],
                                 dimension_numbers=('NWC', 'WIO', 'NWC'),
                                 feature_group_count=RNN_WIDTH) + conv_b
    ub = u.reshape(B, S, RNN_BLOCKS, RNN_BLOCK_W)
    r = jax.nn.sigmoid(jnp.einsum('bsnd,nde->bsne', ub, w_a) + b_a).reshape(B, S, RNN_WIDTH)
    i_g = jax.nn.sigmoid(jnp.einsum('bsnd,nde->bsne', ub, w_x) + b_x).reshape(B, S, RNN_WIDTH)
    log_a = -RG_C * jax.nn.softplus(-lam.astype(jnp.float32)) * r.astype(jnp.float32)
    a = jnp.exp(log_a)
    b = jnp.sqrt(-jnp.expm1(2.0 * log_a)) * (i_g * u).astype(jnp.float32)
    _, hs = lax.associative_scan(lin_rec, (a, b), axis=1)
    y = hs.astype(h.dtype) * jax.nn.gelu(gate)
    return y @ w_out


def swiglu(h, w_in, w_out):
    g, u = jnp.split(h @ w_in, 2, axis=-1)
    return (jax.nn.silu(g) * u) @ w_out


def setup_inputs(seed: int = 0) -> dict:
    key = jax.random.key(seed)
    ks = jax.random.split(key, 24)
    nrm = lambda k, shape, s: jax.random.normal(k, shape, jnp.float32) * s
    D, NN, NR = D_MODEL, N_NSA_LAYERS, N_RG_LAYERS
    a0 = jax.random.uniform(ks[20], (NR, RNN_WIDTH), jnp.float32, 0.9, 0.999)
    s0 = a0 ** (1.0 / RG_C)
    return {
        "x": nrm(ks[0], (BATCH, SEQ, D), 1.0),
        "c": nrm(ks[1], (BATCH, D), 1.0),
        "ada_w": nrm(ks[2], (DEPTH, D, 6 * D), 0.5 * D ** -0.5),
        "ada_b": nrm(ks[3], (DEPTH, 6 * D), 0.02),
        "norm1_g": 1.0 + nrm(ks[4], (DEPTH, D), 0.1),
        "norm2_g": 1.0 + nrm(ks[5], (DEPTH, D), 0.1),
        "nsa_w_in": nrm(ks[6], (NN, D, NSA_IN), D ** -0.5),
        "nsa_w_out": nrm(ks[7], (NN, Q_DIM, D), Q_DIM ** -0.5),
        "nsa_cmp_pos": nrm(ks[8], (NN, 2, CMP_BLOCK, HEAD_DIM), 0.1),
        "nsa_cmp_w1": nrm(ks[9], (NN, 2, CMP_BLOCK * HEAD_DIM, CMP_HIDDEN), (CMP_BLOCK * HEAD_DIM) ** -0.5),
        "nsa_cmp_w2": nrm(ks[10], (NN, 2, CMP_HIDDEN, HEAD_DIM), CMP_HIDDEN ** -0.5),
        "nsa_q_gain": 1.0 + nrm(ks[11], (NN, HEAD_DIM), 0.1),
        "nsa_k_gain": 1.0 + nrm(ks[12], (NN, N_BRANCH, HEAD_DIM), 0.1),
        "rg_w_in": nrm(ks[13], (NR, D, 2 * RNN_WIDTH), D ** -0.5),
        "rg_conv_w": nrm(ks[14], (NR, CONV_WIDTH, RNN_WIDTH), CONV_WIDTH ** -0.5),
        "rg_conv_b": nrm(ks[15], (NR, RNN_WIDTH), 0.02),
        "rg_w_a": nrm(ks[16], (NR, RNN_BLOCKS, RNN_BLOCK_W, RNN_BLOCK_W), RNN_BLOCK_W ** -0.5),
        "rg_b_a": nrm(ks[17], (NR, RNN_BLOCKS, RNN_BLOCK_W), 0.02),
        "rg_w_x": nrm(ks[18], (NR, RNN_BLOCKS, RNN_BLOCK_W, RNN_BLOCK_W), RNN_BLOCK_W ** -0.5),
        "rg_b_x": nrm(ks[19], (NR, RNN_BLOCKS, RNN_BLOCK_W), 0.02),
        "rg_lam": jnp.log(s0) - jnp.log1p(-s0),
        "rg_w_out": nrm(ks[21], (NR, RNN_WIDTH, D), RNN_WIDTH ** -0.5),
        "ffn_w_in": nrm(ks[22], (DEPTH, D, 2 * FFN_HIDDEN), D ** -0.5),
        "ffn_w_out": nrm(ks[23], (DEPTH, FFN_HIDDEN, D), FFN_HIDDEN ** -0.5),
    }


def reference(x, c, ada_w, ada_b, norm1_g, norm2_g, nsa_w_in, nsa_w_out, nsa_cmp_pos,
              nsa_cmp_w1, nsa_cmp_w2, nsa_q_gain, nsa_k_gain, rg_w_in, rg_conv_w, rg_conv_b,
              rg_w_a, rg_b_a, rg_w_x, rg_b_x, rg_lam, rg_w_out, ffn_w_in, ffn_w_out):
    cond = jax.nn.silu(c)
    for layer in range(DEPTH):
        mod = (cond @ ada_w[layer] + ada_b[layer])[:, None, :]
        sh1, sc1, g1, sh2, sc2, g2 = jnp.split(mod, 6, axis=-1)
        hh = rms_norm(x, norm1_g[layer]) * (1.0 + sc1) + sh1
        j = layer // N_MIXERS
        if layer % N_MIXERS == 0:
            mix = nsa_mixer(hh, nsa_w_in[j], nsa_w_out[j], nsa_cmp_pos[j], nsa_cmp_w1[j],
                            nsa_cmp_w2[j], nsa_q_gain[j], nsa_k_gain[j])
        else:
            mix = rglru_mixer(hh, rg_w_in[j], rg_conv_w[j], rg_conv_b[j], rg_w_a[j], rg_b_a[j],
                              rg_w_x[j], rg_b_x[j], rg_lam[j], rg_w_out[j])
        x = x + g1 * mix
        hh = rms_norm(x, norm2_g[layer]) * (1.0 + sc2) + sh2
        x = x + g2 * swiglu(hh, ffn_w_in[layer], ffn_w_out[layer])
    return x
```

```python
import numpy as np
from contextlib import ExitStack
import concourse.bass as bass
import concourse.mybir as mybir
from concourse.bass_utils import run_bass_kernel_spmd

F32 = mybir.dt.float32
BF16 = mybir.dt.bfloat16
AF = mybir.ActivationFunctionType
ALU = mybir.AluOpType
AX = mybir.AxisListType

D = 1024
S = 2048
DEPTH = 4
NKC = 8
TT = 512
NTT = S // TT
FFN_H = 2816
NHC = FFN_H // 128
EPS = 1e-6
RNN = 1408
RB = 16
RBW = 88
NSA_IN = 2608
DBG = {}


class Sched:
    STREAMS = ("pe", "act", "dve", "pool", "sp")

    def __init__(self, nc, n_lanes=6):
        self.nc = nc
        self.ops = []
        self.lastw = {}
        self.readers = {}
        self.n_lanes = n_lanes
        self.pending = {s: set() for s in self.STREAMS}
        self.last_on = {s: None for s in self.STREAMS}
        self.dma_since_barrier = []

    def op(self, stream, fn, r=(), w=(), dma=False):
        i = len(self.ops)
        deps = set()
        for k in list(r) + list(w):
            if k in self.lastw:
                deps.add(self.lastw[k])
        for k in w:
            deps.update(self.readers.get(k, ()))
        if dma and stream == "pool":
            w = list(w) + ["__swdge_chain"]
            if "__swdge_chain" in self.lastw:
                deps.add(self.lastw["__swdge_chain"])
        deps |= self.pending[stream]
        self.pending[stream] = set()
        deps.discard(i)
        self.ops.append(dict(stream=stream, fn=fn, deps=deps, dma=dma, needed=False))
        for k in w:
            self.lastw[k] = i
            self.readers[k] = []
        for k in r:
            self.readers.setdefault(k, []).append(i)
        self.last_on[stream] = i
        if dma:
            self.dma_since_barrier.append(i)
        return i

    def barrier(self, streams=("pe", "act", "dve", "pool", "sp")):
        deps = set(self.dma_since_barrier)
        for s in streams:
            if self.last_on[s] is not None:
                deps.add(self.last_on[s])
        self.dma_since_barrier = []
        for s in streams:
            self.pending[s] |= deps

    def emit(self, es):
        nc = self.nc
        ops = self.ops
        for o in ops:
            for d in o["deps"]:
                dd = ops[d]
                if dd["stream"] == "pe" and o["stream"] == "pe" and not dd["dma"] and not o["dma"]:
                    continue
                dd["needed"] = True
        sems = {s: es.enter_context(nc.semaphore("sem_" + s)) for s in self.STREAMS}
        lanes = {"hw": [es.enter_context(nc.semaphore("lane%d" % i)) for i in range(self.n_lanes)],
                 "sw": [es.enter_context(nc.semaphore("swlane%d" % i)) for i in range(self.n_lanes)]}
        cnt = {s: 0 for s in self.STREAMS}
        ndma = {"hw": 0, "sw": 0}
        for o in ops:
            if o["dma"]:
                kind = "sw" if o["stream"] == "pool" else "hw"
                lane = ndma[kind] % self.n_lanes
                use = ndma[kind] // self.n_lanes
                o["comp"] = (lanes[kind][lane], 16 * (use + 1))
                o["pre"] = (lanes[kind][lane], 16 * use) if use > 0 else None
                ndma[kind] += 1
            else:
                o["pre"] = None
                if o["needed"]:
                    cnt[o["stream"]] += 1
                    o["comp"] = (sems[o["stream"]], cnt[o["stream"]])
                else:
                    o["comp"] = None
        per_stream = {s: [o for o in ops if o["stream"] == s] for s in self.STREAMS}
        block = es.enter_context(nc.Block())

        def run_stream(eng, lst, sname):
            waited = {}
            for o in lst:
                need = {}
                for d in o["deps"]:
                    dd = ops[d]
                    if dd["stream"] == "pe" and sname == "pe" and not dd["dma"] and not o["dma"]:
                        continue
                    sem, val = dd["comp"]
                    key = id(sem)
                    if key not in need or need[key][1] < val:
                        need[key] = (sem, val)
                if o["pre"] is not None:
                    sem, val = o["pre"]
                    key = id(sem)
                    if key not in need or need[key][1] < val:
                        need[key] = (sem, val)
                for key, (sem, val) in need.items():
                    if waited.get(key, 0) >= val:
                        continue
                    eng.wait_ge(sem, val)
                    waited[key] = val
                ins = o["fn"](eng)
                if o["dma"]:
                    ins.then_inc(o["comp"][0], 16)
                elif o["comp"] is not None:
                    ins.then_inc(o["comp"][0], 1)

        @block.tensor
        def _(e):
            run_stream(e, per_stream["pe"], "pe")

        @block.scalar
        def _(e):
            run_stream(e, per_stream["act"], "act")

        @block.vector
        def _(e):
            run_stream(e, per_stream["dve"], "dve")

        @block.gpsimd
        def _(e):
            run_stream(e, per_stream["pool"], "pool")

        @block.sync
        def _(e):
            run_stream(e, per_stream["sp"], "sp")
            for o in ops:
                if o["dma"] and o.get("final"):
                    e.wait_ge(o["comp"][0], o["comp"][1])


def build_program(layers=(0, 1, 2, 3), do_mixer=True, do_ffn=True):
    nc = bass.Bass("TRN2", target_bir_lowering=False)
    es = ExitStack()
    dram = {}

    def din(name, shape, dt=F32):
        dram[name] = nc.dram_tensor(name, list(shape), dt, kind="ExternalInput").ap()
        return dram[name]

    x_d = din("x", [S, D])
    c_d = din("c", [NKC, 128])
    ada_w = din("ada_w", [DEPTH, D, 6 * D])
    ada_b = din("ada_b", [DEPTH, 48, 128])
    n1g = din("norm1_g", [DEPTH, NKC, 128])
    n2g = din("norm2_g", [DEPTH, NKC, 128])
    ffn_w_in = din("ffn_w_in", [DEPTH, D, 2 * FFN_H])
    ffn_w_out = din("ffn_w_out", [DEPTH, FFN_H, D])
    rg_w_in = din("rg_w_in", [2, D, 2 * RNN])
    rg_conv_w = din("rg_conv_w", [2, 4, RB, RBW])
    rg_conv_b = din("rg_conv_b", [2, RB, RBW])
    rg_w_a = din("rg_w_a", [2, RB, RBW, RBW])
    rg_b_a = din("rg_b_a", [2, RB, RBW])
    rg_w_x = din("rg_w_x", [2, RB, RBW, RBW])
    rg_b_x = din("rg_b_x", [2, RB, RBW])
    rg_lam = din("rg_lam", [2, RB, RBW])
    rg_w_out = din("rg_w_out", [2, RNN, D])
    ident_d = din("ident_in", [128, 128])
    nsa_w_in = din("nsa_w_in", [2, D, NSA_IN])
    nsa_w_out = din("nsa_w_out", [2, D, D])
    nsa_pos = din("nsa_cmp_pos", [2, 2, 32, 64])
    nsa_w1 = din("nsa_cmp_w1", [2, 2, 2048, 256])
    nsa_w2 = din("nsa_cmp_w2", [2, 2, 256, 64])
    nsa_qg = din("nsa_q_gain", [2, 1, 64])
    nsa_kg = din("nsa_k_gain", [2, 3, 64])
    c_w01 = din("c_w01", [128, 1408])
    c_cm01 = din("c_cm01", [128, 2048])
    c_selaug = din("c_selaug", [128, 33])
    c_addmask = din("c_addmask", [128, 512])
    c_emat = din("c_emat", [32, 2048])
    c_gsel = din("c_gsel", [12, 768])
    y_d = nc.dram_tensor("y", [S, D], F32, kind="ExternalOutput").ap()

    sch = Sched(nc)

    def sb(name, shape, dt):
        return es.enter_context(nc.sbuf_tensor("sb_" + name, list(shape), dt))

    xT = sb("xT", [128, NKC, S], F32)
    hh = sb("hh", [128, NKC, S], BF16)
    ident = sb("ident", [128, 128], F32)
    identb = sb("identb", [128, 128], BF16)
    onesb = sb("onesb", [128, 128], BF16)
    condT = sb("condT", [128, NKC], F32)
    condTb = sb("condTb", [128, NKC], BF16)
    c_sb = sb("c_sb", [NKC, 128], F32)
    modT = sb("modT", [128, 48], F32)
    adab_sb = sb("adab_sb", [48, 128], F32)
    ng_sb = sb("ng_sb", [2 * NKC, 128], F32)
    ngT = sb("ngT", [128, 2 * NKC], F32)
    Avec = sb("Avec", [128, 2 * NKC], F32)
    sb_rgv = sb("rgv", [RBW, 128], F32)
    gains = sb("gains", [128, 4], F32)
    blockones = sb("blockones", [128, 128], BF16)
    w2sb = sb("w2sb", [128, 2, 2, 64], BF16)
    posT = sb("posT", [64, 2, 32], BF16)
    ident30k = sb("ident30k", [128, 128], BF16)
    bias_sb = sb("bias_sb", [128, 4], F32)
    NW = 4
    WSLOT = 4096
    wring = [sb("wring%d" % i, [128, WSLOT], BF16) for i in range(NW)]
    ARENA = 36 * 1024
    arena = sb("arena", [128, ARENA], BF16)
    ps = [es.enter_context(nc.psum_tensor("ps%d" % i, [128, 512], F32)) for i in range(8)]

    wctr = [0]

    def wslot():
        i = wctr[0] % NW
        wctr[0] += 1
        return i

    def arena_f32(off_bytes, parts, shape_free):
        n = int(np.prod(shape_free))
        ap = arena[0:parts, off_bytes // 2: off_bytes // 2 + 2 * n].bitcast(F32)
        return ap

    sch.op("sp", lambda e: e.dma_start(out=ident[:], in_=ident_d[:, :]), w=["ident"], dma=True)
    sch.op("sp", lambda e: e.dma_start(out=c_sb[:], in_=c_d[:, :]), w=["c_sb"], dma=True)
    sch.op("dve", lambda e: e.tensor_copy(out=identb[:], in_=ident[:]), r=["ident"], w=["identb"])
    sch.op("dve", lambda e: e.memset(onesb[:], 1.0), w=["onesb"])
    sch.op("dve", lambda e: e.memset(blockones[:], 0.0), w=["blockones"])
    sch.op("dve", lambda e: e.memset(blockones[0:64, 0:64], 1.0), w=["blockones"])
    sch.op("dve", lambda e: e.memset(blockones[64:128, 64:128], 1.0), w=["blockones"])
    sch.op("dve", lambda e: e.tensor_scalar(out=ident30k[:], in0=ident[:], scalar1=30000.0, scalar2=None, op0=ALU.mult),
           r=["ident"], w=["ident30k"])
    sch.op("pe", lambda e: e.transpose(out=ps[0][:, 0:NKC], in_=c_sb[:, :], identity=ident[0:NKC, 0:NKC]),
           r=["c_sb", "ident"], w=[("ps", 0)])
    sch.op("act", lambda e: e.activation(out=condT[:], in_=ps[0][:, 0:NKC], func=AF.Silu),
           r=[("ps", 0)], w=["condT"])
    sch.op("dve", lambda e: e.tensor_copy(out=condTb[:], in_=condT[:]), r=["condT"], w=["condTb"])

    xin = [arena_f32(i * 4096, 128, [D]) for i in range(4)]
    for t in range(S // 128):
        bi = t % 4
        sch.op("sp", lambda e, t=t, bi=bi: e.dma_start(out=xin[bi], in_=x_d[t * 128:(t + 1) * 128, :]),
               w=[("xin", bi)], dma=True)
        for half in range(2):
            pb = (2 * t + half) % 2
            for j in range(4):
                c = half * 4 + j
                sch.op("pe", lambda e, bi=bi, c=c, pb=pb, j=j: e.transpose(
                    out=ps[pb][:, j * 128:(j + 1) * 128], in_=xin[bi][:, c * 128:(c + 1) * 128], identity=ident[:]),
                    r=[("xin", bi), "ident"], w=[("ps", pb)])
            eng = "act" if half == 0 else "dve"

            def cp(e, t=t, half=half, pb=pb, eng=eng):
                out = xT[:, half * 4:(half + 1) * 4, t * 128:(t + 1) * 128]
                in_ = ps[pb][:, :].rearrange("p (j q) -> p j q", j=4)
                if eng == "act":
                    return e.copy(out=out, in_=in_)
                return e.tensor_copy(out=out, in_=in_)
            sch.op(eng, cp, r=[("ps", pb)], w=[("xT", half * 4 + j, t // 4) for j in range(4)])
    sch.barrier()

    def load_w(dram_view, ncols_total, keyname):
        si = wslot()
        dst = wring[si][:, 0:ncols_total]
        return si, dst

    def adaln(l):
        sch.op("sp", lambda e: e.dma_start(out=adab_sb[:], in_=ada_b[l, :, :]), w=["adab_sb"], dma=True)
        sch.op("sp", lambda e: e.dma_start(out=ng_sb[0:NKC, :], in_=n1g[l, :, :]), w=["ng_sb0"], dma=True)
        sch.op("sp", lambda e: e.dma_start(out=ng_sb[NKC:2 * NKC, :], in_=n2g[l, :, :]), w=["ng_sb1"], dma=True)
        PB = 7
        GC = 4
        for g in range(48 // GC):
            si = wslot()
            wv = wring[si][:, 0:NKC * GC * 128].rearrange("p (k c) -> p k c", k=NKC)
            src = ada_w[l, :, g * GC * 128:(g + 1) * GC * 128].rearrange("(k p) c -> p k c", p=128)
            sch.op("pool", lambda e, wv=wv, src=src: e.dma_start(out=wv, in_=src), w=[("w", si, i_) for i_ in range(4)], dma=True)
            for cc in range(GC):
                col = g * GC + cc
                for k in range(NKC):
                    sch.op("pe", lambda e, wv=wv, cc=cc, k=k, col=col: e.matmul(
                        ps[PB][:, col:col + 1], lhsT=wv[:, k, cc * 128:(cc + 1) * 128], rhs=condTb[:, k:k + 1],
                        start=(k == 0), stop=(k == NKC - 1)),
                        r=[("w", si, 0), "condTb"], w=[("ps", PB)])
        sch.op("pe", lambda e: e.transpose(out=ps[PB][:, 64:112], in_=adab_sb[:, :], identity=ident[0:48, 0:48]),
               r=["adab_sb", "ident"], w=[("ps", PB)])
        sch.op("pe", lambda e: e.transpose(out=ps[PB][:, 128:144], in_=ng_sb[:, :], identity=ident[0:16, 0:16]),
               r=["ng_sb0", "ng_sb1", "ident"], w=[("ps", PB)])
        sch.op("dve", lambda e: e.tensor_copy(out=modT[:], in_=ps[PB][:, 64:112]), r=[("ps", PB)], w=["modT"])
        sch.op("dve", lambda e: e.tensor_tensor(out=modT[:], in0=ps[PB][:, 0:48], in1=modT[:], op=ALU.add),
               r=[("ps", PB), "modT"], w=["modT"])
        sch.op("dve", lambda e: e.tensor_copy(out=ngT[:], in_=ps[PB][:, 128:144]), r=[("ps", PB)], w=["ngT"])
        for j, (sccol) in enumerate((8, 32)):
            sch.op("dve", lambda e, j=j, sccol=sccol: e.scalar_tensor_tensor(
                out=Avec[:, j * 8:(j + 1) * 8], in0=modT[:, sccol:sccol + 8], scalar=1.0,
                in1=ngT[:, j * 8:(j + 1) * 8], op0=ALU.add, op1=ALU.mult),
                r=["modT", "ngT"], w=["Avec"])

    def rmsnorm_mod(which):
        shcol = 0 if which == 0 else 24
        sq = [arena[:, ARENA - (i + 1) * 512: ARENA - i * 512] for i in range(2)]
        rstd = [arena_f32(ARENA * 2 - 4096 - (i + 1) * 2048, 128, [512]) for i in range(2)]
        tmp = [arena_f32(ARENA * 2 - 8192 - (i + 1) * 2048, 128, [512]) for i in range(2)]
        n = [0]
        for tt in range(NTT):
            tok = slice(tt * TT, (tt + 1) * TT)
            pb = 5 + (tt % 2)
            for c in range(NKC):
                qi = n[0] % 2
                n[0] += 1
                sch.op("act", lambda e, c=c, tok=tok, qi=qi: e.activation(out=sq[qi], in_=xT[:, c, tok], func=AF.Square),
                       r=[("xT", c, tt)], w=[("sq", qi)])
                sch.op("pe", lambda e, c=c, qi=qi, pb=pb: e.matmul(ps[pb][:, :], lhsT=onesb[:, :], rhs=sq[qi],
                                                                  start=(c == 0), stop=(c == NKC - 1)),
                       r=[("sq", qi), "onesb"], w=[("ps", pb)])
            ri = tt % 2
            sch.op("act", lambda e, pb=pb, ri=ri: e.activation(out=rstd[ri], in_=ps[pb][:, :], func=AF.Sqrt,
                                                               scale=1.0 / D, bias=EPS),
                   r=[("ps", pb)], w=[("rstd", ri)])
            sch.op("dve", lambda e, ri=ri: e.reciprocal(out=rstd[ri], in_=rstd[ri]), r=[("rstd", ri)], w=[("rstd", ri)])
            for c in range(NKC):
                ti = c % 2
                sch.op("dve", lambda e, c=c, tok=tok, ri=ri, ti=ti: e.scalar_tensor_tensor(
                    out=tmp[ti], in0=xT[:, c, tok], scalar=Avec[:, which * 8 + c:which * 8 + c + 1], in1=rstd[ri],
                    op0=ALU.mult, op1=ALU.mult),
                    r=[("xT", c, tt), "Avec", ("rstd", ri)], w=[("ntmp", ti)])
                sch.op("act", lambda e, c=c, tok=tok, ti=ti: e.activation(
                    out=hh[:, c, tok], in_=tmp[ti], func=AF.Identity, bias=modT[:, shcol + c:shcol + c + 1], scale=1.0),
                    r=[("ntmp", ti), "modT"], w=[("hh", c, tt)])


    def rglru(l):
        j = l // 2
        g1col = 16
        rgv_in = arena_f32(0, 128, [RBW])
        rgv = sb_rgv
        sch.op("sp", lambda e: e.dma_start(out=rgv_in[0:64, :], in_=rg_conv_w[j].rearrange("a n w -> (a n) w")),
               w=["rgv_in0"], dma=True)
        for qi, src in enumerate((rg_conv_b, rg_b_a, rg_b_x, rg_lam)):
            sch.op("sp", lambda e, qi=qi, src=src: e.dma_start(out=rgv_in[64 + qi * 16:80 + qi * 16, :], in_=src[j, :, :]),
                   w=["rgv_in%d" % (qi + 1)], dma=True)
        sch.op("pe", lambda e: e.transpose(out=ps[7][0:RBW, 0:128], in_=rgv_in[:, :], identity=ident[:, :]),
               r=["rgv_in%d" % i for i in range(5)] + ["ident"], w=[("ps", 7)])
        sch.op("dve", lambda e: e.tensor_copy(out=rgv[:, :], in_=ps[7][0:RBW, 0:128]), r=[("ps", 7)], w=["rgv"])
        sch.op("act", lambda e: e.activation(out=rgv[:, 112:128], in_=rgv[:, 112:128], func=AF.Exp, scale=-1.0),
               r=["rgv"], w=["rgv"])
        sch.op("act", lambda e: e.activation(out=rgv[:, 112:128], in_=rgv[:, 112:128], func=AF.Ln, bias=1.0, scale=1.0),
               r=["rgv"], w=["rgv"])
        sch.op("dve", lambda e: e.tensor_scalar(out=rgv[:, 112:128], in0=rgv[:, 112:128], scalar1=-8.0, scalar2=None,
                                                op0=ALU.mult), r=["rgv"], w=["rgv"])
        sch.barrier()
        ybuf = arena[0:RBW, 0:4 * S].rearrange("p (n t) -> p n t", n=4)
        base = 4 * S * 2
        G = arena_f32(base, RBW, [S])
        Uraw = arena_f32(base + 8192, RBW, [S + 16])
        Up = arena_f32(base + 2 * 8192 + 64, RBW, [S])
        RA = arena_f32(base + 3 * 8192 + 64, RBW, [S])
        IB = arena_f32(base + 4 * 8192 + 64, RBW, [S])
        MH = arena_f32(base + 5 * 8192 + 64, RBW, [S])
        Upb = arena[0:RBW, (base + 6 * 8192 + 64) // 2:(base + 6 * 8192 + 64) // 2 + S]
        sch.op("dve", lambda e: e.memset(Uraw[:, 0:16], 0.0), w=["Uraw"])
        for q4 in range(4):
            for bi in range(4):
                n = q4 * 4 + bi
                si = wslot()
                wv = wring[si][:, 0:NKC * 176].rearrange("p (k c) -> p k c", k=NKC)
                wg = wring[si][0:RBW, NKC * 176:NKC * 176 + 176]
                sch.op("pool", lambda e, wv=wv, n=n: e.dma_start(
                    out=wv[:, :, 0:RBW], in_=rg_w_in[j, :, n * RBW:(n + 1) * RBW].rearrange("(k p) c -> p k c", p=128)),
                    w=[("w", si, 0)], dma=True)
                sch.op("pool", lambda e, wv=wv, n=n: e.dma_start(
                    out=wv[:, :, RBW:2 * RBW], in_=rg_w_in[j, :, RNN + n * RBW:RNN + (n + 1) * RBW].rearrange("(k p) c -> p k c", p=128)),
                    w=[("w", si, 1)], dma=True)
                sch.op("pool", lambda e, wg=wg, n=n: e.dma_start(out=wg[:, 0:RBW], in_=rg_w_a[j, n, :, :]), w=[("w", si, 2)], dma=True)
                sch.op("pool", lambda e, wg=wg, n=n: e.dma_start(out=wg[:, RBW:2 * RBW], in_=rg_w_x[j, n, :, :]), w=[("w", si, 3)], dma=True)
                for tt in range(NTT):
                    tok = slice(tt * TT, (tt + 1) * TT)
                    pg, pu = (0, 1) if tt % 2 == 0 else (2, 3)
                    for k in range(NKC):
                        sch.op("pe", lambda e, wv=wv, k=k, tok=tok, pg=pg: e.matmul(
                            ps[pg][0:RBW, :], lhsT=wv[:, k, 0:RBW], rhs=hh[:, k, tok], start=(k == 0), stop=(k == NKC - 1)),
                            r=[("w", si, 0), ("hh", k, tt)], w=[("ps", pg)])
                    for k in range(NKC):
                        sch.op("pe", lambda e, wv=wv, k=k, tok=tok, pu=pu: e.matmul(
                            ps[pu][0:RBW, :], lhsT=wv[:, k, RBW:2 * RBW], rhs=hh[:, k, tok], start=(k == 0), stop=(k == NKC - 1)),
                            r=[("w", si, 1), ("hh", k, tt)], w=[("ps", pu)])
                    sch.op("act", lambda e, pg=pg, tok=tok: e.activation(out=G[:, tok], in_=ps[pg][0:RBW, :], func=AF.Gelu_apprx_tanh),
                           r=[("ps", pg)], w=[("G", tt)])
                    sch.op("dve", lambda e, pu=pu, tt=tt: e.tensor_copy(out=Uraw[:, 16 + tt * TT:16 + (tt + 1) * TT], in_=ps[pu][0:RBW, :]),
                           r=[("ps", pu)], w=["Uraw"])
                sch.op("dve", lambda e, n=n: e.tensor_scalar(out=Up[:, :], in0=Uraw[:, 16:16 + S], scalar1=rgv[:, 48 + n:49 + n],
                                                          scalar2=rgv[:, 64 + n:65 + n], op0=ALU.mult, op1=ALU.add),
                       r=["Uraw", "rgv"], w=["Up"])
                for a in range(3):
                    sch.op("dve", lambda e, n=n, a=a: e.scalar_tensor_tensor(
                        out=Up[:, :], in0=Uraw[:, 13 + a:13 + a + S], scalar=rgv[:, a * 16 + n:a * 16 + n + 1], in1=Up[:, :],
                        op0=ALU.mult, op1=ALU.add), r=["Uraw", "rgv", "Up"], w=["Up"])
                sch.op("act", lambda e: e.copy(out=Upb[:, :], in_=Up[:, :]), r=["Up"], w=["Upb"])
                for tt in range(NTT):
                    tok = slice(tt * TT, (tt + 1) * TT)
                    pr, pi = (4, 5) if tt % 2 == 0 else (6, 7)
                    sch.op("pe", lambda e, wg=wg, tok=tok, pr=pr: e.matmul(ps[pr][0:RBW, :], lhsT=wg[:, 0:RBW], rhs=Upb[:, tok],
                                                                       start=True, stop=True),
                           r=[("w", si, 2), "Upb"], w=[("ps", pr)])
                    sch.op("pe", lambda e, wg=wg, tok=tok, pi=pi: e.matmul(ps[pi][0:RBW, :], lhsT=wg[:, RBW:2 * RBW], rhs=Upb[:, tok],
                                                                       start=True, stop=True),
                           r=[("w", si, 3), "Upb"], w=[("ps", pi)])
                    sch.op("act", lambda e, n=n, tok=tok, pr=pr: e.activation(out=RA[:, tok], in_=ps[pr][0:RBW, :], func=AF.Sigmoid,
                                                                         bias=rgv[:, 80 + n:81 + n], scale=1.0),
                           r=[("ps", pr), "rgv"], w=[("RA", tt)])
                    sch.op("act", lambda e, n=n, tok=tok, pi=pi: e.activation(out=IB[:, tok], in_=ps[pi][0:RBW, :], func=AF.Sigmoid,
                                                                         bias=rgv[:, 96 + n:97 + n], scale=1.0),
                           r=[("ps", pi), "rgv"], w=[("IB", tt)])
                allRA = [("RA", tt) for tt in range(NTT)]
                allIB = [("IB", tt) for tt in range(NTT)]
                sch.op("act", lambda e, n=n: e.activation(out=RA[:, :], in_=RA[:, :], func=AF.Exp, scale=rgv[:, 112 + n:113 + n]),
                       r=allRA + ["rgv"], w=allRA)
                sch.op("dve", lambda e: e.tensor_tensor(out=MH[:, :], in0=RA[:, :], in1=RA[:, :], op=ALU.mult), r=allRA, w=["MH"])
                sch.op("act", lambda e: e.activation(out=MH[:, :], in_=MH[:, :], func=AF.Sqrt, scale=-1.0, bias=1.0), r=["MH"], w=["MH"])
                sch.op("dve", lambda e: e.tensor_tensor(out=IB[:, :], in0=IB[:, :], in1=Up[:, :], op=ALU.mult), r=allIB + ["Up"], w=allIB)
                sch.op("dve", lambda e: e.tensor_tensor(out=IB[:, :], in0=IB[:, :], in1=MH[:, :], op=ALU.mult), r=allIB + ["MH"], w=allIB)
                sch.op("dve", lambda e: e.tensor_tensor_scan(out=MH[:, :], data0=RA[:, :], data1=IB[:, :], initial=0.0,
                                                            op0=ALU.mult, op1=ALU.add), r=allRA + allIB + ["MH"], w=["MH"])
                sch.op("dve", lambda e, bi=bi: e.tensor_tensor(out=ybuf[:, bi, :], in0=MH[:, :], in1=G[:, :], op=ALU.mult),
                       r=["MH"] + [("G", tt) for tt in range(NTT)], w=[("ybuf", bi)])
            so = wslot()
            wo = wring[so][0:RBW, 0:4 * D].rearrange("p (n d) -> p n d", n=4)
            for bi in range(4):
                n = q4 * 4 + bi
                sch.op("pool", lambda e, wo=wo, bi=bi, n=n: e.dma_start(out=wo[:, bi, :], in_=rg_w_out[j, n * RBW:(n + 1) * RBW, :]),
                       w=[("w", so, 0), ("w", so, 1), ("w", so, 2), ("w", so, 3)], dma=True)
            for dc in range(NKC):
                for tt in range(NTT):
                    tok = slice(tt * TT, (tt + 1) * TT)
                    pb = (dc * NTT + tt) % 4
                    for bi in range(4):
                        sch.op("pe", lambda e, wo=wo, bi=bi, dc=dc, tok=tok, pb=pb: e.matmul(
                            ps[pb][:, :], lhsT=wo[:, bi, dc * 128:(dc + 1) * 128], rhs=ybuf[:, bi, tok], start=(bi == 0), stop=(bi == 3)),
                            r=[("w", so, 0), ("ybuf", bi)], w=[("ps", pb)])
                    sch.op("dve", lambda e, pb=pb, dc=dc, tok=tok: e.scalar_tensor_tensor(
                        out=xT[:, dc, tok], in0=ps[pb][:, :], scalar=modT[:, g1col + dc:g1col + dc + 1], in1=xT[:, dc, tok],
                        op0=ALU.mult, op1=ALU.add),
                        r=[("ps", pb), "modT", ("xT", dc, tt)], w=[("xT", dc, tt)])

    def wkeys(si):
        return [("w", si, i) for i in range(4)]

    def nsa(l):
        j = l // 2
        g1col = 16
        O_W01, O_CM, O_SA, O_AM, O_Q = 0, 2816, 6912, 7040, 9088
        O_KS, O_KW, O_KC, O_VC, O_X, O_Y = 25472, 29568, 33664, 37760, 41856, 45952
        O_VS, O_VW, O_KCN, O_VCA, O_ACC, O_T = 47488, 51584, 55680, 55936, 56192, 64384

        def rb(off, p0, p1, n):
            return arena[p0:p1, off // 2: off // 2 + n]

        def rf(off, p0, p1, n):
            return arena[p0:p1, off // 2: off // 2 + 2 * n].bitcast(F32)

        W01 = rb(O_W01, 0, 128, 1408)
        cm01 = rb(O_CM, 0, 128, 2048)
        selaug = rb(O_SA, 0, 128, 33)
        addmask = rf(O_AM, 0, 128, 512).rearrange("p (t s) -> p t s", s=32)
        qn = rb(O_Q, 0, 64, 8192).rearrange("p (h t) -> p h t", h=4)
        qx = rb(O_Q, 0, 128, 8192).rearrange("p (h t) -> p h t", h=4)
        nsT = rb(O_Q, 64, 96, 8192).rearrange("p (h t) -> p h t", h=4)
        ks = rb(O_KS, 0, 64, 2048)
        kse = rb(O_KS, 0, 128, 2048)
        emat = rb(O_KS, 64, 96, 2048)
        kw = rb(O_KW, 0, 64, 2048)
        kcraw = rb(O_KC, 0, 64, 2048)
        vcraw = rb(O_VC, 0, 64, 2048)
        oT = [rb(O_KW, 64, 128, 2048), rb(O_KC, 64, 128, 2048), rb(O_VC, 64, 128, 2048), rb(O_X, 64, 128, 2048)]
        sgT = rb(O_X, 0, 12, 2048)
        gsel = rb(O_Y, 0, 12, 768).rearrange("p (a m) -> p a m", m=64)
        Vs = rb(O_VS, 0, 128, 2048).rearrange("p (t c) -> p t c", c=128)
        Vw = rb(O_VW, 0, 128, 2048).rearrange("p (t c) -> p t c", c=128)
        kcn = rb(O_KCN, 0, 64, 128)
        vca = rb(O_VCA, 0, 128, 128)
        acc = rf(O_ACC, 0, 64, 2048).rearrange("p (h t) -> p h t", h=4)
        sq = [rb(O_T + i * 1024, 0, 128, 512) for i in range(2)]
        rinv = [rf(O_T + 2048 + i * 2048, 0, 128, 512) for i in range(2)]
        h1 = rb(O_T + 6144, 0, 128, 256).rearrange("p (a n) -> p a n", a=2)
        stg = rf(O_T + 6656, 0, 32, 128)
        PT = [rb(O_T + i * 1024, 0, 128, 512) for i in range(2)]
        rden = rf(O_T + 2048, 0, 64, 512)
        fac = rf(O_T + 4096, 0, 64, 512)
        tmpo = rf(O_T + 6144, 0, 64, 512)
        impacc = rf(O_T + 8192, 0, 128, 128).rearrange("p (q s) -> p q s", s=32)
        rcp = rf(O_T + 8704, 0, 128, 4)
        top8 = rf(O_T + 8720, 0, 128, 8)
        impm = rf(O_T + 8752, 0, 128, 32)
        selm = rb(O_T + 8880, 0, 128, 32)
        dn = rf(O_T + 8944, 0, 128, 4)

        sch.op("pool", lambda e: e.dma_start(out=W01, in_=c_w01[:, :]), w=["W01"], dma=True)
        sch.op("pool", lambda e: e.dma_start(out=cm01, in_=c_cm01[:, :]), w=["cm01"], dma=True)
        sch.op("pool", lambda e: e.dma_start(out=selaug, in_=c_selaug[:, :]), w=["selaug"], dma=True)
        sch.op("dve", lambda e: e.memset(rb(O_KS, 64, 128, 2048), 0.0), w=["emat"])
        sch.op("dve", lambda e: e.memset(rb(O_Q, 64, 128, 8192), 0.0), w=["nsT_init"])
        sch.op("pool", lambda e: e.dma_start(out=emat, in_=c_emat[:, :]), w=["emat"], dma=True)
        sch.op("pool", lambda e: e.dma_start(out=gsel, in_=c_gsel[:, :].rearrange("p (a m) -> p a m", m=64)), w=["gsel"], dma=True)
        sch.op("sp", lambda e: e.dma_start(out=rf(O_AM, 0, 128, 512), in_=c_addmask[:, :]), w=["addmask"], dma=True)
        for kv in range(2):
            sch.op("pool", lambda e, kv=kv: e.dma_start(out=w2sb[:, kv, :, :],
                                                        in_=nsa_w2[j, kv, :, :].rearrange("(h p) d -> p h d", p=128)),
                   w=[("w2sb", kv)], dma=True)
        sch.op("dve", lambda e: e.memset(Vs[:, :, 64:128], 1.0), w=["VsOnes"])
        sch.op("dve", lambda e: e.memset(Vw[:, :, 64:128], 1.0), w=["VwOnes"])
        sch.op("dve", lambda e: e.memset(vca[:, :], 0.0), w=["vca"])
        sch.op("dve", lambda e: e.memset(vca[:, 64:128], 1.0), w=["vca"])
        sch.op("dve", lambda e: e.memset(kcn[:, :], 0.0), w=["kcn"])
        for hf in range(2):
            sch.op("sp", lambda e, hf=hf: e.dma_start(out=stg[0:1, hf * 64:(hf + 1) * 64], in_=nsa_qg[j, :, :]), w=["stg"], dma=True)
            sch.op("sp", lambda e, hf=hf: e.dma_start(out=stg[1:4, hf * 64:(hf + 1) * 64], in_=nsa_kg[j, :, :]), w=["stg"], dma=True)
        sch.op("pe", lambda e: e.transpose(out=ps[7][:, 0:4], in_=stg[0:4, :], identity=ident[0:4, 0:4]),
               r=["stg", "ident"], w=[("ps", 7)])
        sch.op("dve", lambda e: e.tensor_copy(out=gains[:, :], in_=ps[7][:, 0:4]), r=[("ps", 7)], w=["gains"])
        for kv in range(2):
            sch.op("sp", lambda e, kv=kv: e.dma_start(out=stg[0:32, 0:64], in_=nsa_pos[j, kv, :, :]), w=["stg"], dma=True)
            sch.op("pe", lambda e: e.transpose(out=ps[7][0:64, 0:32], in_=stg[0:32, 0:64], identity=ident[0:32, 0:32]),
                   r=["stg", "ident"], w=[("ps", 7)])
            sch.op("dve", lambda e, kv=kv: e.tensor_copy(out=posT[:, kv, :], in_=ps[7][0:64, 0:32]), r=[("ps", 7)],
                   w=[("posT", kv)])
        sch.barrier()

        cnt = {"st": 0, "ot": 0, "cp": 0}
        if DBG.get("nsa_stop") == "setup":
            return

        def finalize(h, I, b, otb, qs):
            sch.op("dve", lambda e: e.tensor_scalar(out=rden[:, :], in0=ps[otb][64:128, :], scalar1=1e-18, scalar2=None,
                                                    op0=ALU.max), r=[("ps", otb)], w=["rden"])
            sch.op("act", lambda e: e.activation(out=rden[:, :], in_=rden[:, :], func=AF.Ln), r=["rden"], w=["rden"])
            sch.op("act", lambda e: e.activation(out=rden[:, :], in_=rden[:, :], func=AF.Exp, scale=-1.0), r=["rden"], w=["rden"])
            sch.op("pe", lambda e: e.matmul(ps[4][0:64, :], lhsT=gsel[:, h * 3 + b, :], rhs=sgT[:, qs], start=True, stop=True),
                   r=["gsel"] + [("sgT", I)], w=[("ps", 4)])
            sch.op("dve", lambda e: e.tensor_tensor(out=fac[:, :], in0=ps[4][0:64, :], in1=rden[:, :], op=ALU.mult),
                   r=[("ps", 4), "rden"], w=["fac"])
            if b == 0:
                sch.op("dve", lambda e: e.tensor_tensor(out=acc[:, h, :], in0=ps[otb][0:64, :], in1=fac[:, :], op=ALU.mult),
                       r=[("ps", otb), "fac"], w=[("acc", h)])
            elif b == 1:
                sch.op("dve", lambda e: e.tensor_tensor(out=tmpo[:, :], in0=ps[otb][0:64, :], in1=fac[:, :], op=ALU.mult),
                       r=[("ps", otb), "fac"], w=["tmpo"])
                sch.op("dve", lambda e: e.tensor_tensor(out=acc[:, h, :], in0=acc[:, h, :], in1=tmpo[:, :], op=ALU.add),
                       r=[("acc", h), "tmpo"], w=[("acc", h)])
            else:
                sch.op("dve", lambda e: e.tensor_tensor(out=tmpo[:, :], in0=ps[otb][0:64, :], in1=fac[:, :], op=ALU.mult),
                       r=[("ps", otb), "fac"], w=["tmpo"])
                sch.op("dve", lambda e: e.tensor_tensor(out=oT[h][:, qs], in0=acc[:, h, :], in1=tmpo[:, :], op=ALU.add),
                       r=[("acc", h), "tmpo"], w=[("oT", h, I)])

        def do_group(g):
            sA = wslot()
            wA = wring[sA][:, 0:4096].rearrange("p (k c) -> p k c", k=NKC)
            colsA = [(0, 256, g * 256), (256, 64, 1536 + g * 64), (320, 64, 2048 + g * 64),
                     (384, 64, 1024 + g * 64), (448, 64, 1280 + g * 64)]
            for (d0, n, c0) in colsA:
                sch.op("pool", lambda e, d0=d0, n=n, c0=c0: e.dma_start(
                    out=wA[:, :, d0:d0 + n], in_=nsa_w_in[j, :, c0:c0 + n].rearrange("(k p) c -> p k c", p=128)),
                    w=wkeys(sA), dma=True)
            sB = wslot()
            wB = wring[sB][:, 0:NKC * 176].rearrange("p (k c) -> p k c", k=NKC)
            colsB = [(0, 64, 1792 + g * 64), (64, 64, 2304 + g * 64), (128, 48, 2560)]
            for (d0, n, c0) in colsB:
                sch.op("pool", lambda e, d0=d0, n=n, c0=c0: e.dma_start(
                    out=wB[:, :, d0:d0 + n], in_=nsa_w_in[j, :, c0:c0 + n].rearrange("(k p) c -> p k c", p=128)),
                    w=wkeys(sB), dma=True)
            nit = [0]

            def proj_pair(tt, p, dstA, gcA, keyA, dstB, gcB, keyB):
                tok = slice(tt * TT, (tt + 1) * TT)
                it = nit[0]
                nit[0] += 1
                pb = it % 4
                ssb = 4 + (it % 2)
                si_ = it % 2
                for k in range(NKC):
                    sch.op("pe", lambda e, k=k: e.matmul(
                        ps[pb][:, :], lhsT=wA[:, k, p * 128:(p + 1) * 128], rhs=hh[:, k, tok], start=(k == 0), stop=(k == NKC - 1)),
                        r=[("w", sA, 0), ("hh", k, tt)], w=[("ps", pb)])
                if gcA is not None:
                    sch.op("act", lambda e: e.activation(out=sq[si_], in_=ps[pb][:, :], func=AF.Square),
                           r=[("ps", pb)], w=[("sq", si_)])
                    sch.op("pe", lambda e: e.matmul(ps[ssb][:, :], lhsT=blockones[:, :], rhs=sq[si_], start=True, stop=True),
                           r=[("sq", si_), "blockones"], w=[("ps", ssb)])
                    sch.op("act", lambda e: e.activation(out=rinv[si_], in_=ps[ssb][:, :], func=AF.Ln, scale=1.0 / 64, bias=EPS),
                           r=[("ps", ssb)], w=[("rinv", si_)])
                    sch.op("act", lambda e: e.activation(out=rinv[si_], in_=rinv[si_], func=AF.Exp, scale=-0.5),
                           r=[("rinv", si_)], w=[("rinv", si_)])
                    sch.op("dve", lambda e: e.scalar_tensor_tensor(
                        out=dstA, in0=ps[pb][0:64, :], scalar=gains[0:64, gcA:gcA + 1], in1=rinv[si_][0:64, :],
                        op0=ALU.mult, op1=ALU.mult),
                        r=[("ps", pb), "gains", ("rinv", si_)], w=[keyA])
                    sch.op("dve", lambda e: e.scalar_tensor_tensor(
                        out=dstB, in0=ps[pb][64:128, :], scalar=gains[64:128, gcB:gcB + 1], in1=rinv[si_][64:128, :],
                        op0=ALU.mult, op1=ALU.mult),
                        r=[("ps", pb), "gains", ("rinv", si_)], w=[keyB])
                else:
                    sch.op("act", lambda e: e.copy(out=dstA, in_=ps[pb][0:64, :]), r=[("ps", pb)], w=[keyA])
                    sch.op("dve", lambda e: e.tensor_scalar(out=dstB, in0=ps[pb][64:128, :], scalar1=1.0, scalar2=None, op0=ALU.mult),
                           r=[("ps", pb)], w=[keyB])

            def proj_gates(tt):
                tok = slice(tt * TT, (tt + 1) * TT)
                for k in range(NKC):
                    sch.op("pe", lambda e, k=k: e.matmul(ps[7][0:12, :], lhsT=wB[:, k, 128 + g * 12:140 + g * 12], rhs=hh[:, k, tok],
                                                         start=(k == 0), stop=(k == NKC - 1)),
                           r=[("w", sB, 0), ("hh", k, tt)], w=[("ps", 7)])
                sch.op("act", lambda e: e.activation(out=sgT[:, tok], in_=ps[7][0:12, :], func=AF.Sigmoid),
                       r=[("ps", 7)], w=[("sgT", tt)])

            for tt in range(0 if not DBG.get("proj_nomm") else NTT, DBG.get("proj_ntt", NTT)):
                tok = slice(tt * TT, (tt + 1) * TT)
                proj_pair(tt, 0, qn[:, 0, tok], 0, ("qn", 0, tt), qn[:, 1, tok], 0, ("qn", 1, tt))
                proj_pair(tt, 1, qn[:, 2, tok], 0, ("qn", 2, tt), qn[:, 3, tok], 0, ("qn", 3, tt))
                proj_pair(tt, 2, ks[:, tok], 2, ("ks", tt), kw[:, tok], 3, ("kw", tt))
                proj_pair(tt, 3, kcraw[:, tok], None, "kcraw", vcraw[:, tok], None, "vcraw")
                if not DBG.get("proj_nogates"):
                    proj_gates(tt)

            vtmp = rf(O_T + 7168, 0, 128, 512)

            def proj_v(tt):
                tok = slice(tt * TT, (tt + 1) * TT)
                it = nit[0]
                nit[0] += 1
                pb = it % 4
                for k in range(NKC):
                    sch.op("pe", lambda e, k=k: e.matmul(
                        ps[pb][:, :], lhsT=wB[:, k, 0:128], rhs=hh[:, k, tok], start=(k == 0), stop=(k == NKC - 1)),
                        r=[("w", sB, 0), ("hh", k, tt)], w=[("ps", pb)])
                sch.op("act", lambda e: e.copy(out=vtmp[:, :], in_=ps[pb][:, :]), r=[("ps", pb)], w=["vtmp"])
                for jq in range(4):
                    sch.op("pe", lambda e, jq=jq: e.transpose(out=ps[6][:, jq * 128:(jq + 1) * 128], in_=vtmp[:, jq * 128:(jq + 1) * 128],
                                                            identity=ident[:, :]),
                           r=["vtmp", "ident"], w=[("ps", 6)])
                p6 = ps[6][:, :].rearrange("p (q c) -> p q c", c=128)
                if DBG.get("v_nocopy"):
                    return
                sch.op("act", lambda e: e.copy(out=Vs[:, 4 * tt:4 * tt + 4, 0:64], in_=p6[:, :, 0:64]), r=[("ps", 6), "VsOnes"],
                       w=[("Vs", 4 * tt + q_) for q_ in range(4)])
                if DBG.get("v_onecopy"):
                    return
                sch.op("act", lambda e: e.copy(out=Vw[:, 4 * tt:4 * tt + 4, 0:64], in_=p6[:, :, 64:128]), r=[("ps", 6), "VwOnes"],
                       w=[("Vw", 4 * tt + q_) for q_ in range(4)])

            for tt in range(NTT):
                if not DBG.get("proj_nov"):
                    proj_v(tt)
            if DBG.get("nsa_stop") == "proj":
                sch.barrier()
                return

            def compress(kv):
                raw = kcraw if kv == 0 else vcraw
                rawkey = "kcraw" if kv == 0 else "vcraw"
                raw3 = raw.rearrange("p (n s) -> p n s", s=16)
                w1 = []
                for hf in range(2):
                    s1 = wslot()
                    wv1 = wring[s1][0:64, 0:4096].rearrange("p (t c) -> p t c", t=16)
                    sch.op("pool", lambda e, wv1=wv1, hf=hf: e.dma_start(
                        out=wv1, in_=nsa_w1[j, kv, hf * 1024:(hf + 1) * 1024, :].rearrange("(t p) c -> p t c", p=64)),
                        w=wkeys(s1), dma=True)
                    w1.append((s1, wv1))

                def c_half(half):
                    for tau in range(32):
                        s1, wv1 = w1[tau // 16]
                        wsrc = wv1[:, tau % 16, half * 128:(half + 1) * 128]
                        n0, sidx = (0, tau) if tau < 16 else (1, tau - 16)
                        sch.op("pe", lambda e, wsrc=wsrc, n0=n0, sidx=sidx, tau=tau: e.matmul(
                            ps[half][:, 0:127], lhsT=wsrc, rhs=raw3[:, n0:n0 + 127, sidx], start=(tau == 0), stop=(tau == 31)),
                            r=[("w", s1, 0), rawkey], w=[("ps", half)])
                        sch.op("pe", lambda e, wsrc=wsrc, tau=tau: e.matmul(
                            ps[2][:, half:half + 1], lhsT=wsrc, rhs=posT[:, kv, tau:tau + 1], start=(tau == 0), stop=(tau == 31)),
                            r=[("w", s1, 0), ("posT", kv)], w=[("ps", 2)])
                    col = kv * 2 + half
                    sch.op("dve", lambda e: e.tensor_copy(out=bias_sb[:, col:col + 1], in_=ps[2][:, half:half + 1]),
                           r=[("ps", 2)], w=[("bias_sb", col)])
                    sch.op("act", lambda e: e.activation(out=h1[:, half, 0:127], in_=ps[half][:, 0:127],
                                                         func=AF.Silu, bias=bias_sb[:, col:col + 1], scale=1.0),
                           r=[("ps", half), ("bias_sb", col)], w=[("h1", half)])

                c_half(0)
                c_half(1)
                if kv == 0:
                    for half in range(2):
                        sch.op("pe", lambda e, half=half: e.matmul(ps[3][0:64, 0:127], lhsT=w2sb[:, 0, half, :], rhs=h1[:, half, 0:127],
                                                                 start=(half == 0), stop=(half == 1)),
                               r=[("w2sb", 0), ("h1", half)], w=[("ps", 3)])
                    sch.op("act", lambda e: e.activation(out=sq[0][0:64, 0:127], in_=ps[3][0:64, 0:127], func=AF.Square),
                           r=[("ps", 3)], w=[("sq", 0)])
                    sch.op("pe", lambda e: e.matmul(ps[4][0:64, 0:127], lhsT=onesb[0:64, 0:64], rhs=sq[0][0:64, 0:127], start=True, stop=True),
                           r=[("sq", 0), "onesb"], w=[("ps", 4)])
                    sch.op("act", lambda e: e.activation(out=rinv[0][0:64, 0:127], in_=ps[4][0:64, 0:127], func=AF.Ln,
                                                         scale=1.0 / 64, bias=EPS), r=[("ps", 4)], w=[("rinv", 0)])
                    sch.op("act", lambda e: e.activation(out=rinv[0][0:64, 0:127], in_=rinv[0][0:64, 0:127], func=AF.Exp, scale=-0.5),
                           r=[("rinv", 0)], w=[("rinv", 0)])
                    sch.op("dve", lambda e: e.scalar_tensor_tensor(out=kcn[:, 0:127], in0=ps[3][0:64, 0:127], scalar=gains[0:64, 1:2],
                                                                  in1=rinv[0][0:64, 0:127], op0=ALU.mult, op1=ALU.mult),
                           r=[("ps", 3), "gains", ("rinv", 0)], w=["kcn"])
                else:
                    for half in range(2):
                        sch.op("pe", lambda e, half=half: e.matmul(ps[3][0:127, 0:64], lhsT=h1[:, half, 0:127], rhs=w2sb[:, 1, half, :],
                                                                 start=(half == 0), stop=(half == 1)),
                               r=[("w2sb", 1), ("h1", half)], w=[("ps", 3)])
                    sch.op("act", lambda e: e.copy(out=vca[0:127, 0:64], in_=ps[3][0:127, 0:64]), r=[("ps", 3)], w=["vca"])

            compress(0)
            compress(1)
            sch.barrier()
            if DBG.get("nsa_stop") == "compress":
                return

            def comp_head(I, h):
                qs = slice(512 * I, 512 * I + 512)
                stb = cnt["st"] % 2
                cnt["st"] += 1
                otb = 2 + cnt["ot"] % 2
                cnt["ot"] += 1
                sch.op("pe", lambda e: e.matmul(ps[stb][:, :], lhsT=kcn[:, :], rhs=qn[:, h, qs], start=True, stop=True),
                       r=["kcn", ("qn", h, I)], w=[("ps", stb)])
                sch.op("act", lambda e: e.activation(out=PT[stb], in_=ps[stb][:, :], func=AF.Exp, scale=0.125),
                       r=[("ps", stb)], w=[("PT", stb)])
                sch.op("pool", lambda e: e.tensor_tensor(out=PT[stb], in0=PT[stb], in1=cm01[:, qs], op=ALU.mult),
                       r=[("PT", stb), "cm01"], w=[("PT", stb)])
                sch.op("pe", lambda e: e.matmul(ps[otb][:, :], lhsT=vca[:, :], rhs=PT[stb], start=True, stop=True),
                       r=["vca", ("PT", stb)], w=[("ps", otb)])
                for qb in range(4):
                    sch.op("pe", lambda e, qb=qb: e.matmul(ps[5][:, qb * 33:(qb + 1) * 33], lhsT=PT[stb][:, qb * 128:(qb + 1) * 128],
                                                           rhs=selaug[:, 0:33], start=True, stop=True),
                           r=[("PT", stb), "selaug"], w=[("ps", 5)])
                imp3 = ps[5][:, 0:132].rearrange("p (q c) -> p q c", c=33)
                sch.op("dve", lambda e: e.tensor_scalar(out=dn[:, :], in0=imp3[:, :, 32], scalar1=1e-30, scalar2=None, op0=ALU.max),
                       r=[("ps", 5)], w=["dn"])
                sch.op("dve", lambda e: e.reciprocal(out=rcp[:, :], in_=dn[:, :]), r=["dn"], w=["rcp"])
                for qb in range(4):
                    if h == 0:
                        sch.op("dve", lambda e, qb=qb: e.tensor_scalar(out=impacc[:, qb, :], in0=imp3[:, qb, 0:32],
                                                                       scalar1=rcp[:, qb:qb + 1], scalar2=None, op0=ALU.mult),
                               r=[("ps", 5), "rcp"], w=[("impacc", qb)])
                    else:
                        sch.op("dve", lambda e, qb=qb: e.scalar_tensor_tensor(
                            out=impacc[:, qb, :], in0=imp3[:, qb, 0:32], scalar=rcp[:, qb:qb + 1], in1=impacc[:, qb, :],
                            op0=ALU.mult, op1=ALU.add), r=[("ps", 5), "rcp", ("impacc", qb)], w=[("impacc", qb)])
                finalize(h, I, 0, otb, qs)

            def topk(I, qb):
                t = 4 * I + qb
                sch.op("dve", lambda e: e.tensor_tensor(out=impm[:, :], in0=impacc[:, qb, :], in1=addmask[:, t, :], op=ALU.add),
                       r=[("impacc", qb), "addmask"], w=["impm"])
                sch.op("dve", lambda e: e.max(out=top8[:, :], in_=impm[:, :]), r=["impm"], w=["top8"])
                sch.op("dve", lambda e: e.tensor_scalar(out=selm[:, :], in0=impm[:, :], scalar1=top8[:, 7:8], scalar2=-1.0,
                                                        op0=ALU.is_ge, op1=ALU.add), r=["impm", "top8"], w=["selm"])
                sch.op("pe", lambda e: e.matmul(ps[6][0:32, 0:128], lhsT=selm[:, :], rhs=ident30k[:, :], start=True, stop=True),
                       r=["selm", "ident30k"], w=[("ps", 6)])
                for h in range(4):
                    if True:
                        sch.op("act", lambda e, h=h: e.copy(out=nsT[:, h, t * 128:(t + 1) * 128], in_=ps[6][0:32, 0:128]),
                               r=[("ps", 6)], w=[("nsT", h, t)])
                    else:
                        sch.op("dve", lambda e, h=h: e.tensor_scalar(out=nsT[:, h, t * 128:(t + 1) * 128], in0=ps[6][0:32, 0:128],
                                                                     scalar1=1.0, scalar2=None, op0=ALU.mult),
                               r=[("ps", 6)], w=[("nsT", h, t)])

            def sel_tile(I, h, jj, otb, nj):
                r_ = jj - 4 * I
                c0 = 128 * r_ if r_ > 0 else 0
                stb = cnt["st"] % 2
                cnt["st"] += 1
                sch.op("pe", lambda e: e.matmul(
                    ps[stb][:, c0:512], lhsT=kse[:, jj * 128:(jj + 1) * 128], rhs=qx[:, h, 512 * I + c0:512 * I + 512],
                    start=True, stop=True),
                    r=[("ks", jj // 4), "emat", ("qn", h, I)] + [("nsT", h, 4 * I + q_) for q_ in range(4)], w=[("ps", stb)])
                sch.op("act", lambda e: e.activation(out=PT[stb][:, c0:512], in_=ps[stb][:, c0:512], func=AF.Exp, scale=0.125),
                       r=[("ps", stb)], w=[("PT", stb)])
                if r_ >= 0:
                    off = 512 * I - 128 * jj + 384
                    sch.op("pool", lambda e: e.tensor_tensor(
                        out=PT[stb][:, c0:512], in0=PT[stb][:, c0:512], in1=W01[:, off + c0:off + 512], op=ALU.mult),
                        r=[("PT", stb), "W01"], w=[("PT", stb)])
                sch.op("pe", lambda e: e.matmul(
                    ps[otb][:, c0:512], lhsT=Vs[:, jj, :], rhs=PT[stb][:, c0:512], start=(jj == 0), stop=(jj == nj - 1)),
                    r=[("Vs", jj), ("PT", stb)], w=[("ps", otb)])

            def win_tile(I, h, jj, otb, j0):
                m = jj - (4 * I - 4)
                r_lo = max(0, m - 4)
                r_hi = min(3, m)
                c0, c1 = 128 * r_lo, 128 * (r_hi + 1)
                off = 512 * I - 128 * jj + 384
                stb = cnt["st"] % 2
                cnt["st"] += 1
                sch.op("pe", lambda e: e.matmul(
                    ps[stb][:, c0:c1], lhsT=kw[:, jj * 128:(jj + 1) * 128], rhs=qn[:, h, 512 * I + c0:512 * I + c1],
                    start=True, stop=True),
                    r=[("kw", jj // 4), ("qn", h, I)], w=[("ps", stb)])
                sch.op("act", lambda e: e.activation(out=PT[stb][:, c0:c1], in_=ps[stb][:, c0:c1], func=AF.Exp, scale=0.125),
                       r=[("ps", stb)], w=[("PT", stb)])
                sch.op("pool", lambda e: e.tensor_tensor(
                    out=PT[stb][:, c0:c1], in0=PT[stb][:, c0:c1], in1=W01[:, off + c0:off + c1], op=ALU.mult),
                    r=[("PT", stb), "W01"], w=[("PT", stb)])
                sch.op("pe", lambda e: e.matmul(
                    ps[otb][:, c0:c1], lhsT=Vw[:, jj, :], rhs=PT[stb][:, c0:c1], start=(jj == j0), stop=(jj == 4 * I + 3),
                    skip_group_check=True),
                    r=[("Vw", jj), ("PT", stb)], w=[("ps", otb)])

            def sel_win_head(I, h):
                qs = slice(512 * I, 512 * I + 512)
                otb = 2 + cnt["ot"] % 2
                cnt["ot"] += 1
                nj = 4 * I + 4
                for jj in range(nj):
                    sel_tile(I, h, jj, otb, nj)
                finalize(h, I, 1, otb, qs)
                otb = 2 + cnt["ot"] % 2
                cnt["ot"] += 1
                j0 = max(0, 4 * I - 4)
                for jj in range(j0, 4 * I + 4):
                    win_tile(I, h, jj, otb, j0)
                finalize(h, I, 2, otb, qs)

            for I in range(DBG.get("nsa_nI", 4)):
                for h in range(4):
                    comp_head(I, h)
                for qb in range(4):
                    topk(I, qb)
                for h in range(4):
                    sel_win_head(I, h)

            so = wslot()
            wo = wring[so][64:128, 0:4096].rearrange("p (h d) -> p h d", h=4)
            for h in range(4):
                sch.op("pool", lambda e, h=h: e.dma_start(
                    out=wo[:, h, :], in_=nsa_w_out[j, (4 * g + h) * 64:(4 * g + h + 1) * 64, :]), w=wkeys(so), dma=True)

            def outproj(dc, tt):
                tok = slice(tt * TT, (tt + 1) * TT)
                pb = (dc * NTT + tt) % 2
                for h in range(4):
                    sch.op("pe", lambda e, h=h: e.matmul(
                        ps[pb][:, :], lhsT=wo[:, h, dc * 128:(dc + 1) * 128], rhs=oT[h][:, tok], start=(h == 0), stop=(h == 3)),
                        r=[("w", so, 0), ("oT", h, tt)], w=[("ps", pb)])
                sch.op("dve", lambda e: e.scalar_tensor_tensor(
                    out=xT[:, dc, tok], in0=ps[pb][:, :], scalar=modT[:, g1col + dc:g1col + dc + 1], in1=xT[:, dc, tok],
                    op0=ALU.mult, op1=ALU.add),
                    r=[("ps", pb), "modT", ("xT", dc, tt)], w=[("xT", dc, tt)])

            for dc in range(NKC):
                for tt in range(NTT):
                    outproj(dc, tt)
            sch.barrier()

        for g in range(DBG.get("ngroups", 4)):
            do_group(g)


    def ffn(l):
        gcol = 40
        HT = 1024
        hbuf = arena[:, 0:NHC * HT].rearrange("p (c t) -> p c t", c=NHC)
        sg = [arena_f32(NHC * HT * 2 + i * 2048, 128, [512]) for i in range(2)]
        for half in range(DBG.get("halves", 2)):
            for cp in range(DBG.get("ncp", NHC // 2)):
                si = wslot()
                wv = wring[si][:, 0:NKC * 512].rearrange("p (k c) -> p k c", k=NKC)
                srcg = ffn_w_in[l, :, cp * 256:(cp + 1) * 256].rearrange("(k p) c -> p k c", p=128)
                srcu = ffn_w_in[l, :, FFN_H + cp * 256:FFN_H + (cp + 1) * 256].rearrange("(k p) c -> p k c", p=128)
                sch.op("pool", lambda e, wv=wv, srcg=srcg: e.dma_start(out=wv[:, :, 0:256], in_=srcg),
                       w=[("w", si, 0), ("w", si, 2), ("w", si, 3)], dma=True)
                sch.op("pool", lambda e, wv=wv, srcu=srcu: e.dma_start(out=wv[:, :, 256:512], in_=srcu),
                       w=[("w", si, 1)], dma=True)
                for ci in range(2):
                    c = cp * 2 + ci
                    for t2 in range(2):
                        tt = half * 2 + t2
                        tok = slice(tt * TT, (tt + 1) * TT)
                        pg = (2 * (ci * 2 + t2)) % 4
                        pu = pg + 1
                        for k in range(NKC):
                            sch.op("pe", lambda e, wv=wv, k=k, ci=ci, tok=tok, pg=pg: e.matmul(
                                ps[pg][:, :], lhsT=wv[:, k, ci * 128:(ci + 1) * 128], rhs=hh[:, k, tok],
                                start=(k == 0), stop=(k == NKC - 1)),
                                r=[("w", si, 0), ("hh", k, tt)], w=[("ps", pg)])
                        for k in range(NKC):
                            sch.op("pe", lambda e, wv=wv, k=k, ci=ci, tok=tok, pu=pu: e.matmul(
                                ps[pu][:, :], lhsT=wv[:, k, 256 + ci * 128:256 + (ci + 1) * 128], rhs=hh[:, k, tok],
                                start=(k == 0), stop=(k == NKC - 1)),
                                r=[("w", si, 1), ("hh", k, tt)], w=[("ps", pu)])
                        gi = (ci * 2 + t2) % 2
                        sch.op("act", lambda e, pg=pg, gi=gi: e.activation(out=sg[gi], in_=ps[pg][:, :], func=AF.Silu),
                               r=[("ps", pg)], w=[("sg", gi)])
                        sch.op("dve", lambda e, pu=pu, gi=gi, c=c, t2=t2: e.tensor_tensor(
                            out=hbuf[:, c, t2 * TT:(t2 + 1) * TT], in0=ps[pu][:, :], in1=sg[gi], op=ALU.mult),
                            r=[("ps", pu), ("sg", gi)], w=[("hbuf", c, t2)])
            for dc in range(DBG.get("ndc", NKC)):
                si = wslot()
                wv = wring[si][:, 0:NHC * 128].rearrange("p (c d) -> p c d", c=NHC)
                src = ffn_w_out[l, :, dc * 128:(dc + 1) * 128].rearrange("(c p) d -> p c d", p=128)
                sch.op("pool", lambda e, wv=wv, src=src: e.dma_start(out=wv[:, 0:11, :], in_=src[:, 0:11, :]),
                       w=[("w", si, 0), ("w", si, 2), ("w", si, 3)], dma=True)
                sch.op("pool", lambda e, wv=wv, src=src: e.dma_start(out=wv[:, 11:22, :], in_=src[:, 11:22, :]),
                       w=[("w", si, 1)], dma=True)
                for t2 in range(2):
                    tt = half * 2 + t2
                    tok = slice(tt * TT, (tt + 1) * TT)
                    pb = 4 + ((dc * 2 + t2) % 2)
                    for c in range(NHC):
                        sch.op("pe", lambda e, wv=wv, c=c, t2=t2, pb=pb: e.matmul(
                            ps[pb][:, :], lhsT=wv[:, c, :], rhs=hbuf[:, c, t2 * TT:(t2 + 1) * TT],
                            start=(c == 0), stop=(c == NHC - 1)),
                            r=[("w", si, 0 if c < 11 else 1), ("hbuf", c, t2)], w=[("ps", pb)])
                    sch.op("dve", lambda e, pb=pb, dc=dc, tok=tok: e.scalar_tensor_tensor(
                        out=xT[:, dc, tok], in0=ps[pb][:, :], scalar=modT[:, gcol + dc:gcol + dc + 1], in1=xT[:, dc, tok],
                        op0=ALU.mult, op1=ALU.add),
                        r=[("ps", pb), "modT", ("xT", dc, tt)], w=[("xT", dc, tt)])

    for l in layers:
        adaln(l)
        if do_mixer:
            rmsnorm_mod(0)
            sch.barrier()
            if l % 2 == 1:
                rglru(l)
            else:
                nsa(l)
            sch.barrier()
        if do_ffn:
            rmsnorm_mod(1)
            sch.barrier()
            if do_ffn != "norm":
                ffn(l)
            sch.barrier()

    xo = [arena_f32(i * 4096, 128, [D]) for i in range(4)]
    for t in range(S // 128):
        bi = t % 4
        for half in range(2):
            pb = (2 * t + half) % 2
            for j in range(4):
                c = half * 4 + j
                sch.op("pe", lambda e, t=t, c=c, pb=pb, j=j: e.transpose(
                    out=ps[pb][:, j * 128:(j + 1) * 128], in_=xT[:, c, t * 128:(t + 1) * 128], identity=ident[:]),
                    r=[("xT", c, t // 4), "ident"], w=[("ps", pb)])
            eng = "act" if half == 0 else "dve"

            def cp(e, bi=bi, half=half, pb=pb, eng=eng):
                out = xo[bi][:, half * 512:(half + 1) * 512]
                if eng == "act":
                    return e.copy(out=out, in_=ps[pb][:, :])
                return e.tensor_copy(out=out, in_=ps[pb][:, :])
            sch.op(eng, cp, r=[("ps", pb)], w=[("xo", bi, half)])
        i = sch.op("sp", lambda e, t=t, bi=bi: e.dma_start(out=y_d[t * 128:(t + 1) * 128, :], in_=xo[bi]),
                   r=[("xo", bi, 0), ("xo", bi, 1)], dma=True)
        sch.ops[i]["final"] = True

    sch.emit(es)
    es.close()
    return nc


def _structural_constants():
    kk = np.arange(128)[:, None]
    xi = np.arange(1408)[None, :]
    dlt = (xi - 384) - kk
    w01 = ((dlt >= 0) & (dlt < 512)).astype(np.float32)
    c = np.arange(128)[:, None]
    t = np.arange(S)[None, :]
    cm01 = ((c < 127) & (16 * c + 31 <= t)).astype(np.float32)
    n_c, n_s = S // 16 - 1, S // 64
    tok = np.arange(S)
    start = np.arange(n_c) * 16
    cover_c = (tok[None, :] >= start[:, None]) & (tok[None, :] < start[:, None] + 32)
    cover_s = (tok[:, None] // 64) == np.arange(n_s)[None, :]
    sm = cover_c.astype(np.float32) @ cover_s.astype(np.float32) / np.float32(32)
    selaug = np.zeros((128, 33), np.float32)
    selaug[:n_c, :32] = sm
    selaug[:n_c, 32] = 1.0
    q = np.arange(128)[:, None, None]
    tb = np.arange(16)[None, :, None]
    sb_ = np.arange(32)[None, None, :]
    cur = (128 * tb + q) // 64
    forced = (sb_ == 0) | (sb_ == cur) | (sb_ == cur - 1)
    addmask = np.where(forced, 1e30, np.where(sb_ > cur, -1e30, 0.0)).astype(np.float32).reshape(128, 512)
    emat = ((np.arange(S)[None, :] // 64) == np.arange(32)[:, None]).astype(np.float32)
    gsel = np.zeros((12, 12, 64), np.float32)
    for a in range(12):
        gsel[a, a, :] = 1.0
    return {"c_w01": w01, "c_cm01": cm01, "c_selaug": selaug, "c_addmask": addmask, "c_emat": emat,
            "c_gsel": gsel.reshape(12, 768)}


def make_in_maps(inputs):
    f = lambda a: np.ascontiguousarray(np.asarray(a, dtype=np.float32))
    shared = {
        "ada_w": f(inputs["ada_w"]),
        "ada_b": f(inputs["ada_b"]).reshape(DEPTH, 48, 128),
        "norm1_g": f(inputs["norm1_g"]).reshape(DEPTH, NKC, 128),
        "norm2_g": f(inputs["norm2_g"]).reshape(DEPTH, NKC, 128),
        "ffn_w_in": f(inputs["ffn_w_in"]),
        "ffn_w_out": f(inputs["ffn_w_out"]),
        "rg_w_in": f(inputs["rg_w_in"]),
        "rg_conv_w": f(inputs["rg_conv_w"]).reshape(2, 4, RB, RBW),
        "rg_conv_b": f(inputs["rg_conv_b"]).reshape(2, RB, RBW),
        "rg_w_a": f(inputs["rg_w_a"]),
        "rg_b_a": f(inputs["rg_b_a"]),
        "rg_w_x": f(inputs["rg_w_x"]),
        "rg_b_x": f(inputs["rg_b_x"]),
        "rg_lam": f(inputs["rg_lam"]).reshape(2, RB, RBW),
        "rg_w_out": f(inputs["rg_w_out"]),
        "ident_in": np.eye(128, dtype=np.float32),
        "nsa_w_in": f(inputs["nsa_w_in"]),
        "nsa_w_out": f(inputs["nsa_w_out"]),
        "nsa_cmp_pos": f(inputs["nsa_cmp_pos"]),
        "nsa_cmp_w1": f(inputs["nsa_cmp_w1"]),
        "nsa_cmp_w2": f(inputs["nsa_cmp_w2"]),
        "nsa_q_gain": f(inputs["nsa_q_gain"]).reshape(2, 1, 64),
        "nsa_k_gain": f(inputs["nsa_k_gain"]),
    }
    shared.update(_structural_constants())
    x = f(inputs["x"])
    c = f(inputs["c"])
    maps = []
    for b in range(8):
        m = dict(shared)
        m["x"] = x[b]
        m["c"] = c[b].reshape(NKC, 128)
        maps.append(m)
    return maps


_NC_CACHE = {}


def kernel(**inputs):
    if "nc" not in _NC_CACHE:
        _NC_CACHE["nc"] = build_program()
    nc = _NC_CACHE["nc"]
    maps = make_in_maps(inputs)
    res = run_bass_kernel_spmd(nc, maps, core_ids=list(range(8)))
    out = np.stack([np.asarray(r["y"], dtype=np.float32) for r in res.results], axis=0)
    return out
```

```python
import numpy as np
from contextlib import ExitStack
import concourse.bass as bass
import concourse.mybir as mybir
from concourse.bass_utils import run_bass_kernel_spmd

F32 = mybir.dt.float32
BF16 = mybir.dt.bfloat16
AF = mybir.ActivationFunctionType
ALU = mybir.AluOpType
AX = mybir.AxisListType

D = 1024
S = 2048
DEPTH = 4
NKC = 8
TT = 512
NTT = S // TT
FFN_H = 2816
NHC = FFN_H // 128
EPS = 1e-6
RNN = 1408
RB = 16
RBW = 88
NSA_IN = 2608
DBG = {}
SWDGE_DEPTH = 2


class Sched:
    STREAMS = ("pe", "act", "dve", "pool", "sp")

    def __init__(self, nc, n_lanes=6):
        self.nc = nc
        self.ops = []
        self.lastw = {}
        self.readers = {}
        self.n_lanes = n_lanes
        self.pending = {s: set() for s in self.STREAMS}
        self.last_on = {s: None for s in self.STREAMS}
        self.dma_since_barrier = []

    def op(self, stream, fn, r=(), w=(), dma=False):
        i = len(self.ops)
        deps = set()
        for k in list(r) + list(w):
            if k in self.lastw:
                deps.add(self.lastw[k])
        for k in w:
            deps.update(self.readers.get(k, ()))
        if dma and stream == "pool":
            self._swq = getattr(self, "_swq", [])
            if len(self._swq) >= SWDGE_DEPTH:
                deps.add(self._swq[-SWDGE_DEPTH])
            self._swq.append(i)
        deps |= self.pending[stream]
        self.pending[stream] = set()
        deps.discard(i)
        self.ops.append(dict(stream=stream, fn=fn, deps=deps, dma=dma, needed=False))
        for k in w:
            self.lastw[k] = i
            self.readers[k] = []
        for k in r:
            self.readers.setdefault(k, []).append(i)
        self.last_on[stream] = i
        if dma:
            self.dma_since_barrier.append(i)
        return i

    def barrier(self, streams=("pe", "act", "dve", "pool", "sp")):
        deps = set(self.dma_since_barrier)
        for s in streams:
            if self.last_on[s] is not None:
                deps.add(self.last_on[s])
        self.dma_since_barrier = []
        for s in streams:
            self.pending[s] |= deps

    def emit(self, es):
        nc = self.nc
        ops = self.ops
        for o in ops:
            for d in o["deps"]:
                dd = ops[d]
                if dd["stream"] == "pe" and o["stream"] == "pe" and not dd["dma"] and not o["dma"]:
                    continue
                dd["needed"] = True
        sems = {s: es.enter_context(nc.semaphore("sem_" + s)) for s in self.STREAMS}
        lanes = {"hw": [es.enter_context(nc.semaphore("lane%d" % i)) for i in range(self.n_lanes)],
                 "sw": [es.enter_context(nc.semaphore("swlane%d" % i)) for i in range(self.n_lanes)]}
        cnt = {s: 0 for s in self.STREAMS}
        ndma = {"hw": 0, "sw": 0}
        for o in ops:
            if o["dma"]:
                kind = "sw" if o["stream"] == "pool" else "hw"
                lane = ndma[kind] % self.n_lanes
                use = ndma[kind] // self.n_lanes
                o["comp"] = (lanes[kind][lane], 16 * (use + 1))
                o["pre"] = (lanes[kind][lane], 16 * use) if use > 0 else None
                ndma[kind] += 1
            else:
                o["pre"] = None
                if o["needed"]:
                    cnt[o["stream"]] += 1
                    o["comp"] = (sems[o["stream"]], cnt[o["stream"]])
                else:
                    o["comp"] = None
        per_stream = {s: [o for o in ops if o["stream"] == s] for s in self.STREAMS}
        block = es.enter_context(nc.Block())

        def run_stream(eng, lst, sname):
            waited = {}
            for o in lst:
                need = {}
                for d in o["deps"]:
                    dd = ops[d]
                    if dd["stream"] == "pe" and sname == "pe" and not dd["dma"] and not o["dma"]:
                        continue
                    sem, val = dd["comp"]
                    key = id(sem)
                    if key not in need or need[key][1] < val:
                        need[key] = (sem, val)
                if o["pre"] is not None:
                    sem, val = o["pre"]
                    key = id(sem)
                    if key not in need or need[key][1] < val:
                        need[key] = (sem, val)
                for key, (sem, val) in need.items():
                    if waited.get(key, 0) >= val:
                        continue
                    eng.wait_ge(sem, val)
                    waited[key] = val
                ins = o["fn"](eng)
                if o["dma"]:
                    ins.then_inc(o["comp"][0], 16)
                elif o["comp"] is not None:
                    ins.then_inc(o["comp"][0], 1)

        @block.tensor
        def _(e):
            run_stream(e, per_stream["pe"], "pe")

        @block.scalar
        def _(e):
            run_stream(e, per_stream["act"], "act")

        @block.vector
        def _(e):
            run_stream(e, per_stream["dve"], "dve")

        @block.gpsimd
        def _(e):
            run_stream(e, per_stream["pool"], "pool")

        @block.sync
        def _(e):
            run_stream(e, per_stream["sp"], "sp")
            for o in ops:
                if o["dma"] and o.get("final"):
                    e.wait_ge(o["comp"][0], o["comp"][1])


def build_program(layers=(0, 1, 2, 3), do_mixer=True, do_ffn=True):
    nc = bass.Bass("TRN2", target_bir_lowering=False)
    es = ExitStack()
    dram = {}

    def din(name, shape, dt=F32):
        dram[name] = nc.dram_tensor(name, list(shape), dt, kind="ExternalInput").ap()
        return dram[name]

    x_d = din("x", [S, D])
    c_d = din("c", [NKC, 128])
    ada_w = din("ada_w", [DEPTH, D, 6 * D])
    ada_b = din("ada_b", [DEPTH, 48, 128])
    n1g = din("norm1_g", [DEPTH, NKC, 128])
    n2g = din("norm2_g", [DEPTH, NKC, 128])
    ffn_w_in = din("ffn_w_in", [DEPTH, D, 2 * FFN_H])
    ffn_w_out = din("ffn_w_out", [DEPTH, FFN_H, D])
    rg_w_in = din("rg_w_in", [2, D, 2 * RNN])
    rg_conv_w = din("rg_conv_w", [2, 4, RB, RBW])
    rg_conv_b = din("rg_conv_b", [2, RB, RBW])
    rg_w_a = din("rg_w_a", [2, RB, RBW, RBW])
    rg_b_a = din("rg_b_a", [2, RB, RBW])
    rg_w_x = din("rg_w_x", [2, RB, RBW, RBW])
    rg_b_x = din("rg_b_x", [2, RB, RBW])
    rg_lam = din("rg_lam", [2, RB, RBW])
    rg_w_out = din("rg_w_out", [2, RNN, D])
    ident_d = din("ident_in", [128, 128])
    nsa_w_in = din("nsa_w_in", [2, D, NSA_IN])
    nsa_w_out = din("nsa_w_out", [2, D, D])
    nsa_pos = din("nsa_cmp_pos", [2, 2, 32, 64])
    nsa_w1 = din("nsa_cmp_w1", [2, 2, 2048, 256])
    nsa_w2 = din("nsa_cmp_w2", [2, 2, 256, 64])
    nsa_qg = din("nsa_q_gain", [2, 1, 64])
    nsa_kg = din("nsa_k_gain", [2, 3, 64])
    c_w01 = din("c_w01", [128, 1408])
    c_cm01 = din("c_cm01", [128, 2048])
    c_selaug = din("c_selaug", [128, 33])
    c_addmask = din("c_addmask", [128, 512])
    c_emat = din("c_emat", [32, 2048])
    c_gsel = din("c_gsel", [12, 768])
    y_d = nc.dram_tensor("y", [S, D], F32, kind="ExternalOutput").ap()

    sch = Sched(nc)

    def sb(name, shape, dt):
        return es.enter_context(nc.sbuf_tensor("sb_" + name, list(shape), dt))

    xT = sb("xT", [128, NKC, S], F32)
    hh = sb("hh", [128, NKC, S], BF16)
    ident = sb("ident", [128, 128], F32)
    identb = sb("identb", [128, 128], BF16)
    onesb = sb("onesb", [128, 128], BF16)
    condT = sb("condT", [128, NKC], F32)
    condTb = sb("condTb", [128, NKC], BF16)
    c_sb = sb("c_sb", [NKC, 128], F32)
    modT = sb("modT", [128, 48], F32)
    adab_sb = sb("adab_sb", [48, 128], F32)
    ng_sb = sb("ng_sb", [2 * NKC, 128], F32)
    ngT = sb("ngT", [128, 2 * NKC], F32)
    Avec = sb("Avec", [128, 2 * NKC], F32)
    sb_rgv = sb("rgv", [RBW, 128], F32)
    gains = sb("gains", [128, 4], F32)
    blockones = sb("blockones", [128, 128], BF16)
    w2sb = sb("w2sb", [128, 2, 2, 64], BF16)
    posT = sb("posT", [64, 2, 32], BF16)
    ident30k = sb("ident30k", [128, 128], BF16)
    bias_sb = sb("bias_sb", [128, 4], F32)
    NW = 4
    WSLOT = 4096
    wring = [sb("wring%d" % i, [128, WSLOT], BF16) for i in range(NW)]
    ARENA = 36 * 1024
    arena = sb("arena", [128, ARENA], BF16)
    ps = [es.enter_context(nc.psum_tensor("ps%d" % i, [128, 512], F32)) for i in range(8)]

    wctr = [0]

    def wslot():
        i = wctr[0] % NW
        wctr[0] += 1
        return i

    def arena_f32(off_bytes, parts, shape_free):
        n = int(np.prod(shape_free))
        ap = arena[0:parts, off_bytes // 2: off_bytes // 2 + 2 * n].bitcast(F32)
        return ap

    sch.op("sp", lambda e: e.dma_start(out=ident[:], in_=ident_d[:, :]), w=["ident"], dma=True)
    sch.op("sp", lambda e: e.dma_start(out=c_sb[:], in_=c_d[:, :]), w=["c_sb"], dma=True)
    sch.op("dve", lambda e: e.tensor_copy(out=identb[:], in_=ident[:]), r=["ident"], w=["identb"])
    sch.op("dve", lambda e: e.memset(onesb[:], 1.0), w=["onesb"])
    sch.op("dve", lambda e: e.memset(blockones[:], 0.0), w=["blockones"])
    sch.op("dve", lambda e: e.memset(blockones[0:64, 0:64], 1.0), w=["blockones"])
    sch.op("dve", lambda e: e.memset(blockones[64:128, 64:128], 1.0), w=["blockones"])
    sch.op("dve", lambda e: e.tensor_scalar(out=ident30k[:], in0=ident[:], scalar1=30000.0, scalar2=None, op0=ALU.mult),
           r=["ident"], w=["ident30k"])
    sch.op("pe", lambda e: e.transpose(out=ps[0][:, 0:NKC], in_=c_sb[:, :], identity=ident[0:NKC, 0:NKC]),
           r=["c_sb", "ident"], w=[("ps", 0)])
    sch.op("act", lambda e: e.activation(out=condT[:], in_=ps[0][:, 0:NKC], func=AF.Silu),
           r=[("ps", 0)], w=["condT"])
    sch.op("dve", lambda e: e.tensor_copy(out=condTb[:], in_=condT[:]), r=["condT"], w=["condTb"])

    xin = [arena_f32(i * 4096, 128, [D]) for i in range(4)]
    for t in range(S // 128):
        bi = t % 4
        sch.op("sp", lambda e, t=t, bi=bi: e.dma_start(out=xin[bi], in_=x_d[t * 128:(t + 1) * 128, :]),
               w=[("xin", bi)], dma=True)
        for half in range(2):
            pb = (2 * t + half) % 2
            for j in range(4):
                c = half * 4 + j
                sch.op("pe", lambda e, bi=bi, c=c, pb=pb, j=j: e.transpose(
                    out=ps[pb][:, j * 128:(j + 1) * 128], in_=xin[bi][:, c * 128:(c + 1) * 128], identity=ident[:]),
                    r=[("xin", bi), "ident"], w=[("ps", pb)])
            eng = "act" if half == 0 else "dve"

            def cp(e, t=t, half=half, pb=pb, eng=eng):
                out = xT[:, half * 4:(half + 1) * 4, t * 128:(t + 1) * 128]
                in_ = ps[pb][:, :].rearrange("p (j q) -> p j q", j=4)
                if eng == "act":
                    return e.copy(out=out, in_=in_)
                return e.tensor_copy(out=out, in_=in_)
            sch.op(eng, cp, r=[("ps", pb)], w=[("xT", half * 4 + j, t // 4) for j in range(4)])
    sch.barrier()

    def load_w(dram_view, ncols_total, keyname):
        si = wslot()
        dst = wring[si][:, 0:ncols_total]
        return si, dst

    def adaln(l):
        sch.op("sp", lambda e: e.dma_start(out=adab_sb[:], in_=ada_b[l, :, :]), w=["adab_sb"], dma=True)
        sch.op("sp", lambda e: e.dma_start(out=ng_sb[0:NKC, :], in_=n1g[l, :, :]), w=["ng_sb0"], dma=True)
        sch.op("sp", lambda e: e.dma_start(out=ng_sb[NKC:2 * NKC, :], in_=n2g[l, :, :]), w=["ng_sb1"], dma=True)
        PB = 7
        GC = 4
        for g in range(48 // GC):
            si = wslot()
            wv = wring[si][:, 0:NKC * GC * 128].rearrange("p (k c) -> p k c", k=NKC)
            src = ada_w[l, :, g * GC * 128:(g + 1) * GC * 128].rearrange("(k p) c -> p k c", p=128)
            sch.op("pool", lambda e, wv=wv, src=src: e.dma_start(out=wv, in_=src), w=[("w", si, i_) for i_ in range(4)], dma=True)
            for cc in range(GC):
                col = g * GC + cc
                for k in range(NKC):
                    sch.op("pe", lambda e, wv=wv, cc=cc, k=k, col=col: e.matmul(
                        ps[PB][:, col:col + 1], lhsT=wv[:, k, cc * 128:(cc + 1) * 128], rhs=condTb[:, k:k + 1],
                        start=(k == 0), stop=(k == NKC - 1)),
                        r=[("w", si, 0), "condTb"], w=[("ps", PB)])
        sch.op("pe", lambda e: e.transpose(out=ps[PB][:, 64:112], in_=adab_sb[:, :], identity=ident[0:48, 0:48]),
               r=["adab_sb", "ident"], w=[("ps", PB)])
        sch.op("pe", lambda e: e.transpose(out=ps[PB][:, 128:144], in_=ng_sb[:, :], identity=ident[0:16, 0:16]),
               r=["ng_sb0", "ng_sb1", "ident"], w=[("ps", PB)])
        sch.op("dve", lambda e: e.tensor_copy(out=modT[:], in_=ps[PB][:, 64:112]), r=[("ps", PB)], w=["modT"])
        sch.op("dve", lambda e: e.tensor_tensor(out=modT[:], in0=ps[PB][:, 0:48], in1=modT[:], op=ALU.add),
               r=[("ps", PB), "modT"], w=["modT"])
        sch.op("dve", lambda e: e.tensor_copy(out=ngT[:], in_=ps[PB][:, 128:144]), r=[("ps", PB)], w=["ngT"])
        for j, (sccol) in enumerate((8, 32)):
            sch.op("dve", lambda e, j=j, sccol=sccol: e.scalar_tensor_tensor(
                out=Avec[:, j * 8:(j + 1) * 8], in0=modT[:, sccol:sccol + 8], scalar=1.0,
                in1=ngT[:, j * 8:(j + 1) * 8], op0=ALU.add, op1=ALU.mult),
                r=["modT", "ngT"], w=["Avec"])

    def rmsnorm_mod(which):
        shcol = 0 if which == 0 else 24
        sq = [arena[:, ARENA - (i + 1) * 512: ARENA - i * 512] for i in range(2)]
        rstd = [arena_f32(ARENA * 2 - 4096 - (i + 1) * 2048, 128, [512]) for i in range(2)]
        tmp = [arena_f32(ARENA * 2 - 8192 - (i + 1) * 2048, 128, [512]) for i in range(2)]
        n = [0]
        for tt in range(NTT):
            tok = slice(tt * TT, (tt + 1) * TT)
            pb = 5 + (tt % 2)
            for c in range(NKC):
                qi = n[0] % 2
                n[0] += 1
                sch.op("act", lambda e, c=c, tok=tok, qi=qi: e.activation(out=sq[qi], in_=xT[:, c, tok], func=AF.Square),
                       r=[("xT", c, tt)], w=[("sq", qi)])
                sch.op("pe", lambda e, c=c, qi=qi, pb=pb: e.matmul(ps[pb][:, :], lhsT=onesb[:, :], rhs=sq[qi],
                                                                  start=(c == 0), stop=(c == NKC - 1)),
                       r=[("sq", qi), "onesb"], w=[("ps", pb)])
            ri = tt % 2
            sch.op("act", lambda e, pb=pb, ri=ri: e.activation(out=rstd[ri], in_=ps[pb][:, :], func=AF.Sqrt,
                                                               scale=1.0 / D, bias=EPS),
                   r=[("ps", pb)], w=[("rstd", ri)])
            sch.op("dve", lambda e, ri=ri: e.reciprocal(out=rstd[ri], in_=rstd[ri]), r=[("rstd", ri)], w=[("rstd", ri)])
            for c in range(NKC):
                ti = c % 2
                sch.op("dve", lambda e, c=c, tok=tok, ri=ri, ti=ti: e.scalar_tensor_tensor(
                    out=tmp[ti], in0=xT[:, c, tok], scalar=Avec[:, which * 8 + c:which * 8 + c + 1], in1=rstd[ri],
                    op0=ALU.mult, op1=ALU.mult),
                    r=[("xT", c, tt), "Avec", ("rstd", ri)], w=[("ntmp", ti)])
                sch.op("act", lambda e, c=c, tok=tok, ti=ti: e.activation(
                    out=hh[:, c, tok], in_=tmp[ti], func=AF.Identity, bias=modT[:, shcol + c:shcol + c + 1], scale=1.0),
                    r=[("ntmp", ti), "modT"], w=[("hh", c, tt)])


    def rglru(l):
        j = l // 2
        g1col = 16
        rgv_in = arena_f32(0, 128, [RBW])
        rgv = sb_rgv
        sch.op("sp", lambda e: e.dma_start(out=rgv_in[0:64, :], in_=rg_conv_w[j].rearrange("a n w -> (a n) w")),
               w=["rgv_in0"], dma=True)
        for qi, src in enumerate((rg_conv_b, rg_b_a, rg_b_x, rg_lam)):
            sch.op("sp", lambda e, qi=qi, src=src: e.dma_start(out=rgv_in[64 + qi * 16:80 + qi * 16, :], in_=src[j, :, :]),
                   w=["rgv_in%d" % (qi + 1)], dma=True)
        sch.op("pe", lambda e: e.transpose(out=ps[7][0:RBW, 0:128], in_=rgv_in[:, :], identity=ident[:, :]),
               r=["rgv_in%d" % i for i in range(5)] + ["ident"], w=[("ps", 7)])
        sch.op("dve", lambda e: e.tensor_copy(out=rgv[:, :], in_=ps[7][0:RBW, 0:128]), r=[("ps", 7)], w=["rgv"])
        sch.op("act", lambda e: e.activation(out=rgv[:, 112:128], in_=rgv[:, 112:128], func=AF.Exp, scale=-1.0),
               r=["rgv"], w=["rgv"])
        sch.op("act", lambda e: e.activation(out=rgv[:, 112:128], in_=rgv[:, 112:128], func=AF.Ln, bias=1.0, scale=1.0),
               r=["rgv"], w=["rgv"])
        sch.op("dve", lambda e: e.tensor_scalar(out=rgv[:, 112:128], in0=rgv[:, 112:128], scalar1=-8.0, scalar2=None,
                                                op0=ALU.mult), r=["rgv"], w=["rgv"])
        sch.barrier()
        ybuf = arena[0:RBW, 0:4 * S].rearrange("p (n t) -> p n t", n=4)
        base = 4 * S * 2
        G = arena_f32(base, RBW, [S])
        Uraw = arena_f32(base + 8192, RBW, [S + 16])
        Up = arena_f32(base + 2 * 8192 + 64, RBW, [S])
        RA = arena_f32(base + 3 * 8192 + 64, RBW, [S])
        IB = arena_f32(base + 4 * 8192 + 64, RBW, [S])
        MH = arena_f32(base + 5 * 8192 + 64, RBW, [S])
        Upb = arena[0:RBW, (base + 6 * 8192 + 64) // 2:(base + 6 * 8192 + 64) // 2 + S]
        sch.op("dve", lambda e: e.memset(Uraw[:, 0:16], 0.0), w=["Uraw"])
        for q4 in range(4):
            for bi in range(4):
                n = q4 * 4 + bi
                si = wslot()
                wv = wring[si][:, 0:NKC * 176].rearrange("p (k c) -> p k c", k=NKC)
                wg = wring[si][0:RBW, NKC * 176:NKC * 176 + 176]
                sch.op("pool", lambda e, wv=wv, n=n: e.dma_start(
                    out=wv[:, :, 0:RBW], in_=rg_w_in[j, :, n * RBW:(n + 1) * RBW].rearrange("(k p) c -> p k c", p=128)),
                    w=[("w", si, 0), ("w", si, 1), ("w", si, 2), ("w", si, 3)], dma=True)
                sch.op("pool", lambda e, wv=wv, n=n: e.dma_start(
                    out=wv[:, :, RBW:2 * RBW], in_=rg_w_in[j, :, RNN + n * RBW:RNN + (n + 1) * RBW].rearrange("(k p) c -> p k c", p=128)),
                    w=[("w", si, 1)], dma=True)
                sch.op("pool", lambda e, wg=wg, n=n: e.dma_start(out=wg[:, 0:RBW], in_=rg_w_a[j, n, :, :]), w=[("w", si, 2)], dma=True)
                sch.op("pool", lambda e, wg=wg, n=n: e.dma_start(out=wg[:, RBW:2 * RBW], in_=rg_w_x[j, n, :, :]), w=[("w", si, 3)], dma=True)
                for tt in range(NTT):
                    tok = slice(tt * TT, (tt + 1) * TT)
                    pg, pu = (0, 1) if tt % 2 == 0 else (2, 3)
                    for k in range(NKC):
                        sch.op("pe", lambda e, wv=wv, k=k, tok=tok, pg=pg: e.matmul(
                            ps[pg][0:RBW, :], lhsT=wv[:, k, 0:RBW], rhs=hh[:, k, tok], start=(k == 0), stop=(k == NKC - 1)),
                            r=[("w", si, 0), ("hh", k, tt)], w=[("ps", pg)])
                    for k in range(NKC):
                        sch.op("pe", lambda e, wv=wv, k=k, tok=tok, pu=pu: e.matmul(
                            ps[pu][0:RBW, :], lhsT=wv[:, k, RBW:2 * RBW], rhs=hh[:, k, tok], start=(k == 0), stop=(k == NKC - 1)),
                            r=[("w", si, 1), ("hh", k, tt)], w=[("ps", pu)])
                    sch.op("act", lambda e, pg=pg, tok=tok: e.activation(out=G[:, tok], in_=ps[pg][0:RBW, :], func=AF.Gelu_apprx_tanh),
                           r=[("ps", pg)], w=[("G", tt)])
                    sch.op("dve", lambda e, pu=pu, tt=tt: e.tensor_copy(out=Uraw[:, 16 + tt * TT:16 + (tt + 1) * TT], in_=ps[pu][0:RBW, :]),
                           r=[("ps", pu)], w=["Uraw"])
                sch.op("dve", lambda e, n=n: e.tensor_scalar(out=Up[:, :], in0=Uraw[:, 16:16 + S], scalar1=rgv[:, 48 + n:49 + n],
                                                          scalar2=rgv[:, 64 + n:65 + n], op0=ALU.mult, op1=ALU.add),
                       r=["Uraw", "rgv"], w=["Up"])
                for a in range(3):
                    sch.op("dve", lambda e, n=n, a=a: e.scalar_tensor_tensor(
                        out=Up[:, :], in0=Uraw[:, 13 + a:13 + a + S], scalar=rgv[:, a * 16 + n:a * 16 + n + 1], in1=Up[:, :],
                        op0=ALU.mult, op1=ALU.add), r=["Uraw", "rgv", "Up"], w=["Up"])
                sch.op("act", lambda e: e.copy(out=Upb[:, :], in_=Up[:, :]), r=["Up"], w=["Upb"])
                for tt in range(NTT):
                    tok = slice(tt * TT, (tt + 1) * TT)
                    pr, pi = (4, 5) if tt % 2 == 0 else (6, 7)
                    sch.op("pe", lambda e, wg=wg, tok=tok, pr=pr: e.matmul(ps[pr][0:RBW, :], lhsT=wg[:, 0:RBW], rhs=Upb[:, tok],
                                                                       start=True, stop=True),
                           r=[("w", si, 2), "Upb"], w=[("ps", pr)])
                    sch.op("pe", lambda e, wg=wg, tok=tok, pi=pi: e.matmul(ps[pi][0:RBW, :], lhsT=wg[:, RBW:2 * RBW], rhs=Upb[:, tok],
                                                                       start=True, stop=True),
                           r=[("w", si, 3), "Upb"], w=[("ps", pi)])
                    sch.op("act", lambda e, n=n, tok=tok, pr=pr: e.activation(out=RA[:, tok], in_=ps[pr][0:RBW, :], func=AF.Sigmoid,
                                                                         bias=rgv[:, 80 + n:81 + n], scale=1.0),
                           r=[("ps", pr), "rgv"], w=[("RA", tt)])
                    sch.op("act", lambda e, n=n, tok=tok, pi=pi: e.activation(out=IB[:, tok], in_=ps[pi][0:RBW, :], func=AF.Sigmoid,
                                                                         bias=rgv[:, 96 + n:97 + n], scale=1.0),
                           r=[("ps", pi), "rgv"], w=[("IB", tt)])
                allRA = [("RA", tt) for tt in range(NTT)]
                allIB = [("IB", tt) for tt in range(NTT)]
                sch.op("act", lambda e, n=n: e.activation(out=RA[:, :], in_=RA[:, :], func=AF.Exp, scale=rgv[:, 112 + n:113 + n]),
                       r=allRA + ["rgv"], w=allRA)
                sch.op("dve", lambda e: e.tensor_tensor(out=MH[:, :], in0=RA[:, :], in1=RA[:, :], op=ALU.mult), r=allRA, w=["MH"])
                sch.op("act", lambda e: e.activation(out=MH[:, :], in_=MH[:, :], func=AF.Sqrt, scale=-1.0, bias=1.0), r=["MH"], w=["MH"])
                sch.op("dve", lambda e: e.tensor_tensor(out=IB[:, :], in0=IB[:, :], in1=Up[:, :], op=ALU.mult), r=allIB + ["Up"], w=allIB)
                sch.op("dve", lambda e: e.tensor_tensor(out=IB[:, :], in0=IB[:, :], in1=MH[:, :], op=ALU.mult), r=allIB + ["MH"], w=allIB)
                sch.op("dve", lambda e: e.tensor_tensor_scan(out=MH[:, :], data0=RA[:, :], data1=IB[:, :], initial=0.0,
                                                            op0=ALU.mult, op1=ALU.add), r=allRA + allIB + ["MH"], w=["MH"])
                sch.op("dve", lambda e, bi=bi: e.tensor_tensor(out=ybuf[:, bi, :], in0=MH[:, :], in1=G[:, :], op=ALU.mult),
                       r=["MH"] + [("G", tt) for tt in range(NTT)], w=[("ybuf", bi)])
            so = wslot()
            wo = wring[so][0:RBW, 0:4 * D].rearrange("p (n d) -> p n d", n=4)
            for bi in range(4):
                n = q4 * 4 + bi
                sch.op("pool", lambda e, wo=wo, bi=bi, n=n: e.dma_start(out=wo[:, bi, :], in_=rg_w_out[j, n * RBW:(n + 1) * RBW, :]),
                       w=[("w", so, 0), ("w", so, 1), ("w", so, 2), ("w", so, 3)], dma=True)
            for dc in range(NKC):
                for tt in range(NTT):
                    tok = slice(tt * TT, (tt + 1) * TT)
                    pb = (dc * NTT + tt) % 4
                    for bi in range(4):
                        sch.op("pe", lambda e, wo=wo, bi=bi, dc=dc, tok=tok, pb=pb: e.matmul(
                            ps[pb][:, :], lhsT=wo[:, bi, dc * 128:(dc + 1) * 128], rhs=ybuf[:, bi, tok], start=(bi == 0), stop=(bi == 3)),
                            r=[("w", so, 0), ("ybuf", bi)], w=[("ps", pb)])
                    sch.op("dve", lambda e, pb=pb, dc=dc, tok=tok: e.scalar_tensor_tensor(
                        out=xT[:, dc, tok], in0=ps[pb][:, :], scalar=modT[:, g1col + dc:g1col + dc + 1], in1=xT[:, dc, tok],
                        op0=ALU.mult, op1=ALU.add),
                        r=[("ps", pb), "modT", ("xT", dc, tt)], w=[("xT", dc, tt)])

    def wkeys(si):
        return [("w", si, i) for i in range(4)]

    def nsa(l):
        j = l // 2
        g1col = 16
        O_W01, O_CM, O_SA, O_AM, O_Q = 0, 2816, 6912, 7040, 9088
        O_KS, O_KW, O_KC, O_VC, O_X, O_Y = 25472, 29568, 33664, 37760, 41856, 45952
        O_VS, O_VW, O_KCN, O_VCA, O_ACC, O_T = 47488, 51584, 55680, 55936, 56192, 64384

        def rb(off, p0, p1, n):
            return arena[p0:p1, off // 2: off // 2 + n]

        def rf(off, p0, p1, n):
            return arena[p0:p1, off // 2: off // 2 + 2 * n].bitcast(F32)

        W01 = rb(O_W01, 0, 128, 1408)
        cm01 = rb(O_CM, 0, 128, 2048)
        selaug = rb(O_SA, 0, 128, 33)
        addmask = rf(O_AM, 0, 128, 512).rearrange("p (t s) -> p t s", s=32)
        qn = rb(O_Q, 0, 64, 8192).rearrange("p (h t) -> p h t", h=4)
        qx = rb(O_Q, 0, 128, 8192).rearrange("p (h t) -> p h t", h=4)
        nsT = rb(O_Q, 64, 96, 8192).rearrange("p (h t) -> p h t", h=4)
        ks = rb(O_KS, 0, 64, 2048)
        kse = rb(O_KS, 0, 128, 2048)
        emat = rb(O_KS, 64, 96, 2048)
        kw = rb(O_KW, 0, 64, 2048)
        kcraw = rb(O_KC, 0, 64, 2048)
        vcraw = rb(O_VC, 0, 64, 2048)
        oT = [rb(O_KW, 64, 128, 2048), rb(O_KC, 64, 128, 2048), rb(O_VC, 64, 128, 2048), rb(O_X, 64, 128, 2048)]
        sgT = rb(O_X, 0, 12, 2048)
        gsel = rb(O_Y, 0, 12, 768).rearrange("p (a m) -> p a m", m=64)
        Vs = rb(O_VS, 0, 128, 2048).rearrange("p (t c) -> p t c", c=128)
        Vw = rb(O_VW, 0, 128, 2048).rearrange("p (t c) -> p t c", c=128)
        kcn = rb(O_KCN, 0, 64, 128)
        vca = rb(O_VCA, 0, 128, 128)
        acc = rf(O_ACC, 0, 64, 2048).rearrange("p (h t) -> p h t", h=4)
        sq = [rb(O_T + i * 1024, 0, 128, 512) for i in range(2)]
        rinv = [rf(O_T + 2048 + i * 2048, 0, 128, 512) for i in range(2)]
        h1 = rb(O_T + 6144, 0, 128, 256).rearrange("p (a n) -> p a n", a=2)
        stg = rf(O_T + 6656, 0, 32, 128)
        PT = [rb(O_T + i * 1024, 0, 128, 512) for i in range(2)]
        rden = rf(O_T + 2048, 0, 64, 512)
        fac = rf(O_T + 4096, 0, 64, 512)
        tmpo = rf(O_T + 6144, 0, 64, 512)
        impacc = rf(O_T + 8192, 0, 128, 128).rearrange("p (q s) -> p q s", s=32)
        rcp = rf(O_T + 8704, 0, 128, 4)
        top8 = rf(O_T + 8720, 0, 128, 8)
        impm = rf(O_T + 8752, 0, 128, 32)
        selm = rb(O_T + 8880, 0, 128, 32)
        dn = rf(O_T + 8944, 0, 128, 4)

        sch.op("pool", lambda e: e.dma_start(out=W01, in_=c_w01[:, :]), w=["W01"], dma=True)
        sch.op("pool", lambda e: e.dma_start(out=cm01, in_=c_cm01[:, :]), w=["cm01"], dma=True)
        sch.op("pool", lambda e: e.dma_start(out=selaug, in_=c_selaug[:, :]), w=["selaug"], dma=True)
        sch.op("dve", lambda e: e.memset(rb(O_KS, 64, 128, 2048), 0.0), w=["emat"])
        sch.op("dve", lambda e: e.memset(rb(O_Q, 64, 128, 8192), 0.0), w=["nsT_init"])
        sch.op("pool", lambda e: e.dma_start(out=emat, in_=c_emat[:, :]), w=["emat"], dma=True)
        sch.op("pool", lambda e: e.dma_start(out=gsel, in_=c_gsel[:, :].rearrange("p (a m) -> p a m", m=64)), w=["gsel"], dma=True)
        sch.op("sp", lambda e: e.dma_start(out=rf(O_AM, 0, 128, 512), in_=c_addmask[:, :]), w=["addmask"], dma=True)
        for kv in range(2):
            sch.op("pool", lambda e, kv=kv: e.dma_start(out=w2sb[:, kv, :, :],
                                                        in_=nsa_w2[j, kv, :, :].rearrange("(h p) d -> p h d", p=128)),
                   w=[("w2sb", kv)], dma=True)
        sch.op("dve", lambda e: e.memset(Vs[:, :, 64:128], 1.0), w=["VsOnes"])
        sch.op("dve", lambda e: e.memset(Vw[:, :, 64:128], 1.0), w=["VwOnes"])
        sch.op("dve", lambda e: e.memset(vca[:, :], 0.0), w=["vca"])
        sch.op("dve", lambda e: e.memset(vca[:, 64:128], 1.0), w=["vca"])
        sch.op("dve", lambda e: e.memset(kcn[:, :], 0.0), w=["kcn"])
        for hf in range(2):
            sch.op("sp", lambda e, hf=hf: e.dma_start(out=stg[0:1, hf * 64:(hf + 1) * 64], in_=nsa_qg[j, :, :]), w=["stg"], dma=True)
            sch.op("sp", lambda e, hf=hf: e.dma_start(out=stg[1:4, hf * 64:(hf + 1) * 64], in_=nsa_kg[j, :, :]), w=["stg"], dma=True)
        sch.op("pe", lambda e: e.transpose(out=ps[7][:, 0:4], in_=stg[0:4, :], identity=ident[0:4, 0:4]),
               r=["stg", "ident"], w=[("ps", 7)])
        sch.op("dve", lambda e: e.tensor_copy(out=gains[:, :], in_=ps[7][:, 0:4]), r=[("ps", 7)], w=["gains"])
        for kv in range(2):
            sch.op("sp", lambda e, kv=kv: e.dma_start(out=stg[0:32, 0:64], in_=nsa_pos[j, kv, :, :]), w=["stg"], dma=True)
            sch.op("pe", lambda e: e.transpose(out=ps[7][0:64, 0:32], in_=stg[0:32, 0:64], identity=ident[0:32, 0:32]),
                   r=["stg", "ident"], w=[("ps", 7)])
            sch.op("dve", lambda e, kv=kv: e.tensor_copy(out=posT[:, kv, :], in_=ps[7][0:64, 0:32]), r=[("ps", 7)],
                   w=[("posT", kv)])
        sch.barrier()

        cnt = {"st": 0, "ot": 0, "cp": 0}
        if DBG.get("nsa_stop") == "setup":
            return

        def finalize(h, I, b, otb, qs):
            sch.op("dve", lambda e: e.tensor_scalar(out=rden[:, :], in0=ps[otb][64:128, :], scalar1=1e-18, scalar2=None,
                                                    op0=ALU.max), r=[("ps", otb)], w=["rden"])
            sch.op("act", lambda e: e.activation(out=rden[:, :], in_=rden[:, :], func=AF.Ln), r=["rden"], w=["rden"])
            sch.op("act", lambda e: e.activation(out=rden[:, :], in_=rden[:, :], func=AF.Exp, scale=-1.0), r=["rden"], w=["rden"])
            sch.op("pe", lambda e: e.matmul(ps[4][0:64, :], lhsT=gsel[:, h * 3 + b, :], rhs=sgT[:, qs], start=True, stop=True),
                   r=["gsel"] + [("sgT", I)], w=[("ps", 4)])
            sch.op("dve", lambda e: e.tensor_tensor(out=fac[:, :], in0=ps[4][0:64, :], in1=rden[:, :], op=ALU.mult),
                   r=[("ps", 4), "rden"], w=["fac"])
            if b == 0:
                sch.op("dve", lambda e: e.tensor_tensor(out=acc[:, h, :], in0=ps[otb][0:64, :], in1=fac[:, :], op=ALU.mult),
                       r=[("ps", otb), "fac"], w=[("acc", h)])
            elif b == 1:
                sch.op("dve", lambda e: e.tensor_tensor(out=tmpo[:, :], in0=ps[otb][0:64, :], in1=fac[:, :], op=ALU.mult),
                       r=[("ps", otb), "fac"], w=["tmpo"])
                sch.op("dve", lambda e: e.tensor_tensor(out=acc[:, h, :], in0=acc[:, h, :], in1=tmpo[:, :], op=ALU.add),
                       r=[("acc", h), "tmpo"], w=[("acc", h)])
            else:
                sch.op("dve", lambda e: e.tensor_tensor(out=tmpo[:, :], in0=ps[otb][0:64, :], in1=fac[:, :], op=ALU.mult),
                       r=[("ps", otb), "fac"], w=["tmpo"])
                sch.op("dve", lambda e: e.tensor_tensor(out=oT[h][:, qs], in0=acc[:, h, :], in1=tmpo[:, :], op=ALU.add),
                       r=[("acc", h), "tmpo"], w=[("oT", h, I)])

        def do_group(g):
            sA = wslot()
            wA = wring[sA][:, 0:4096].rearrange("p (k c) -> p k c", k=NKC)
            colsA = [(0, 256, g * 256), (256, 64, 1536 + g * 64), (320, 64, 2048 + g * 64),
                     (384, 64, 1024 + g * 64), (448, 64, 1280 + g * 64)]
            for (d0, n, c0) in colsA:
                sch.op("pool", lambda e, d0=d0, n=n, c0=c0: e.dma_start(
                    out=wA[:, :, d0:d0 + n], in_=nsa_w_in[j, :, c0:c0 + n].rearrange("(k p) c -> p k c", p=128)),
                    w=wkeys(sA), dma=True)
            sB = wslot()
            wB = wring[sB][:, 0:NKC * 176].rearrange("p (k c) -> p k c", k=NKC)
            colsB = [(0, 64, 1792 + g * 64), (64, 64, 2304 + g * 64), (128, 48, 2560)]
            for (d0, n, c0) in colsB:
                sch.op("pool", lambda e, d0=d0, n=n, c0=c0: e.dma_start(
                    out=wB[:, :, d0:d0 + n], in_=nsa_w_in[j, :, c0:c0 + n].rearrange("(k p) c -> p k c", p=128)),
                    w=wkeys(sB), dma=True)
            nit = [0]

            def proj_pair(tt, p, dstA, gcA, keyA, dstB, gcB, keyB):
                tok = slice(tt * TT, (tt + 1) * TT)
                it = nit[0]
                nit[0] += 1
                pb = it % 4
                ssb = 4 + (it % 2)
                si_ = it % 2
                for k in range(NKC):
                    sch.op("pe", lambda e, k=k: e.matmul(
                        ps[pb][:, :], lhsT=wA[:, k, p * 128:(p + 1) * 128], rhs=hh[:, k, tok], start=(k == 0), stop=(k == NKC - 1)),
                        r=[("w", sA, 0), ("hh", k, tt)], w=[("ps", pb)])
                if gcA is not None:
                    sch.op("act", lambda e: e.activation(out=sq[si_], in_=ps[pb][:, :], func=AF.Square),
                           r=[("ps", pb)], w=[("sq", si_)])
                    sch.op("pe", lambda e: e.matmul(ps[ssb][:, :], lhsT=blockones[:, :], rhs=sq[si_], start=True, stop=True),
                           r=[("sq", si_), "blockones"], w=[("ps", ssb)])
                    sch.op("act", lambda e: e.activation(out=rinv[si_], in_=ps[ssb][:, :], func=AF.Ln, scale=1.0 / 64, bias=EPS),
                           r=[("ps", ssb)], w=[("rinv", si_)])
                    sch.op("act", lambda e: e.activation(out=rinv[si_], in_=rinv[si_], func=AF.Exp, scale=-0.5),
                           r=[("rinv", si_)], w=[("rinv", si_)])
                    sch.op("dve", lambda e: e.scalar_tensor_tensor(
                        out=dstA, in0=ps[pb][0:64, :], scalar=gains[0:64, gcA:gcA + 1], in1=rinv[si_][0:64, :],
                        op0=ALU.mult, op1=ALU.mult),
                        r=[("ps", pb), "gains", ("rinv", si_)], w=[keyA])
                    sch.op("dve", lambda e: e.scalar_tensor_tensor(
                        out=dstB, in0=ps[pb][64:128, :], scalar=gains[64:128, gcB:gcB + 1], in1=rinv[si_][64:128, :],
                        op0=ALU.mult, op1=ALU.mult),
                        r=[("ps", pb), "gains", ("rinv", si_)], w=[keyB])
                else:
                    sch.op("act", lambda e: e.copy(out=dstA, in_=ps[pb][0:64, :]), r=[("ps", pb)], w=[keyA])
                    sch.op("dve", lambda e: e.tensor_scalar(out=dstB, in0=ps[pb][64:128, :], scalar1=1.0, scalar2=None, op0=ALU.mult),
                           r=[("ps", pb)], w=[keyB])

            def proj_gates(tt):
                tok = slice(tt * TT, (tt + 1) * TT)
                for k in range(NKC):
                    sch.op("pe", lambda e, k=k: e.matmul(ps[7][0:12, :], lhsT=wB[:, k, 128 + g * 12:140 + g * 12], rhs=hh[:, k, tok],
                                                         start=(k == 0), stop=(k == NKC - 1)),
                           r=[("w", sB, 0), ("hh", k, tt)], w=[("ps", 7)])
                sch.op("act", lambda e: e.activation(out=sgT[:, tok], in_=ps[7][0:12, :], func=AF.Sigmoid),
                       r=[("ps", 7)], w=[("sgT", tt)])

            for tt in range(0 if not DBG.get("proj_nomm") else NTT, DBG.get("proj_ntt", NTT)):
                tok = slice(tt * TT, (tt + 1) * TT)
                proj_pair(tt, 0, qn[:, 0, tok], 0, ("qn", 0, tt), qn[:, 1, tok], 0, ("qn", 1, tt))
                proj_pair(tt, 1, qn[:, 2, tok], 0, ("qn", 2, tt), qn[:, 3, tok], 0, ("qn", 3, tt))
                proj_pair(tt, 2, ks[:, tok], 2, ("ks", tt), kw[:, tok], 3, ("kw", tt))
                proj_pair(tt, 3, kcraw[:, tok], None, "kcraw", vcraw[:, tok], None, "vcraw")
                if not DBG.get("proj_nogates"):
                    proj_gates(tt)

            vtmp = rf(O_T + 7168, 0, 128, 512)

            def proj_v(tt):
                tok = slice(tt * TT, (tt + 1) * TT)
                it = nit[0]
                nit[0] += 1
                pb = it % 4
                for k in range(NKC):
                    sch.op("pe", lambda e, k=k: e.matmul(
                        ps[pb][:, :], lhsT=wB[:, k, 0:128], rhs=hh[:, k, tok], start=(k == 0), stop=(k == NKC - 1)),
                        r=[("w", sB, 0), ("hh", k, tt)], w=[("ps", pb)])
                sch.op("act", lambda e: e.copy(out=vtmp[:, :], in_=ps[pb][:, :]), r=[("ps", pb)], w=["vtmp"])
                for jq in range(4):
                    sch.op("pe", lambda e, jq=jq: e.transpose(out=ps[6][:, jq * 128:(jq + 1) * 128], in_=vtmp[:, jq * 128:(jq + 1) * 128],
                                                            identity=ident[:, :]),
                           r=["vtmp", "ident"], w=[("ps", 6)])
                p6 = ps[6][:, :].rearrange("p (q c) -> p q c", c=128)
                if DBG.get("v_nocopy"):
                    return
                sch.op("act", lambda e: e.copy(out=Vs[:, 4 * tt:4 * tt + 4, 0:64], in_=p6[:, :, 0:64]), r=[("ps", 6), "VsOnes"],
                       w=[("Vs", 4 * tt + q_) for q_ in range(4)])
                if DBG.get("v_onecopy"):
                    return
                sch.op("act", lambda e: e.copy(out=Vw[:, 4 * tt:4 * tt + 4, 0:64], in_=p6[:, :, 64:128]), r=[("ps", 6), "VwOnes"],
                       w=[("Vw", 4 * tt + q_) for q_ in range(4)])

            for tt in range(NTT):
                if not DBG.get("proj_nov"):
                    proj_v(tt)
            if DBG.get("nsa_stop") == "proj":
                sch.barrier()
                return

            def compress(kv):
                raw = kcraw if kv == 0 else vcraw
                rawkey = "kcraw" if kv == 0 else "vcraw"
                raw3 = raw.rearrange("p (n s) -> p n s", s=16)
                w1 = []
                for hf in range(2):
                    s1 = wslot()
                    wv1 = wring[s1][0:64, 0:4096].rearrange("p (t c) -> p t c", t=16)
                    sch.op("pool", lambda e, wv1=wv1, hf=hf: e.dma_start(
                        out=wv1, in_=nsa_w1[j, kv, hf * 1024:(hf + 1) * 1024, :].rearrange("(t p) c -> p t c", p=64)),
                        w=wkeys(s1), dma=True)
                    w1.append((s1, wv1))

                def c_half(half):
                    for tau in range(32):
                        s1, wv1 = w1[tau // 16]
                        wsrc = wv1[:, tau % 16, half * 128:(half + 1) * 128]
                        n0, sidx = (0, tau) if tau < 16 else (1, tau - 16)
                        sch.op("pe", lambda e, wsrc=wsrc, n0=n0, sidx=sidx, tau=tau: e.matmul(
                            ps[half][:, 0:127], lhsT=wsrc, rhs=raw3[:, n0:n0 + 127, sidx], start=(tau == 0), stop=(tau == 31)),
                            r=[("w", s1, 0), rawkey], w=[("ps", half)])
                        sch.op("pe", lambda e, wsrc=wsrc, tau=tau: e.matmul(
                            ps[2][:, half:half + 1], lhsT=wsrc, rhs=posT[:, kv, tau:tau + 1], start=(tau == 0), stop=(tau == 31)),
                            r=[("w", s1, 0), ("posT", kv)], w=[("ps", 2)])
                    col = kv * 2 + half
                    sch.op("dve", lambda e: e.tensor_copy(out=bias_sb[:, col:col + 1], in_=ps[2][:, half:half + 1]),
                           r=[("ps", 2)], w=[("bias_sb", col)])
                    sch.op("act", lambda e: e.activation(out=h1[:, half, 0:127], in_=ps[half][:, 0:127],
                                                         func=AF.Silu, bias=bias_sb[:, col:col + 1], scale=1.0),
                           r=[("ps", half), ("bias_sb", col)], w=[("h1", half)])

                c_half(0)
                c_half(1)
                if kv == 0:
                    for half in range(2):
                        sch.op("pe", lambda e, half=half: e.matmul(ps[3][0:64, 0:127], lhsT=w2sb[:, 0, half, :], rhs=h1[:, half, 0:127],
                                                                 start=(half == 0), stop=(half == 1)),
                               r=[("w2sb", 0), ("h1", half)], w=[("ps", 3)])
                    sch.op("act", lambda e: e.activation(out=sq[0][0:64, 0:127], in_=ps[3][0:64, 0:127], func=AF.Square),
                           r=[("ps", 3)], w=[("sq", 0)])
                    sch.op("pe", lambda e: e.matmul(ps[4][0:64, 0:127], lhsT=onesb[0:64, 0:64], rhs=sq[0][0:64, 0:127], start=True, stop=True),
                           r=[("sq", 0), "onesb"], w=[("ps", 4)])
                    sch.op("act", lambda e: e.activation(out=rinv[0][0:64, 0:127], in_=ps[4][0:64, 0:127], func=AF.Ln,
                                                         scale=1.0 / 64, bias=EPS), r=[("ps", 4)], w=[("rinv", 0)])
                    sch.op("act", lambda e: e.activation(out=rinv[0][0:64, 0:127], in_=rinv[0][0:64, 0:127], func=AF.Exp, scale=-0.5),
                           r=[("rinv", 0)], w=[("rinv", 0)])
                    sch.op("dve", lambda e: e.scalar_tensor_tensor(out=kcn[:, 0:127], in0=ps[3][0:64, 0:127], scalar=gains[0:64, 1:2],
                                                                  in1=rinv[0][0:64, 0:127], op0=ALU.mult, op1=ALU.mult),
                           r=[("ps", 3), "gains", ("rinv", 0)], w=["kcn"])
                else:
                    for half in range(2):
                        sch.op("pe", lambda e, half=half: e.matmul(ps[3][0:127, 0:64], lhsT=h1[:, half, 0:127], rhs=w2sb[:, 1, half, :],
                                                                 start=(half == 0), stop=(half == 1)),
                               r=[("w2sb", 1), ("h1", half)], w=[("ps", 3)])
                    sch.op("act", lambda e: e.copy(out=vca[0:127, 0:64], in_=ps[3][0:127, 0:64]), r=[("ps", 3)], w=["vca"])

            compress(0)
            compress(1)
            sch.barrier()
            if DBG.get("nsa_stop") == "compress":
                return

            def comp_head(I, h):
                qs = slice(512 * I, 512 * I + 512)
                stb = cnt["st"] % 2
                cnt["st"] += 1
                otb = 2 + cnt["ot"] % 2
                cnt["ot"] += 1
                sch.op("pe", lambda e: e.matmul(ps[stb][:, :], lhsT=kcn[:, :], rhs=qn[:, h, qs], start=True, stop=True),
                       r=["kcn", ("qn", h, I)], w=[("ps", stb)])
                sch.op("act", lambda e: e.activation(out=PT[stb], in_=ps[stb][:, :], func=AF.Exp, scale=0.125),
                       r=[("ps", stb)], w=[("PT", stb)])
                sch.op("pool", lambda e: e.tensor_tensor(out=PT[stb], in0=PT[stb], in1=cm01[:, qs], op=ALU.mult),
                       r=[("PT", stb), "cm01"], w=[("PT", stb)])
                sch.op("pe", lambda e: e.matmul(ps[otb][:, :], lhsT=vca[:, :], rhs=PT[stb], start=True, stop=True),
                       r=["vca", ("PT", stb)], w=[("ps", otb)])
                for qb in range(4):
                    sch.op("pe", lambda e, qb=qb: e.matmul(ps[5][:, qb * 33:(qb + 1) * 33], lhsT=PT[stb][:, qb * 128:(qb + 1) * 128],
                                                           rhs=selaug[:, 0:33], start=True, stop=True),
                           r=[("PT", stb), "selaug"], w=[("ps", 5)])
                imp3 = ps[5][:, 0:132].rearrange("p (q c) -> p q c", c=33)
                sch.op("dve", lambda e: e.tensor_scalar(out=dn[:, :], in0=imp3[:, :, 32], scalar1=1e-30, scalar2=None, op0=ALU.max),
                       r=[("ps", 5)], w=["dn"])
                sch.op("dve", lambda e: e.reciprocal(out=rcp[:, :], in_=dn[:, :]), r=["dn"], w=["rcp"])
                for qb in range(4):
                    if h == 0:
                        sch.op("dve", lambda e, qb=qb: e.tensor_scalar(out=impacc[:, qb, :], in0=imp3[:, qb, 0:32],
                                                                       scalar1=rcp[:, qb:qb + 1], scalar2=None, op0=ALU.mult),
                               r=[("ps", 5), "rcp"], w=[("impacc", qb)])
                    else:
                        sch.op("dve", lambda e, qb=qb: e.scalar_tensor_tensor(
                            out=impacc[:, qb, :], in0=imp3[:, qb, 0:32], scalar=rcp[:, qb:qb + 1], in1=impacc[:, qb, :],
                            op0=ALU.mult, op1=ALU.add), r=[("ps", 5), "rcp", ("impacc", qb)], w=[("impacc", qb)])
                finalize(h, I, 0, otb, qs)

            def topk(I, qb):
                t = 4 * I + qb
                sch.op("dve", lambda e: e.tensor_tensor(out=impm[:, :], in0=impacc[:, qb, :], in1=addmask[:, t, :], op=ALU.add),
                       r=[("impacc", qb), "addmask"], w=["impm"])
                sch.op("dve", lambda e: e.max(out=top8[:, :], in_=impm[:, :]), r=["impm"], w=["top8"])
                sch.op("dve", lambda e: e.tensor_scalar(out=selm[:, :], in0=impm[:, :], scalar1=top8[:, 7:8], scalar2=-1.0,
                                                        op0=ALU.is_ge, op1=ALU.add), r=["impm", "top8"], w=["selm"])
                sch.op("pe", lambda e: e.matmul(ps[6][0:32, 0:128], lhsT=selm[:, :], rhs=ident30k[:, :], start=True, stop=True),
                       r=["selm", "ident30k"], w=[("ps", 6)])
                for h in range(4):
                    if True:
                        sch.op("act", lambda e, h=h: e.copy(out=nsT[:, h, t * 128:(t + 1) * 128], in_=ps[6][0:32, 0:128]),
                               r=[("ps", 6)], w=[("nsT", h, t)])
                    else:
                        sch.op("dve", lambda e, h=h: e.tensor_scalar(out=nsT[:, h, t * 128:(t + 1) * 128], in0=ps[6][0:32, 0:128],
                                                                     scalar1=1.0, scalar2=None, op0=ALU.mult),
                               r=[("ps", 6)], w=[("nsT", h, t)])

            def sel_tile(I, h, jj, otb, nj):
                r_ = jj - 4 * I
                c0 = 128 * r_ if r_ > 0 else 0
                stb = cnt["st"] % 2
                cnt["st"] += 1
                sch.op("pe", lambda e: e.matmul(
                    ps[stb][:, c0:512], lhsT=kse[:, jj * 128:(jj + 1) * 128], rhs=qx[:, h, 512 * I + c0:512 * I + 512],
                    start=True, stop=True),
                    r=[("ks", jj // 4), "emat", ("qn", h, I)] + [("nsT", h, 4 * I + q_) for q_ in range(4)], w=[("ps", stb)])
                sch.op("act", lambda e: e.activation(out=PT[stb][:, c0:512], in_=ps[stb][:, c0:512], func=AF.Exp, scale=0.125),
                       r=[("ps", stb)], w=[("PT", stb)])
                if r_ >= 0:
                    off = 512 * I - 128 * jj + 384
                    sch.op("pool", lambda e: e.tensor_tensor(
                        out=PT[stb][:, c0:512], in0=PT[stb][:, c0:512], in1=W01[:, off + c0:off + 512], op=ALU.mult),
                        r=[("PT", stb), "W01"], w=[("PT", stb)])
                sch.op("pe", lambda e: e.matmul(
                    ps[otb][:, c0:512], lhsT=Vs[:, jj, :], rhs=PT[stb][:, c0:512], start=(jj == 0), stop=(jj == nj - 1)),
                    r=[("Vs", jj), ("PT", stb)], w=[("ps", otb)])

            def win_tile(I, h, jj, otb, j0):
                m = jj - (4 * I - 4)
                r_lo = max(0, m - 4)
                r_hi = min(3, m)
                c0, c1 = 128 * r_lo, 128 * (r_hi + 1)
                off = 512 * I - 128 * jj + 384
                stb = cnt["st"] % 2
                cnt["st"] += 1
                sch.op("pe", lambda e: e.matmul(
                    ps[stb][:, c0:c1], lhsT=kw[:, jj * 128:(jj + 1) * 128], rhs=qn[:, h, 512 * I + c0:512 * I + c1],
                    start=True, stop=True),
                    r=[("kw", jj // 4), ("qn", h, I)], w=[("ps", stb)])
                sch.op("act", lambda e: e.activation(out=PT[stb][:, c0:c1], in_=ps[stb][:, c0:c1], func=AF.Exp, scale=0.125),
                       r=[("ps", stb)], w=[("PT", stb)])
                sch.op("pool", lambda e: e.tensor_tensor(
                    out=PT[stb][:, c0:c1], in0=PT[stb][:, c0:c1], in1=W01[:, off + c0:off + c1], op=ALU.mult),
                    r=[("PT", stb), "W01"], w=[("PT", stb)])
                sch.op("pe", lambda e: e.matmul(
                    ps[otb][:, c0:c1], lhsT=Vw[:, jj, :], rhs=PT[stb][:, c0:c1], start=(jj == j0), stop=(jj == 4 * I + 3),
                    skip_group_check=True),
                    r=[("Vw", jj), ("PT", stb)], w=[("ps", otb)])

            def sel_win_head(I, h):
                qs = slice(512 * I, 512 * I + 512)
                otb = 2 + cnt["ot"] % 2
                cnt["ot"] += 1
                nj = 4 * I + 4
                for jj in range(nj):
                    sel_tile(I, h, jj, otb, nj)
                finalize(h, I, 1, otb, qs)
                otb = 2 + cnt["ot"] % 2
                cnt["ot"] += 1
                j0 = max(0, 4 * I - 4)
                for jj in range(j0, 4 * I + 4):
                    win_tile(I, h, jj, otb, j0)
                finalize(h, I, 2, otb, qs)

            for I in range(DBG.get("nsa_nI", 4)):
                for h in range(4):
                    comp_head(I, h)
                for qb in range(4):
                    topk(I, qb)
                for h in range(4):
                    sel_win_head(I, h)

            so = wslot()
            wo = wring[so][64:128, 0:4096].rearrange("p (h d) -> p h d", h=4)
            for h in range(4):
                sch.op("pool", lambda e, h=h: e.dma_start(
                    out=wo[:, h, :], in_=nsa_w_out[j, (4 * g + h) * 64:(4 * g + h + 1) * 64, :]), w=wkeys(so), dma=True)

            def outproj(dc, tt):
                tok = slice(tt * TT, (tt + 1) * TT)
                pb = (dc * NTT + tt) % 2
                for h in range(4):
                    sch.op("pe", lambda e, h=h: e.matmul(
                        ps[pb][:, :], lhsT=wo[:, h, dc * 128:(dc + 1) * 128], rhs=oT[h][:, tok], start=(h == 0), stop=(h == 3)),
                        r=[("w", so, 0), ("oT", h, tt)], w=[("ps", pb)])
                sch.op("dve", lambda e: e.scalar_tensor_tensor(
                    out=xT[:, dc, tok], in0=ps[pb][:, :], scalar=modT[:, g1col + dc:g1col + dc + 1], in1=xT[:, dc, tok],
                    op0=ALU.mult, op1=ALU.add),
                    r=[("ps", pb), "modT", ("xT", dc, tt)], w=[("xT", dc, tt)])

            for dc in range(NKC):
                for tt in range(NTT):
                    outproj(dc, tt)
            sch.barrier()

        for g in range(DBG.get("ngroups", 4)):
            do_group(g)


    def ffn(l):
        gcol = 40
        HT = 1024
        hbuf = arena[:, 0:NHC * HT].rearrange("p (c t) -> p c t", c=NHC)
        sg = [arena_f32(NHC * HT * 2 + i * 2048, 128, [512]) for i in range(2)]
        for half in range(DBG.get("halves", 2)):
            for cp in range(DBG.get("ncp", NHC // 2)):
                si = wslot()
                wv = wring[si][:, 0:NKC * 512].rearrange("p (k c) -> p k c", k=NKC)
                srcg = ffn_w_in[l, :, cp * 256:(cp + 1) * 256].rearrange("(k p) c -> p k c", p=128)
                srcu = ffn_w_in[l, :, FFN_H + cp * 256:FFN_H + (cp + 1) * 256].rearrange("(k p) c -> p k c", p=128)
                sch.op("pool", lambda e, wv=wv, srcg=srcg: e.dma_start(out=wv[:, :, 0:256], in_=srcg),
                       w=[("w", si, 0), ("w", si, 1), ("w", si, 2), ("w", si, 3)], dma=True)
                sch.op("pool", lambda e, wv=wv, srcu=srcu: e.dma_start(out=wv[:, :, 256:512], in_=srcu),
                       w=[("w", si, 1)], dma=True)
                for ci in range(2):
                    c = cp * 2 + ci
                    for t2 in range(2):
                        tt = half * 2 + t2
                        tok = slice(tt * TT, (tt + 1) * TT)
                        pg = (2 * (ci * 2 + t2)) % 4
                        pu = pg + 1
                        for k in range(NKC):
                            sch.op("pe", lambda e, wv=wv, k=k, ci=ci, tok=tok, pg=pg: e.matmul(
                                ps[pg][:, :], lhsT=wv[:, k, ci * 128:(ci + 1) * 128], rhs=hh[:, k, tok],
                                start=(k == 0), stop=(k == NKC - 1)),
                                r=[("w", si, 0), ("hh", k, tt)], w=[("ps", pg)])
                        for k in range(NKC):
                            sch.op("pe", lambda e, wv=wv, k=k, ci=ci, tok=tok, pu=pu: e.matmul(
                                ps[pu][:, :], lhsT=wv[:, k, 256 + ci * 128:256 + (ci + 1) * 128], rhs=hh[:, k, tok],
                                start=(k == 0), stop=(k == NKC - 1)),
                                r=[("w", si, 1), ("hh", k, tt)], w=[("ps", pu)])
                        gi = (ci * 2 + t2) % 2
                        sch.op("act", lambda e, pg=pg, gi=gi: e.activation(out=sg[gi], in_=ps[pg][:, :], func=AF.Silu),
                               r=[("ps", pg)], w=[("sg", gi)])
                        sch.op("dve", lambda e, pu=pu, gi=gi, c=c, t2=t2: e.tensor_tensor(
                            out=hbuf[:, c, t2 * TT:(t2 + 1) * TT], in0=ps[pu][:, :], in1=sg[gi], op=ALU.mult),
                            r=[("ps", pu), ("sg", gi)], w=[("hbuf", c, t2)])
            for dc in range(DBG.get("ndc", NKC)):
                si = wslot()
                wv = wring[si][:, 0:NHC * 128].rearrange("p (c d) -> p c d", c=NHC)
                src = ffn_w_out[l, :, dc * 128:(dc + 1) * 128].rearrange("(c p) d -> p c d", p=128)
                sch.op("pool", lambda e, wv=wv, src=src: e.dma_start(out=wv[:, 0:11, :], in_=src[:, 0:11, :]),
                       w=[("w", si, 0), ("w", si, 1), ("w", si, 2), ("w", si, 3)], dma=True)
                sch.op("pool", lambda e, wv=wv, src=src: e.dma_start(out=wv[:, 11:22, :], in_=src[:, 11:22, :]),
                       w=[("w", si, 1)], dma=True)
                for t2 in range(2):
                    tt = half * 2 + t2
                    tok = slice(tt * TT, (tt + 1) * TT)
                    pb = 4 + ((dc * 2 + t2) % 2)
                    for c in range(NHC):
                        sch.op("pe", lambda e, wv=wv, c=c, t2=t2, pb=pb: e.matmul(
                            ps[pb][:, :], lhsT=wv[:, c, :], rhs=hbuf[:, c, t2 * TT:(t2 + 1) * TT],
                            start=(c == 0), stop=(c == NHC - 1)),
                            r=[("w", si, 0 if c < 11 else 1), ("hbuf", c, t2)], w=[("ps", pb)])
                    sch.op("dve", lambda e, pb=pb, dc=dc, tok=tok: e.scalar_tensor_tensor(
                        out=xT[:, dc, tok], in0=ps[pb][:, :], scalar=modT[:, gcol + dc:gcol + dc + 1], in1=xT[:, dc, tok],
                        op0=ALU.mult, op1=ALU.add),
                        r=[("ps", pb), "modT", ("xT", dc, tt)], w=[("xT", dc, tt)])

    for l in layers:
        adaln(l)
        if do_mixer:
            rmsnorm_mod(0)
            sch.barrier()
            if l % 2 == 1:
                rglru(l)
            else:
                nsa(l)
            sch.barrier()
        if do_ffn:
            rmsnorm_mod(1)
            sch.barrier()
            if do_ffn != "norm":
                ffn(l)
            sch.barrier()

    xo = [arena_f32(i * 4096, 128, [D]) for i in range(4)]
    for t in range(S // 128):
        bi = t % 4
        for half in range(2):
            pb = (2 * t + half) % 2
            for j in range(4):
                c = half * 4 + j
                sch.op("pe", lambda e, t=t, c=c, pb=pb, j=j: e.transpose(
                    out=ps[pb][:, j * 128:(j + 1) * 128], in_=xT[:, c, t * 128:(t + 1) * 128], identity=ident[:]),
                    r=[("xT", c, t // 4), "ident"], w=[("ps", pb)])
            eng = "act" if half == 0 else "dve"

            def cp(e, bi=bi, half=half, pb=pb, eng=eng):
                out = xo[bi][:, half * 512:(half + 1) * 512]
                if eng == "act":
                    return e.copy(out=out, in_=ps[pb][:, :])
                return e.tensor_copy(out=out, in_=ps[pb][:, :])
            sch.op(eng, cp, r=[("ps", pb)], w=[("xo", bi, half)])
        i = sch.op("sp", lambda e, t=t, bi=bi: e.dma_start(out=y_d[t * 128:(t + 1) * 128, :], in_=xo[bi]),
                   r=[("xo", bi, 0), ("xo", bi, 1)], dma=True)
        sch.ops[i]["final"] = True

    sch.emit(es)
    es.close()
    return nc


def _structural_constants():
    kk = np.arange(128)[:, None]
    xi = np.arange(1408)[None, :]
    dlt = (xi - 384) - kk
    w01 = ((dlt >= 0) & (dlt < 512)).astype(np.float32)
    c = np.arange(128)[:, None]
    t = np.arange(S)[None, :]
    cm01 = ((c < 127) & (16 * c + 31 <= t)).astype(np.float32)
    n_c, n_s = S // 16 - 1, S // 64
    tok = np.arange(S)
    start = np.arange(n_c) * 16
    cover_c = (tok[None, :] >= start[:, None]) & (tok[None, :] < start[:, None] + 32)
    cover_s = (tok[:, None] // 64) == np.arange(n_s)[None, :]
    sm = cover_c.astype(np.float32) @ cover_s.astype(np.float32) / np.float32(32)
    selaug = np.zeros((128, 33), np.float32)
    selaug[:n_c, :32] = sm
    selaug[:n_c, 32] = 1.0
    q = np.arange(128)[:, None, None]
    tb = np.arange(16)[None, :, None]
    sb_ = np.arange(32)[None, None, :]
    cur = (128 * tb + q) // 64
    forced = (sb_ == 0) | (sb_ == cur) | (sb_ == cur - 1)
    addmask = np.where(forced, 1e30, np.where(sb_ > cur, -1e30, 0.0)).astype(np.float32).reshape(128, 512)
    emat = ((np.arange(S)[None, :] // 64) == np.arange(32)[:, None]).astype(np.float32)
    gsel = np.zeros((12, 12, 64), np.float32)
    for a in range(12):
        gsel[a, a, :] = 1.0
    return {"c_w01": w01, "c_cm01": cm01, "c_selaug": selaug, "c_addmask": addmask, "c_emat": emat,
            "c_gsel": gsel.reshape(12, 768)}


def make_in_maps(inputs):
    f = lambda a: np.ascontiguousarray(np.asarray(a, dtype=np.float32))
    shared = {
        "ada_w": f(inputs["ada_w"]),
        "ada_b": f(inputs["ada_b"]).reshape(DEPTH, 48, 128),
        "norm1_g": f(inputs["norm1_g"]).reshape(DEPTH, NKC, 128),
        "norm2_g": f(inputs["norm2_g"]).reshape(DEPTH, NKC, 128),
        "ffn_w_in": f(inputs["ffn_w_in"]),
        "ffn_w_out": f(inputs["ffn_w_out"]),
        "rg_w_in": f(inputs["rg_w_in"]),
        "rg_conv_w": f(inputs["rg_conv_w"]).reshape(2, 4, RB, RBW),
        "rg_conv_b": f(inputs["rg_conv_b"]).reshape(2, RB, RBW),
        "rg_w_a": f(inputs["rg_w_a"]),
        "rg_b_a": f(inputs["rg_b_a"]),
        "rg_w_x": f(inputs["rg_w_x"]),
        "rg_b_x": f(inputs["rg_b_x"]),
        "rg_lam": f(inputs["rg_lam"]).reshape(2, RB, RBW),
        "rg_w_out": f(inputs["rg_w_out"]),
        "ident_in": np.eye(128, dtype=np.float32),
        "nsa_w_in": f(inputs["nsa_w_in"]),
        "nsa_w_out": f(inputs["nsa_w_out"]),
        "nsa_cmp_pos": f(inputs["nsa_cmp_pos"]),
        "nsa_cmp_w1": f(inputs["nsa_cmp_w1"]),
        "nsa_cmp_w2": f(inputs["nsa_cmp_w2"]),
        "nsa_q_gain": f(inputs["nsa_q_gain"]).reshape(2, 1, 64),
        "nsa_k_gain": f(inputs["nsa_k_gain"]),
    }
    shared.update(_structural_constants())
    x = f(inputs["x"])
    c = f(inputs["c"])
    maps = []
    for b in range(8):
        m = dict(shared)
        m["x"] = x[b]
        m["c"] = c[b].reshape(NKC, 128)
        maps.append(m)
    return maps


_NC_CACHE = {}


def kernel(**inputs):
    if "nc" not in _NC_CACHE:
        _NC_CACHE["nc"] = build_program()
    nc = _NC_CACHE["nc"]
    maps = make_in_maps(inputs)
    res = run_bass_kernel_spmd(nc, maps, core_ids=list(range(8)))
    out = np.stack([np.asarray(r["y"], dtype=np.float32) for r in res.results], axis=0)
    return out
```

```python
import numpy as np
from contextlib import ExitStack
import concourse.bass as bass
import concourse.mybir as mybir
from concourse.bass_utils import run_bass_kernel_spmd

F32 = mybir.dt.float32
BF16 = mybir.dt.bfloat16
AF = mybir.ActivationFunctionType
ALU = mybir.AluOpType
AX = mybir.AxisListType

D = 1024
S = 2048
DEPTH = 4
NKC = 8
TT = 512
NTT = S // TT
FFN_H = 2816
NHC = FFN_H // 128
EPS = 1e-6
RNN = 1408
RB = 16
RBW = 88
NSA_IN = 2608
DBG = {}
SWDGE_DEPTH = 2


class Sched:
    STREAMS = ("pe", "act", "dve", "pool", "sp")

    def __init__(self, nc, n_lanes=6):
        self.nc = nc
        self.ops = []
        self.lastw = {}
        self.readers = {}
        self.n_lanes = n_lanes
        self.pending = {s: set() for s in self.STREAMS}
        self.last_on = {s: None for s in self.STREAMS}
        self.dma_since_barrier = []

    def op(self, stream, fn, r=(), w=(), dma=False):
        i = len(self.ops)
        deps = set()
        for k in list(r) + list(w):
            if k in self.lastw:
                deps.add(self.lastw[k])
        for k in w:
            deps.update(self.readers.get(k, ()))
        if dma and stream == "pool":
            self._swq = getattr(self, "_swq", [])
            if len(self._swq) >= SWDGE_DEPTH:
                deps.add(self._swq[-SWDGE_DEPTH])
            self._swq.append(i)
        deps |= self.pending[stream]
        self.pending[stream] = set()
        deps.discard(i)
        self.ops.append(dict(stream=stream, fn=fn, deps=deps, dma=dma, needed=False))
        for k in w:
            self.lastw[k] = i
            self.readers[k] = []
        for k in r:
            self.readers.setdefault(k, []).append(i)
        self.last_on[stream] = i
        if dma:
            self.dma_since_barrier.append(i)
        return i

    def barrier(self, streams=("pe", "act", "dve", "sp")):
        deps = set(self.dma_since_barrier)
        for s in streams:
            if self.last_on[s] is not None:
                deps.add(self.last_on[s])
        self.dma_since_barrier = []
        for s in streams:
            self.pending[s] |= deps

    def emit(self, es):
        nc = self.nc
        ops = self.ops
        for o in ops:
            for d in o["deps"]:
                dd = ops[d]
                if dd["stream"] == "pe" and o["stream"] == "pe" and not dd["dma"] and not o["dma"]:
                    continue
                dd["needed"] = True
        sems = {s: es.enter_context(nc.semaphore("sem_" + s)) for s in self.STREAMS}
        lanes = {"hw": [es.enter_context(nc.semaphore("lane%d" % i)) for i in range(self.n_lanes)],
                 "sw": [es.enter_context(nc.semaphore("swlane%d" % i)) for i in range(self.n_lanes)]}
        cnt = {s: 0 for s in self.STREAMS}
        ndma = {"hw": 0, "sw": 0}
        for o in ops:
            if o["dma"]:
                kind = "sw" if o["stream"] == "pool" else "hw"
                lane = ndma[kind] % self.n_lanes
                use = ndma[kind] // self.n_lanes
                o["comp"] = (lanes[kind][lane], 16 * (use + 1))
                o["pre"] = (lanes[kind][lane], 16 * use) if use > 0 else None
                ndma[kind] += 1
            else:
                o["pre"] = None
                if o["needed"]:
                    cnt[o["stream"]] += 1
                    o["comp"] = (sems[o["stream"]], cnt[o["stream"]])
                else:
                    o["comp"] = None
        per_stream = {s: [o for o in ops if o["stream"] == s] for s in self.STREAMS}
        block = es.enter_context(nc.Block())

        def run_stream(eng, lst, sname):
            waited = {}
            for o in lst:
                need = {}
                for d in o["deps"]:
                    dd = ops[d]
                    if dd["stream"] == "pe" and sname == "pe" and not dd["dma"] and not o["dma"]:
                        continue
                    sem, val = dd["comp"]
                    key = id(sem)
                    if key not in need or need[key][1] < val:
                        need[key] = (sem, val)
                if o["pre"] is not None:
                    sem, val = o["pre"]
                    key = id(sem)
                    if key not in need or need[key][1] < val:
                        need[key] = (sem, val)
                for key, (sem, val) in need.items():
                    if waited.get(key, 0) >= val:
                        continue
                    eng.wait_ge(sem, val)
                    waited[key] = val
                ins = o["fn"](eng)
                if o["dma"]:
                    ins.then_inc(o["comp"][0], 16)
                elif o["comp"] is not None:
                    ins.then_inc(o["comp"][0], 1)

        @block.tensor
        def _(e):
            run_stream(e, per_stream["pe"], "pe")

        @block.scalar
        def _(e):
            run_stream(e, per_stream["act"], "act")

        @block.vector
        def _(e):
            run_stream(e, per_stream["dve"], "dve")

        @block.gpsimd
        def _(e):
            run_stream(e, per_stream["pool"], "pool")

        @block.sync
        def _(e):
            run_stream(e, per_stream["sp"], "sp")
            for o in ops:
                if o["dma"] and o.get("final"):
                    e.wait_ge(o["comp"][0], o["comp"][1])


def build_program(layers=(0, 1, 2, 3), do_mixer=True, do_ffn=True):
    nc = bass.Bass("TRN2", target_bir_lowering=False)
    es = ExitStack()
    dram = {}

    def din(name, shape, dt=F32):
        dram[name] = nc.dram_tensor(name, list(shape), dt, kind="ExternalInput").ap()
        return dram[name]

    x_d = din("x", [S, D])
    c_d = din("c", [NKC, 128])
    ada_w = din("ada_w", [DEPTH, D, 6 * D])
    ada_b = din("ada_b", [DEPTH, 48, 128])
    n1g = din("norm1_g", [DEPTH, NKC, 128])
    n2g = din("norm2_g", [DEPTH, NKC, 128])
    ffn_w_in = din("ffn_w_in", [DEPTH, D, 2 * FFN_H])
    ffn_w_out = din("ffn_w_out", [DEPTH, FFN_H, D])
    rg_w_in = din("rg_w_in", [2, D, 2 * RNN])
    rg_conv_w = din("rg_conv_w", [2, 4, RB, RBW])
    rg_conv_b = din("rg_conv_b", [2, RB, RBW])
    rg_w_a = din("rg_w_a", [2, RB, RBW, RBW])
    rg_b_a = din("rg_b_a", [2, RB, RBW])
    rg_w_x = din("rg_w_x", [2, RB, RBW, RBW])
    rg_b_x = din("rg_b_x", [2, RB, RBW])
    rg_lam = din("rg_lam", [2, RB, RBW])
    rg_w_out = din("rg_w_out", [2, RNN, D])
    ident_d = din("ident_in", [128, 128])
    nsa_w_in = din("nsa_w_in", [2, D, NSA_IN])
    nsa_w_out = din("nsa_w_out", [2, D, D])
    nsa_pos = din("nsa_cmp_pos", [2, 2, 32, 64])
    nsa_w1 = din("nsa_cmp_w1", [2, 2, 2048, 256])
    nsa_w2 = din("nsa_cmp_w2", [2, 2, 256, 64])
    nsa_qg = din("nsa_q_gain", [2, 1, 64])
    nsa_kg = din("nsa_k_gain", [2, 3, 64])
    c_w01 = din("c_w01", [128, 1408])
    c_cm01 = din("c_cm01", [128, 2048])
    c_selaug = din("c_selaug", [128, 33])
    c_addmask = din("c_addmask", [128, 512])
    c_emat = din("c_emat", [32, 2048])
    c_gsel = din("c_gsel", [12, 768])
    y_d = nc.dram_tensor("y", [S, D], F32, kind="ExternalOutput").ap()

    sch = Sched(nc)

    def sb(name, shape, dt):
        return es.enter_context(nc.sbuf_tensor("sb_" + name, list(shape), dt))

    xT = sb("xT", [128, NKC, S], F32)
    hh = sb("hh", [128, NKC, S], BF16)
    ident = sb("ident", [128, 128], F32)
    identb = sb("identb", [128, 128], BF16)
    onesb = sb("onesb", [128, 128], BF16)
    condT = sb("condT", [128, NKC], F32)
    condTb = sb("condTb", [128, NKC], BF16)
    c_sb = sb("c_sb", [NKC, 128], F32)
    modT = sb("modT", [128, 48], F32)
    adab_sb = sb("adab_sb", [48, 128], F32)
    ng_sb = sb("ng_sb", [2 * NKC, 128], F32)
    ngT = sb("ngT", [128, 2 * NKC], F32)
    Avec = sb("Avec", [128, 2 * NKC], F32)
    sb_rgv = sb("rgv", [RBW, 128], F32)
    gains = sb("gains", [128, 4], F32)
    blockones = sb("blockones", [128, 128], BF16)
    w2sb = sb("w2sb", [128, 2, 2, 64], BF16)
    posT = sb("posT", [64, 2, 32], BF16)
    ident30k = sb("ident30k", [128, 128], BF16)
    bias_sb = sb("bias_sb", [128, 4], F32)
    NW = 4
    WSLOT = 4096
    wring = [sb("wring%d" % i, [128, WSLOT], BF16) for i in range(NW)]
    ARENA = 36 * 1024
    arena = sb("arena", [128, ARENA], BF16)
    ps = [es.enter_context(nc.psum_tensor("ps%d" % i, [128, 512], F32)) for i in range(8)]

    wctr = [0]

    def wslot():
        i = wctr[0] % NW
        wctr[0] += 1
        return i

    def arena_f32(off_bytes, parts, shape_free):
        n = int(np.prod(shape_free))
        ap = arena[0:parts, off_bytes // 2: off_bytes // 2 + 2 * n].bitcast(F32)
        return ap

    sch.op("sp", lambda e: e.dma_start(out=ident[:], in_=ident_d[:, :]), w=["ident"], dma=True)
    sch.op("sp", lambda e: e.dma_start(out=c_sb[:], in_=c_d[:, :]), w=["c_sb"], dma=True)
    sch.op("dve", lambda e: e.tensor_copy(out=identb[:], in_=ident[:]), r=["ident"], w=["identb"])
    sch.op("dve", lambda e: e.memset(onesb[:], 1.0), w=["onesb"])
    sch.op("dve", lambda e: e.memset(blockones[:], 0.0), w=["blockones"])
    sch.op("dve", lambda e: e.memset(blockones[0:64, 0:64], 1.0), w=["blockones"])
    sch.op("dve", lambda e: e.memset(blockones[64:128, 64:128], 1.0), w=["blockones"])
    sch.op("dve", lambda e: e.tensor_scalar(out=ident30k[:], in0=ident[:], scalar1=30000.0, scalar2=None, op0=ALU.mult),
           r=["ident"], w=["ident30k"])
    sch.op("pe", lambda e: e.transpose(out=ps[0][:, 0:NKC], in_=c_sb[:, :], identity=ident[0:NKC, 0:NKC]),
           r=["c_sb", "ident"], w=[("ps", 0)])
    sch.op("act", lambda e: e.activation(out=condT[:], in_=ps[0][:, 0:NKC], func=AF.Silu),
           r=[("ps", 0)], w=["condT"])
    sch.op("dve", lambda e: e.tensor_copy(out=condTb[:], in_=condT[:]), r=["condT"], w=["condTb"])

    xin = [arena_f32(i * 4096, 128, [D]) for i in range(4)]
    for t in range(S // 128):
        bi = t % 4
        sch.op("sp", lambda e, t=t, bi=bi: e.dma_start(out=xin[bi], in_=x_d[t * 128:(t + 1) * 128, :]),
               w=[("xin", bi)], dma=True)
        for half in range(2):
            pb = (2 * t + half) % 2
            for j in range(4):
                c = half * 4 + j
                sch.op("pe", lambda e, bi=bi, c=c, pb=pb, j=j: e.transpose(
                    out=ps[pb][:, j * 128:(j + 1) * 128], in_=xin[bi][:, c * 128:(c + 1) * 128], identity=ident[:]),
                    r=[("xin", bi), "ident"], w=[("ps", pb)])
            eng = "act" if half == 0 else "dve"

            def cp(e, t=t, half=half, pb=pb, eng=eng):
                out = xT[:, half * 4:(half + 1) * 4, t * 128:(t + 1) * 128]
                in_ = ps[pb][:, :].rearrange("p (j q) -> p j q", j=4)
                if eng == "act":
                    return e.copy(out=out, in_=in_)
                return e.tensor_copy(out=out, in_=in_)
            sch.op(eng, cp, r=[("ps", pb)], w=[("xT", half * 4 + j, t // 4) for j in range(4)])
    sch.barrier()

    def load_w(dram_view, ncols_total, keyname):
        si = wslot()
        dst = wring[si][:, 0:ncols_total]
        return si, dst

    def adaln(l):
        sch.op("sp", lambda e: e.dma_start(out=adab_sb[:], in_=ada_b[l, :, :]), w=["adab_sb"], dma=True)
        sch.op("sp", lambda e: e.dma_start(out=ng_sb[0:NKC, :], in_=n1g[l, :, :]), w=["ng_sb0"], dma=True)
        sch.op("sp", lambda e: e.dma_start(out=ng_sb[NKC:2 * NKC, :], in_=n2g[l, :, :]), w=["ng_sb1"], dma=True)
        PB = 7
        GC = 4
        for g in range(48 // GC):
            si = wslot()
            wv = wring[si][:, 0:NKC * GC * 128].rearrange("p (k c) -> p k c", k=NKC)
            src = ada_w[l, :, g * GC * 128:(g + 1) * GC * 128].rearrange("(k p) c -> p k c", p=128)
            sch.op("pool", lambda e, wv=wv, src=src: e.dma_start(out=wv, in_=src), w=[("w", si, i_) for i_ in range(4)], dma=True)
            for cc in range(GC):
                col = g * GC + cc
                for k in range(NKC):
                    sch.op("pe", lambda e, wv=wv, cc=cc, k=k, col=col: e.matmul(
                        ps[PB][:, col:col + 1], lhsT=wv[:, k, cc * 128:(cc + 1) * 128], rhs=condTb[:, k:k + 1],
                        start=(k == 0), stop=(k == NKC - 1)),
                        r=[("w", si, 0), "condTb"], w=[("ps", PB)])
        sch.op("pe", lambda e: e.transpose(out=ps[PB][:, 64:112], in_=adab_sb[:, :], identity=ident[0:48, 0:48]),
               r=["adab_sb", "ident"], w=[("ps", PB)])
        sch.op("pe", lambda e: e.transpose(out=ps[PB][:, 128:144], in_=ng_sb[:, :], identity=ident[0:16, 0:16]),
               r=["ng_sb0", "ng_sb1", "ident"], w=[("ps", PB)])
        sch.op("dve", lambda e: e.tensor_copy(out=modT[:], in_=ps[PB][:, 64:112]), r=[("ps", PB)], w=["modT"])
        sch.op("dve", lambda e: e.tensor_tensor(out=modT[:], in0=ps[PB][:, 0:48], in1=modT[:], op=ALU.add),
               r=[("ps", PB), "modT"], w=["modT"])
        sch.op("dve", lambda e: e.tensor_copy(out=ngT[:], in_=ps[PB][:, 128:144]), r=[("ps", PB)], w=["ngT"])
        for j, (sccol) in enumerate((8, 32)):
            sch.op("dve", lambda e, j=j, sccol=sccol: e.scalar_tensor_tensor(
                out=Avec[:, j * 8:(j + 1) * 8], in0=modT[:, sccol:sccol + 8], scalar=1.0,
                in1=ngT[:, j * 8:(j + 1) * 8], op0=ALU.add, op1=ALU.mult),
                r=["modT", "ngT"], w=["Avec"])

    def rmsnorm_mod(which):
        shcol = 0 if which == 0 else 24
        sq = [arena[:, ARENA - (i + 1) * 512: ARENA - i * 512] for i in range(2)]
        rstd = [arena_f32(ARENA * 2 - 4096 - (i + 1) * 2048, 128, [512]) for i in range(2)]
        tmp = [arena_f32(ARENA * 2 - 8192 - (i + 1) * 2048, 128, [512]) for i in range(2)]
        n = [0]
        for tt in range(NTT):
            tok = slice(tt * TT, (tt + 1) * TT)
            pb = 5 + (tt % 2)
            for c in range(NKC):
                qi = n[0] % 2
                n[0] += 1
                sch.op("act", lambda e, c=c, tok=tok, qi=qi: e.activation(out=sq[qi], in_=xT[:, c, tok], func=AF.Square),
                       r=[("xT", c, tt)], w=[("sq", qi)])
                sch.op("pe", lambda e, c=c, qi=qi, pb=pb: e.matmul(ps[pb][:, :], lhsT=onesb[:, :], rhs=sq[qi],
                                                                  start=(c == 0), stop=(c == NKC - 1)),
                       r=[("sq", qi), "onesb"], w=[("ps", pb)])
            ri = tt % 2
            sch.op("act", lambda e, pb=pb, ri=ri: e.activation(out=rstd[ri], in_=ps[pb][:, :], func=AF.Sqrt,
                                                               scale=1.0 / D, bias=EPS),
                   r=[("ps", pb)], w=[("rstd", ri)])
            sch.op("dve", lambda e, ri=ri: e.reciprocal(out=rstd[ri], in_=rstd[ri]), r=[("rstd", ri)], w=[("rstd", ri)])
            for c in range(NKC):
                ti = c % 2
                sch.op("dve", lambda e, c=c, tok=tok, ri=ri, ti=ti: e.scalar_tensor_tensor(
                    out=tmp[ti], in0=xT[:, c, tok], scalar=Avec[:, which * 8 + c:which * 8 + c + 1], in1=rstd[ri],
                    op0=ALU.mult, op1=ALU.mult),
                    r=[("xT", c, tt), "Avec", ("rstd", ri)], w=[("ntmp", ti)])
                sch.op("act", lambda e, c=c, tok=tok, ti=ti: e.activation(
                    out=hh[:, c, tok], in_=tmp[ti], func=AF.Identity, bias=modT[:, shcol + c:shcol + c + 1], scale=1.0),
                    r=[("ntmp", ti), "modT"], w=[("hh", c, tt)])


    def rglru(l):
        j = l // 2
        g1col = 16
        rgv_in = arena_f32(0, 128, [RBW])
        rgv = sb_rgv
        sch.op("sp", lambda e: e.dma_start(out=rgv_in[0:64, :], in_=rg_conv_w[j].rearrange("a n w -> (a n) w")),
               w=["rgv_in0"], dma=True)
        for qi, src in enumerate((rg_conv_b, rg_b_a, rg_b_x, rg_lam)):
            sch.op("sp", lambda e, qi=qi, src=src: e.dma_start(out=rgv_in[64 + qi * 16:80 + qi * 16, :], in_=src[j, :, :]),
                   w=["rgv_in%d" % (qi + 1)], dma=True)
        sch.op("pe", lambda e: e.transpose(out=ps[7][0:RBW, 0:128], in_=rgv_in[:, :], identity=ident[:, :]),
               r=["rgv_in%d" % i for i in range(5)] + ["ident"], w=[("ps", 7)])
        sch.op("dve", lambda e: e.tensor_copy(out=rgv[:, :], in_=ps[7][0:RBW, 0:128]), r=[("ps", 7)], w=["rgv"])
        sch.op("act", lambda e: e.activation(out=rgv[:, 112:128], in_=rgv[:, 112:128], func=AF.Exp, scale=-1.0),
               r=["rgv"], w=["rgv"])
        sch.op("act", lambda e: e.activation(out=rgv[:, 112:128], in_=rgv[:, 112:128], func=AF.Ln, bias=1.0, scale=1.0),
               r=["rgv"], w=["rgv"])
        sch.op("dve", lambda e: e.tensor_scalar(out=rgv[:, 112:128], in0=rgv[:, 112:128], scalar1=-8.0, scalar2=None,
                                                op0=ALU.mult), r=["rgv"], w=["rgv"])
        sch.barrier()
        ybuf = arena[0:RBW, 0:4 * S].rearrange("p (n t) -> p n t", n=4)
        base = 4 * S * 2
        G = arena_f32(base, RBW, [S])
        Uraw = arena_f32(base + 8192, RBW, [S + 16])
        Up = arena_f32(base + 2 * 8192 + 64, RBW, [S])
        RA = arena_f32(base + 3 * 8192 + 64, RBW, [S])
        IB = arena_f32(base + 4 * 8192 + 64, RBW, [S])
        MH = arena_f32(base + 5 * 8192 + 64, RBW, [S])
        Upb = arena[0:RBW, (base + 6 * 8192 + 64) // 2:(base + 6 * 8192 + 64) // 2 + S]
        sch.op("dve", lambda e: e.memset(Uraw[:, 0:16], 0.0), w=["Uraw"])
        for q4 in range(4):
            for bi in range(4):
                n = q4 * 4 + bi
                si = wslot()
                wv = wring[si][:, 0:NKC * 176].rearrange("p (k c) -> p k c", k=NKC)
                wg = wring[si][0:RBW, NKC * 176:NKC * 176 + 176]
                sch.op("pool", lambda e, wv=wv, n=n: e.dma_start(
                    out=wv[:, :, 0:RBW], in_=rg_w_in[j, :, n * RBW:(n + 1) * RBW].rearrange("(k p) c -> p k c", p=128)),
                    w=[("w", si, 0), ("w", si, 1), ("w", si, 2), ("w", si, 3)], dma=True)
                sch.op("pool", lambda e, wv=wv, n=n: e.dma_start(
                    out=wv[:, :, RBW:2 * RBW], in_=rg_w_in[j, :, RNN + n * RBW:RNN + (n + 1) * RBW].rearrange("(k p) c -> p k c", p=128)),
                    w=[("w", si, 1)], dma=True)
                sch.op("pool", lambda e, wg=wg, n=n: e.dma_start(out=wg[:, 0:RBW], in_=rg_w_a[j, n, :, :]), w=[("w", si, 2)], dma=True)
                sch.op("pool", lambda e, wg=wg, n=n: e.dma_start(out=wg[:, RBW:2 * RBW], in_=rg_w_x[j, n, :, :]), w=[("w", si, 3)], dma=True)
                for tt in range(NTT):
                    tok = slice(tt * TT, (tt + 1) * TT)
                    pg, pu = (0, 1) if tt % 2 == 0 else (2, 3)
                    for k in range(NKC):
                        sch.op("pe", lambda e, wv=wv, k=k, tok=tok, pg=pg: e.matmul(
                            ps[pg][0:RBW, :], lhsT=wv[:, k, 0:RBW], rhs=hh[:, k, tok], start=(k == 0), stop=(k == NKC - 1)),
                            r=[("w", si, 0), ("hh", k, tt)], w=[("ps", pg)])
                    for k in range(NKC):
                        sch.op("pe", lambda e, wv=wv, k=k, tok=tok, pu=pu: e.matmul(
                            ps[pu][0:RBW, :], lhsT=wv[:, k, RBW:2 * RBW], rhs=hh[:, k, tok], start=(k == 0), stop=(k == NKC - 1)),
                            r=[("w", si, 1), ("hh", k, tt)], w=[("ps", pu)])
                    sch.op("act", lambda e, pg=pg, tok=tok: e.activation(out=G[:, tok], in_=ps[pg][0:RBW, :], func=AF.Gelu_apprx_tanh),
                           r=[("ps", pg)], w=[("G", tt)])
                    sch.op("dve", lambda e, pu=pu, tt=tt: e.tensor_copy(out=Uraw[:, 16 + tt * TT:16 + (tt + 1) * TT], in_=ps[pu][0:RBW, :]),
                           r=[("ps", pu)], w=["Uraw"])
                sch.op("dve", lambda e, n=n: e.tensor_scalar(out=Up[:, :], in0=Uraw[:, 16:16 + S], scalar1=rgv[:, 48 + n:49 + n],
                                                          scalar2=rgv[:, 64 + n:65 + n], op0=ALU.mult, op1=ALU.add),
                       r=["Uraw", "rgv"], w=["Up"])
                for a in range(3):
                    sch.op("dve", lambda e, n=n, a=a: e.scalar_tensor_tensor(
                        out=Up[:, :], in0=Uraw[:, 13 + a:13 + a + S], scalar=rgv[:, a * 16 + n:a * 16 + n + 1], in1=Up[:, :],
                        op0=ALU.mult, op1=ALU.add), r=["Uraw", "rgv", "Up"], w=["Up"])
                sch.op("act", lambda e: e.copy(out=Upb[:, :], in_=Up[:, :]), r=["Up"], w=["Upb"])
                for tt in range(NTT):
                    tok = slice(tt * TT, (tt + 1) * TT)
                    pr, pi = (4, 5) if tt % 2 == 0 else (6, 7)
                    sch.op("pe", lambda e, wg=wg, tok=tok, pr=pr: e.matmul(ps[pr][0:RBW, :], lhsT=wg[:, 0:RBW], rhs=Upb[:, tok],
                                                                       start=True, stop=True),
                           r=[("w", si, 2), "Upb"], w=[("ps", pr)])
                    sch.op("pe", lambda e, wg=wg, tok=tok, pi=pi: e.matmul(ps[pi][0:RBW, :], lhsT=wg[:, RBW:2 * RBW], rhs=Upb[:, tok],
                                                                       start=True, stop=True),
                           r=[("w", si, 3), "Upb"], w=[("ps", pi)])
                    sch.op("act", lambda e, n=n, tok=tok, pr=pr: e.activation(out=RA[:, tok], in_=ps[pr][0:RBW, :], func=AF.Sigmoid,
                                                                         bias=rgv[:, 80 + n:81 + n], scale=1.0),
                           r=[("ps", pr), "rgv"], w=[("RA", tt)])
                    sch.op("act", lambda e, n=n, tok=tok, pi=pi: e.activation(out=IB[:, tok], in_=ps[pi][0:RBW, :], func=AF.Sigmoid,
                                                                         bias=rgv[:, 96 + n:97 + n], scale=1.0),
                           r=[("ps", pi), "rgv"], w=[("IB", tt)])
                allRA = [("RA", tt) for tt in range(NTT)]
                allIB = [("IB", tt) for tt in range(NTT)]
                sch.op("act", lambda e, n=n: e.activation(out=RA[:, :], in_=RA[:, :], func=AF.Exp, scale=rgv[:, 112 + n:113 + n]),
                       r=allRA + ["rgv"], w=allRA)
                sch.op("dve", lambda e: e.tensor_tensor(out=MH[:, :], in0=RA[:, :], in1=RA[:, :], op=ALU.mult), r=allRA, w=["MH"])
                sch.op("act", lambda e: e.activation(out=MH[:, :], in_=MH[:, :], func=AF.Sqrt, scale=-1.0, bias=1.0), r=["MH"], w=["MH"])
                sch.op("dve", lambda e: e.tensor_tensor(out=IB[:, :], in0=IB[:, :], in1=Up[:, :], op=ALU.mult), r=allIB + ["Up"], w=allIB)
                sch.op("dve", lambda e: e.tensor_tensor(out=IB[:, :], in0=IB[:, :], in1=MH[:, :], op=ALU.mult), r=allIB + ["MH"], w=allIB)
                sch.op("dve", lambda e: e.tensor_tensor_scan(out=MH[:, :], data0=RA[:, :], data1=IB[:, :], initial=0.0,
                                                            op0=ALU.mult, op1=ALU.add), r=allRA + allIB + ["MH"], w=["MH"])
                sch.op("dve", lambda e, bi=bi: e.tensor_tensor(out=ybuf[:, bi, :], in0=MH[:, :], in1=G[:, :], op=ALU.mult),
                       r=["MH"] + [("G", tt) for tt in range(NTT)], w=[("ybuf", bi)])
            so = wslot()
            wo = wring[so][0:RBW, 0:4 * D].rearrange("p (n d) -> p n d", n=4)
            for bi in range(4):
                n = q4 * 4 + bi
                sch.op("pool", lambda e, wo=wo, bi=bi, n=n: e.dma_start(out=wo[:, bi, :], in_=rg_w_out[j, n * RBW:(n + 1) * RBW, :]),
                       w=[("w", so, 0), ("w", so, 1), ("w", so, 2), ("w", so, 3)], dma=True)
            for dc in range(NKC):
                for tt in range(NTT):
                    tok = slice(tt * TT, (tt + 1) * TT)
                    pb = (dc * NTT + tt) % 4
                    for bi in range(4):
                        sch.op("pe", lambda e, wo=wo, bi=bi, dc=dc, tok=tok, pb=pb: e.matmul(
                            ps[pb][:, :], lhsT=wo[:, bi, dc * 128:(dc + 1) * 128], rhs=ybuf[:, bi, tok], start=(bi == 0), stop=(bi == 3)),
                            r=[("w", so, 0), ("ybuf", bi)], w=[("ps", pb)])
                    sch.op("dve", lambda e, pb=pb, dc=dc, tok=tok: e.scalar_tensor_tensor(
                        out=xT[:, dc, tok], in0=ps[pb][:, :], scalar=modT[:, g1col + dc:g1col + dc + 1], in1=xT[:, dc, tok],
                        op0=ALU.mult, op1=ALU.add),
                        r=[("ps", pb), "modT", ("xT", dc, tt)], w=[("xT", dc, tt)])

    def wkeys(si):
        return [("w", si, i) for i in range(4)]

    def nsa(l):
        j = l // 2
        g1col = 16
        O_W01, O_CM, O_SA, O_AM, O_Q = 0, 2816, 6912, 7040, 9088
        O_KS, O_KW, O_KC, O_VC, O_X, O_Y = 25472, 29568, 33664, 37760, 41856, 45952
        O_VS, O_VW, O_KCN, O_VCA, O_ACC, O_T = 47488, 51584, 55680, 55936, 56192, 64384

        def rb(off, p0, p1, n):
            return arena[p0:p1, off // 2: off // 2 + n]

        def rf(off, p0, p1, n):
            return arena[p0:p1, off // 2: off // 2 + 2 * n].bitcast(F32)

        W01 = rb(O_W01, 0, 128, 1408)
        cm01 = rb(O_CM, 0, 128, 2048)
        selaug = rb(O_SA, 0, 128, 33)
        addmask = rf(O_AM, 0, 128, 512).rearrange("p (t s) -> p t s", s=32)
        qn = rb(O_Q, 0, 64, 8192).rearrange("p (h t) -> p h t", h=4)
        qx = rb(O_Q, 0, 128, 8192).rearrange("p (h t) -> p h t", h=4)
        nsT = rb(O_Q, 64, 96, 8192).rearrange("p (h t) -> p h t", h=4)
        ks = rb(O_KS, 0, 64, 2048)
        kse = rb(O_KS, 0, 128, 2048)
        emat = rb(O_KS, 64, 96, 2048)
        kw = rb(O_KW, 0, 64, 2048)
        kcraw = rb(O_KC, 0, 64, 2048)
        vcraw = rb(O_VC, 0, 64, 2048)
        oT = [rb(O_KW, 64, 128, 2048), rb(O_KC, 64, 128, 2048), rb(O_VC, 64, 128, 2048), rb(O_X, 64, 128, 2048)]
        sgT = rb(O_X, 0, 12, 2048)
        gsel = rb(O_Y, 0, 12, 768).rearrange("p (a m) -> p a m", m=64)
        Vs = rb(O_VS, 0, 128, 2048).rearrange("p (t c) -> p t c", c=128)
        Vw = rb(O_VW, 0, 128, 2048).rearrange("p (t c) -> p t c", c=128)
        kcn = rb(O_KCN, 0, 64, 128)
        vca = rb(O_VCA, 0, 128, 128)
        acc = rf(O_ACC, 0, 64, 2048).rearrange("p (h t) -> p h t", h=4)
        sq = [rb(O_T + i * 1024, 0, 128, 512) for i in range(2)]
        rinv = [rf(O_T + 2048 + i * 2048, 0, 128, 512) for i in range(2)]
        h1 = rb(O_T + 6144, 0, 128, 256).rearrange("p (a n) -> p a n", a=2)
        stg = rf(O_T + 6656, 0, 32, 128)
        PT = [rb(O_T + i * 1024, 0, 128, 512) for i in range(2)]
        rden = rf(O_T + 2048, 0, 64, 512)
        fac = rf(O_T + 4096, 0, 64, 512)
        tmpo = rf(O_T + 6144, 0, 64, 512)
        impacc = rf(O_T + 8192, 0, 128, 128).rearrange("p (q s) -> p q s", s=32)
        rcp = rf(O_T + 8704, 0, 128, 4)
        top8 = rf(O_T + 8720, 0, 128, 8)
        impm = rf(O_T + 8752, 0, 128, 32)
        selm = rb(O_T + 8880, 0, 128, 32)
        dn = rf(O_T + 8944, 0, 128, 4)

        sch.barrier(streams=("pe", "act", "dve", "pool", "sp"))
        sch.op("pool", lambda e: e.dma_start(out=W01, in_=c_w01[:, :]), w=["W01"], dma=True)
        sch.op("pool", lambda e: e.dma_start(out=cm01, in_=c_cm01[:, :]), w=["cm01"], dma=True)
        sch.op("pool", lambda e: e.dma_start(out=selaug, in_=c_selaug[:, :]), w=["selaug"], dma=True)
        sch.op("dve", lambda e: e.memset(rb(O_KS, 64, 128, 2048), 0.0), w=["emat"])
        sch.op("dve", lambda e: e.memset(rb(O_Q, 64, 128, 8192), 0.0), w=["nsT_init"])
        sch.op("pool", lambda e: e.dma_start(out=emat, in_=c_emat[:, :]), w=["emat"], dma=True)
        sch.op("pool", lambda e: e.dma_start(out=gsel, in_=c_gsel[:, :].rearrange("p (a m) -> p a m", m=64)), w=["gsel"], dma=True)
        sch.op("sp", lambda e: e.dma_start(out=rf(O_AM, 0, 128, 512), in_=c_addmask[:, :]), w=["addmask"], dma=True)
        for kv in range(2):
            sch.op("pool", lambda e, kv=kv: e.dma_start(out=w2sb[:, kv, :, :],
                                                        in_=nsa_w2[j, kv, :, :].rearrange("(h p) d -> p h d", p=128)),
                   w=[("w2sb", kv)], dma=True)
        sch.op("dve", lambda e: e.memset(Vs[:, :, 64:128], 1.0), w=["VsOnes"])
        sch.op("dve", lambda e: e.memset(Vw[:, :, 64:128], 1.0), w=["VwOnes"])
        sch.op("dve", lambda e: e.memset(vca[:, :], 0.0), w=["vca"])
        sch.op("dve", lambda e: e.memset(vca[:, 64:128], 1.0), w=["vca"])
        sch.op("dve", lambda e: e.memset(kcn[:, :], 0.0), w=["kcn"])
        for hf in range(2):
            sch.op("sp", lambda e, hf=hf: e.dma_start(out=stg[0:1, hf * 64:(hf + 1) * 64], in_=nsa_qg[j, :, :]), w=["stg"], dma=True)
            sch.op("sp", lambda e, hf=hf: e.dma_start(out=stg[1:4, hf * 64:(hf + 1) * 64], in_=nsa_kg[j, :, :]), w=["stg"], dma=True)
        sch.op("pe", lambda e: e.transpose(out=ps[7][:, 0:4], in_=stg[0:4, :], identity=ident[0:4, 0:4]),
               r=["stg", "ident"], w=[("ps", 7)])
        sch.op("dve", lambda e: e.tensor_copy(out=gains[:, :], in_=ps[7][:, 0:4]), r=[("ps", 7)], w=["gains"])
        for kv in range(2):
            sch.op("sp", lambda e, kv=kv: e.dma_start(out=stg[0:32, 0:64], in_=nsa_pos[j, kv, :, :]), w=["stg"], dma=True)
            sch.op("pe", lambda e: e.transpose(out=ps[7][0:64, 0:32], in_=stg[0:32, 0:64], identity=ident[0:32, 0:32]),
                   r=["stg", "ident"], w=[("ps", 7)])
            sch.op("dve", lambda e, kv=kv: e.tensor_copy(out=posT[:, kv, :], in_=ps[7][0:64, 0:32]), r=[("ps", 7)],
                   w=[("posT", kv)])
        sch.barrier(streams=("pe", "act", "dve", "pool", "sp"))

        cnt = {"st": 0, "ot": 0, "cp": 0}
        if DBG.get("nsa_stop") == "setup":
            return

        def finalize(h, I, b, otb, qs):
            sch.op("dve", lambda e: e.tensor_scalar(out=rden[:, :], in0=ps[otb][64:128, :], scalar1=1e-18, scalar2=None,
                                                    op0=ALU.max), r=[("ps", otb)], w=["rden"])
            sch.op("act", lambda e: e.activation(out=rden[:, :], in_=rden[:, :], func=AF.Ln), r=["rden"], w=["rden"])
            sch.op("act", lambda e: e.activation(out=rden[:, :], in_=rden[:, :], func=AF.Exp, scale=-1.0), r=["rden"], w=["rden"])
            sch.op("pe", lambda e: e.matmul(ps[4][0:64, :], lhsT=gsel[:, h * 3 + b, :], rhs=sgT[:, qs], start=True, stop=True),
                   r=["gsel"] + [("sgT", I)], w=[("ps", 4)])
            sch.op("dve", lambda e: e.tensor_tensor(out=fac[:, :], in0=ps[4][0:64, :], in1=rden[:, :], op=ALU.mult),
                   r=[("ps", 4), "rden"], w=["fac"])
            if b == 0:
                sch.op("dve", lambda e: e.tensor_tensor(out=acc[:, h, :], in0=ps[otb][0:64, :], in1=fac[:, :], op=ALU.mult),
                       r=[("ps", otb), "fac"], w=[("acc", h)])
            elif b == 1:
                sch.op("dve", lambda e: e.tensor_tensor(out=tmpo[:, :], in0=ps[otb][0:64, :], in1=fac[:, :], op=ALU.mult),
                       r=[("ps", otb), "fac"], w=["tmpo"])
                sch.op("dve", lambda e: e.tensor_tensor(out=acc[:, h, :], in0=acc[:, h, :], in1=tmpo[:, :], op=ALU.add),
                       r=[("acc", h), "tmpo"], w=[("acc", h)])
            else:
                sch.op("dve", lambda e: e.tensor_tensor(out=tmpo[:, :], in0=ps[otb][0:64, :], in1=fac[:, :], op=ALU.mult),
                       r=[("ps", otb), "fac"], w=["tmpo"])
                sch.op("dve", lambda e: e.tensor_tensor(out=oT[h][:, qs], in0=acc[:, h, :], in1=tmpo[:, :], op=ALU.add),
                       r=[("acc", h), "tmpo"], w=[("oT", h, I)])

        def do_group(g):
            sA = wslot()
            wA = wring[sA][:, 0:4096].rearrange("p (k c) -> p k c", k=NKC)
            colsA = [(0, 256, g * 256), (256, 64, 1536 + g * 64), (320, 64, 2048 + g * 64),
                     (384, 64, 1024 + g * 64), (448, 64, 1280 + g * 64)]
            for (d0, n, c0) in colsA:
                sch.op("pool", lambda e, d0=d0, n=n, c0=c0: e.dma_start(
                    out=wA[:, :, d0:d0 + n], in_=nsa_w_in[j, :, c0:c0 + n].rearrange("(k p) c -> p k c", p=128)),
                    w=wkeys(sA), dma=True)
            sB = wslot()
            wB = wring[sB][:, 0:NKC * 176].rearrange("p (k c) -> p k c", k=NKC)
            colsB = [(0, 64, 1792 + g * 64), (64, 64, 2304 + g * 64), (128, 48, 2560)]
            for (d0, n, c0) in colsB:
                sch.op("pool", lambda e, d0=d0, n=n, c0=c0: e.dma_start(
                    out=wB[:, :, d0:d0 + n], in_=nsa_w_in[j, :, c0:c0 + n].rearrange("(k p) c -> p k c", p=128)),
                    w=wkeys(sB), dma=True)
            nit = [0]

            def proj_pair(tt, p, dstA, gcA, keyA, dstB, gcB, keyB):
                tok = slice(tt * TT, (tt + 1) * TT)
                it = nit[0]
                nit[0] += 1
                pb = it % 4
                ssb = 4 + (it % 2)
                si_ = it % 2
                for k in range(NKC):
                    sch.op("pe", lambda e, k=k: e.matmul(
                        ps[pb][:, :], lhsT=wA[:, k, p * 128:(p + 1) * 128], rhs=hh[:, k, tok], start=(k == 0), stop=(k == NKC - 1)),
                        r=[("w", sA, 0), ("hh", k, tt)], w=[("ps", pb)])
                if gcA is not None:
                    sch.op("act", lambda e: e.activation(out=sq[si_], in_=ps[pb][:, :], func=AF.Square),
                           r=[("ps", pb)], w=[("sq", si_)])
                    sch.op("pe", lambda e: e.matmul(ps[ssb][:, :], lhsT=blockones[:, :], rhs=sq[si_], start=True, stop=True),
                           r=[("sq", si_), "blockones"], w=[("ps", ssb)])
                    sch.op("act", lambda e: e.activation(out=rinv[si_], in_=ps[ssb][:, :], func=AF.Ln, scale=1.0 / 64, bias=EPS),
                           r=[("ps", ssb)], w=[("rinv", si_)])
                    sch.op("act", lambda e: e.activation(out=rinv[si_], in_=rinv[si_], func=AF.Exp, scale=-0.5),
                           r=[("rinv", si_)], w=[("rinv", si_)])
                    sch.op("dve", lambda e: e.scalar_tensor_tensor(
                        out=dstA, in0=ps[pb][0:64, :], scalar=gains[0:64, gcA:gcA + 1], in1=rinv[si_][0:64, :],
                        op0=ALU.mult, op1=ALU.mult),
                        r=[("ps", pb), "gains", ("rinv", si_)], w=[keyA])
                    sch.op("dve", lambda e: e.scalar_tensor_tensor(
                        out=dstB, in0=ps[pb][64:128, :], scalar=gains[64:128, gcB:gcB + 1], in1=rinv[si_][64:128, :],
                        op0=ALU.mult, op1=ALU.mult),
                        r=[("ps", pb), "gains", ("rinv", si_)], w=[keyB])
                else:
                    sch.op("act", lambda e: e.copy(out=dstA, in_=ps[pb][0:64, :]), r=[("ps", pb)], w=[keyA])
                    sch.op("dve", lambda e: e.tensor_scalar(out=dstB, in0=ps[pb][64:128, :], scalar1=1.0, scalar2=None, op0=ALU.mult),
                           r=[("ps", pb)], w=[keyB])

            def proj_gates(tt):
                tok = slice(tt * TT, (tt + 1) * TT)
                for k in range(NKC):
                    sch.op("pe", lambda e, k=k: e.matmul(ps[7][0:12, :], lhsT=wB[:, k, 128 + g * 12:140 + g * 12], rhs=hh[:, k, tok],
                                                         start=(k == 0), stop=(k == NKC - 1)),
                           r=[("w", sB, 0), ("hh", k, tt)], w=[("ps", 7)])
                sch.op("act", lambda e: e.activation(out=sgT[:, tok], in_=ps[7][0:12, :], func=AF.Sigmoid),
                       r=[("ps", 7)], w=[("sgT", tt)])

            for tt in range(0 if not DBG.get("proj_nomm") else NTT, DBG.get("proj_ntt", NTT)):
                tok = slice(tt * TT, (tt + 1) * TT)
                proj_pair(tt, 0, qn[:, 0, tok], 0, ("qn", 0, tt), qn[:, 1, tok], 0, ("qn", 1, tt))
                proj_pair(tt, 1, qn[:, 2, tok], 0, ("qn", 2, tt), qn[:, 3, tok], 0, ("qn", 3, tt))
                proj_pair(tt, 2, ks[:, tok], 2, ("ks", tt), kw[:, tok], 3, ("kw", tt))
                proj_pair(tt, 3, kcraw[:, tok], None, "kcraw", vcraw[:, tok], None, "vcraw")
                if not DBG.get("proj_nogates"):
                    proj_gates(tt)

            vtmp = rf(O_T + 7168, 0, 128, 512)

            def proj_v(tt):
                tok = slice(tt * TT, (tt + 1) * TT)
                it = nit[0]
                nit[0] += 1
                pb = it % 4
                for k in range(NKC):
                    sch.op("pe", lambda e, k=k: e.matmul(
                        ps[pb][:, :], lhsT=wB[:, k, 0:128], rhs=hh[:, k, tok], start=(k == 0), stop=(k == NKC - 1)),
                        r=[("w", sB, 0), ("hh", k, tt)], w=[("ps", pb)])
                sch.op("act", lambda e: e.copy(out=vtmp[:, :], in_=ps[pb][:, :]), r=[("ps", pb)], w=["vtmp"])
                for jq in range(4):
                    sch.op("pe", lambda e, jq=jq: e.transpose(out=ps[6][:, jq * 128:(jq + 1) * 128], in_=vtmp[:, jq * 128:(jq + 1) * 128],
                                                            identity=ident[:, :]),
                           r=["vtmp", "ident"], w=[("ps", 6)])
                p6 = ps[6][:, :].rearrange("p (q c) -> p q c", c=128)
                if DBG.get("v_nocopy"):
                    return
                sch.op("act", lambda e: e.copy(out=Vs[:, 4 * tt:4 * tt + 4, 0:64], in_=p6[:, :, 0:64]), r=[("ps", 6), "VsOnes"],
                       w=[("Vs", 4 * tt + q_) for q_ in range(4)])
                if DBG.get("v_onecopy"):
                    return
                sch.op("act", lambda e: e.copy(out=Vw[:, 4 * tt:4 * tt + 4, 0:64], in_=p6[:, :, 64:128]), r=[("ps", 6), "VwOnes"],
                       w=[("Vw", 4 * tt + q_) for q_ in range(4)])

            for tt in range(NTT):
                if not DBG.get("proj_nov"):
                    proj_v(tt)
            if DBG.get("nsa_stop") == "proj":
                sch.barrier()
                return

            def compress(kv):
                raw = kcraw if kv == 0 else vcraw
                rawkey = "kcraw" if kv == 0 else "vcraw"
                raw3 = raw.rearrange("p (n s) -> p n s", s=16)
                w1 = []
                for hf in range(2):
                    s1 = wslot()
                    wv1 = wring[s1][0:64, 0:4096].rearrange("p (t c) -> p t c", t=16)
                    sch.op("pool", lambda e, wv1=wv1, hf=hf: e.dma_start(
                        out=wv1, in_=nsa_w1[j, kv, hf * 1024:(hf + 1) * 1024, :].rearrange("(t p) c -> p t c", p=64)),
                        w=wkeys(s1), dma=True)
                    w1.append((s1, wv1))

                def c_half(half):
                    for tau in range(32):
                        s1, wv1 = w1[tau // 16]
                        wsrc = wv1[:, tau % 16, half * 128:(half + 1) * 128]
                        n0, sidx = (0, tau) if tau < 16 else (1, tau - 16)
                        sch.op("pe", lambda e, wsrc=wsrc, n0=n0, sidx=sidx, tau=tau: e.matmul(
                            ps[half][:, 0:127], lhsT=wsrc, rhs=raw3[:, n0:n0 + 127, sidx], start=(tau == 0), stop=(tau == 31)),
                            r=[("w", s1, 0), rawkey], w=[("ps", half)])
                        sch.op("pe", lambda e, wsrc=wsrc, tau=tau: e.matmul(
                            ps[2][:, half:half + 1], lhsT=wsrc, rhs=posT[:, kv, tau:tau + 1], start=(tau == 0), stop=(tau == 31)),
                            r=[("w", s1, 0), ("posT", kv)], w=[("ps", 2)])
                    col = kv * 2 + half
                    sch.op("dve", lambda e: e.tensor_copy(out=bias_sb[:, col:col + 1], in_=ps[2][:, half:half + 1]),
                           r=[("ps", 2)], w=[("bias_sb", col)])
                    sch.op("act", lambda e: e.activation(out=h1[:, half, 0:127], in_=ps[half][:, 0:127],
                                                         func=AF.Silu, bias=bias_sb[:, col:col + 1], scale=1.0),
                           r=[("ps", half), ("bias_sb", col)], w=[("h1", half)])

                c_half(0)
                c_half(1)
                if kv == 0:
                    for half in range(2):
                        sch.op("pe", lambda e, half=half: e.matmul(ps[3][0:64, 0:127], lhsT=w2sb[:, 0, half, :], rhs=h1[:, half, 0:127],
                                                                 start=(half == 0), stop=(half == 1)),
                               r=[("w2sb", 0), ("h1", half)], w=[("ps", 3)])
                    sch.op("act", lambda e: e.activation(out=sq[0][0:64, 0:127], in_=ps[3][0:64, 0:127], func=AF.Square),
                           r=[("ps", 3)], w=[("sq", 0)])
                    sch.op("pe", lambda e: e.matmul(ps[4][0:64, 0:127], lhsT=onesb[0:64, 0:64], rhs=sq[0][0:64, 0:127], start=True, stop=True),
                           r=[("sq", 0), "onesb"], w=[("ps", 4)])
                    sch.op("act", lambda e: e.activation(out=rinv[0][0:64, 0:127], in_=ps[4][0:64, 0:127], func=AF.Ln,
                                                         scale=1.0 / 64, bias=EPS), r=[("ps", 4)], w=[("rinv", 0)])
                    sch.op("act", lambda e: e.activation(out=rinv[0][0:64, 0:127], in_=rinv[0][0:64, 0:127], func=AF.Exp, scale=-0.5),
                           r=[("rinv", 0)], w=[("rinv", 0)])
                    sch.op("dve", lambda e: e.scalar_tensor_tensor(out=kcn[:, 0:127], in0=ps[3][0:64, 0:127], scalar=gains[0:64, 1:2],
                                                                  in1=rinv[0][0:64, 0:127], op0=ALU.mult, op1=ALU.mult),
                           r=[("ps", 3), "gains", ("rinv", 0)], w=["kcn"])
                else:
                    for half in range(2):
                        sch.op("pe", lambda e, half=half: e.matmul(ps[3][0:127, 0:64], lhsT=h1[:, half, 0:127], rhs=w2sb[:, 1, half, :],
                                                                 start=(half == 0), stop=(half == 1)),
                               r=[("w2sb", 1), ("h1", half)], w=[("ps", 3)])
                    sch.op("act", lambda e: e.copy(out=vca[0:127, 0:64], in_=ps[3][0:127, 0:64]), r=[("ps", 3)], w=["vca"])

            compress(0)
            compress(1)
            sch.barrier()
            if DBG.get("nsa_stop") == "compress":
                return

            def comp_head(I, h):
                qs = slice(512 * I, 512 * I + 512)
                stb = cnt["st"] % 2
                cnt["st"] += 1
                otb = 2 + cnt["ot"] % 2
                cnt["ot"] += 1
                sch.op("pe", lambda e: e.matmul(ps[stb][:, :], lhsT=kcn[:, :], rhs=qn[:, h, qs], start=True, stop=True),
                       r=["kcn", ("qn", h, I)], w=[("ps", stb)])
                sch.op("act", lambda e: e.activation(out=PT[stb], in_=ps[stb][:, :], func=AF.Exp, scale=0.125),
                       r=[("ps", stb)], w=[("PT", stb)])
                sch.op("pool", lambda e: e.tensor_tensor(out=PT[stb], in0=PT[stb], in1=cm01[:, qs], op=ALU.mult),
                       r=[("PT", stb), "cm01"], w=[("PT", stb)])
                sch.op("pe", lambda e: e.matmul(ps[otb][:, :], lhsT=vca[:, :], rhs=PT[stb], start=True, stop=True),
                       r=["vca", ("PT", stb)], w=[("ps", otb)])
                for qb in range(4):
                    sch.op("pe", lambda e, qb=qb: e.matmul(ps[5][:, qb * 33:(qb + 1) * 33], lhsT=PT[stb][:, qb * 128:(qb + 1) * 128],
                                                           rhs=selaug[:, 0:33], start=True, stop=True),
                           r=[("PT", stb), "selaug"], w=[("ps", 5)])
                imp3 = ps[5][:, 0:132].rearrange("p (q c) -> p q c", c=33)
                sch.op("dve", lambda e: e.tensor_scalar(out=dn[:, :], in0=imp3[:, :, 32], scalar1=1e-30, scalar2=None, op0=ALU.max),
                       r=[("ps", 5)], w=["dn"])
                sch.op("dve", lambda e: e.reciprocal(out=rcp[:, :], in_=dn[:, :]), r=["dn"], w=["rcp"])
                for qb in range(4):
                    if h == 0:
                        sch.op("dve", lambda e, qb=qb: e.tensor_scalar(out=impacc[:, qb, :], in0=imp3[:, qb, 0:32],
                                                                       scalar1=rcp[:, qb:qb + 1], scalar2=None, op0=ALU.mult),
                               r=[("ps", 5), "rcp"], w=[("impacc", qb)])
                    else:
                        sch.op("dve", lambda e, qb=qb: e.scalar_tensor_tensor(
                            out=impacc[:, qb, :], in0=imp3[:, qb, 0:32], scalar=rcp[:, qb:qb + 1], in1=impacc[:, qb, :],
                            op0=ALU.mult, op1=ALU.add), r=[("ps", 5), "rcp", ("impacc", qb)], w=[("impacc", qb)])
                finalize(h, I, 0, otb, qs)

            def topk(I, qb):
                t = 4 * I + qb
                sch.op("dve", lambda e: e.tensor_tensor(out=impm[:, :], in0=impacc[:, qb, :], in1=addmask[:, t, :], op=ALU.add),
                       r=[("impacc", qb), "addmask"], w=["impm"])
                sch.op("dve", lambda e: e.max(out=top8[:, :], in_=impm[:, :]), r=["impm"], w=["top8"])
                sch.op("dve", lambda e: e.tensor_scalar(out=selm[:, :], in0=impm[:, :], scalar1=top8[:, 7:8], scalar2=-1.0,
                                                        op0=ALU.is_ge, op1=ALU.add), r=["impm", "top8"], w=["selm"])
                sch.op("pe", lambda e: e.matmul(ps[6][0:32, 0:128], lhsT=selm[:, :], rhs=ident30k[:, :], start=True, stop=True),
                       r=["selm", "ident30k"], w=[("ps", 6)])
                for h in range(4):
                    if True:
                        sch.op("act", lambda e, h=h: e.copy(out=nsT[:, h, t * 128:(t + 1) * 128], in_=ps[6][0:32, 0:128]),
                               r=[("ps", 6)], w=[("nsT", h, t)])
                    else:
                        sch.op("dve", lambda e, h=h: e.tensor_scalar(out=nsT[:, h, t * 128:(t + 1) * 128], in0=ps[6][0:32, 0:128],
                                                                     scalar1=1.0, scalar2=None, op0=ALU.mult),
                               r=[("ps", 6)], w=[("nsT", h, t)])

            def sel_tile(I, h, jj, otb, nj):
                r_ = jj - 4 * I
                c0 = 128 * r_ if r_ > 0 else 0
                stb = cnt["st"] % 2
                cnt["st"] += 1
                sch.op("pe", lambda e: e.matmul(
                    ps[stb][:, c0:512], lhsT=kse[:, jj * 128:(jj + 1) * 128], rhs=qx[:, h, 512 * I + c0:512 * I + 512],
                    start=True, stop=True),
                    r=[("ks", jj // 4), "emat", ("qn", h, I)] + [("nsT", h, 4 * I + q_) for q_ in range(4)], w=[("ps", stb)])
                sch.op("act", lambda e: e.activation(out=PT[stb][:, c0:512], in_=ps[stb][:, c0:512], func=AF.Exp, scale=0.125),
                       r=[("ps", stb)], w=[("PT", stb)])
                if r_ >= 0:
                    off = 512 * I - 128 * jj + 384
                    sch.op("pool", lambda e: e.tensor_tensor(
                        out=PT[stb][:, c0:512], in0=PT[stb][:, c0:512], in1=W01[:, off + c0:off + 512], op=ALU.mult),
                        r=[("PT", stb), "W01"], w=[("PT", stb)])
                sch.op("pe", lambda e: e.matmul(
                    ps[otb][:, c0:512], lhsT=Vs[:, jj, :], rhs=PT[stb][:, c0:512], start=(jj == 0), stop=(jj == nj - 1)),
                    r=[("Vs", jj), ("PT", stb)], w=[("ps", otb)])

            def win_tile(I, h, jj, otb, j0):
                m = jj - (4 * I - 4)
                r_lo = max(0, m - 4)
                r_hi = min(3, m)
                c0, c1 = 128 * r_lo, 128 * (r_hi + 1)
                off = 512 * I - 128 * jj + 384
                stb = cnt["st"] % 2
                cnt["st"] += 1
                sch.op("pe", lambda e: e.matmul(
                    ps[stb][:, c0:c1], lhsT=kw[:, jj * 128:(jj + 1) * 128], rhs=qn[:, h, 512 * I + c0:512 * I + c1],
                    start=True, stop=True),
                    r=[("kw", jj // 4), ("qn", h, I)], w=[("ps", stb)])
                sch.op("act", lambda e: e.activation(out=PT[stb][:, c0:c1], in_=ps[stb][:, c0:c1], func=AF.Exp, scale=0.125),
                       r=[("ps", stb)], w=[("PT", stb)])
                sch.op("pool", lambda e: e.tensor_tensor(
                    out=PT[stb][:, c0:c1], in0=PT[stb][:, c0:c1], in1=W01[:, off + c0:off + c1], op=ALU.mult),
                    r=[("PT", stb), "W01"], w=[("PT", stb)])
                sch.op("pe", lambda e: e.matmul(
                    ps[otb][:, c0:c1], lhsT=Vw[:, jj, :], rhs=PT[stb][:, c0:c1], start=(jj == j0), stop=(jj == 4 * I + 3),
                    skip_group_check=True),
                    r=[("Vw", jj), ("PT", stb)], w=[("ps", otb)])

            def sel_win_head(I, h):
                qs = slice(512 * I, 512 * I + 512)
                otb = 2 + cnt["ot"] % 2
                cnt["ot"] += 1
                nj = 4 * I + 4
                for jj in range(nj):
                    sel_tile(I, h, jj, otb, nj)
                finalize(h, I, 1, otb, qs)
                otb = 2 + cnt["ot"] % 2
                cnt["ot"] += 1
                j0 = max(0, 4 * I - 4)
                for jj in range(j0, 4 * I + 4):
                    win_tile(I, h, jj, otb, j0)
                finalize(h, I, 2, otb, qs)

            for I in range(DBG.get("nsa_nI", 4)):
                for h in range(4):
                    comp_head(I, h)
                for qb in range(4):
                    topk(I, qb)
                for h in range(4):
                    sel_win_head(I, h)

            so = wslot()
            wo = wring[so][64:128, 0:4096].rearrange("p (h d) -> p h d", h=4)
            for h in range(4):
                sch.op("pool", lambda e, h=h: e.dma_start(
                    out=wo[:, h, :], in_=nsa_w_out[j, (4 * g + h) * 64:(4 * g + h + 1) * 64, :]), w=wkeys(so), dma=True)

            def outproj(dc, tt):
                tok = slice(tt * TT, (tt + 1) * TT)
                pb = (dc * NTT + tt) % 2
                for h in range(4):
                    sch.op("pe", lambda e, h=h: e.matmul(
                        ps[pb][:, :], lhsT=wo[:, h, dc * 128:(dc + 1) * 128], rhs=oT[h][:, tok], start=(h == 0), stop=(h == 3)),
                        r=[("w", so, 0), ("oT", h, tt)], w=[("ps", pb)])
                sch.op("dve", lambda e: e.scalar_tensor_tensor(
                    out=xT[:, dc, tok], in0=ps[pb][:, :], scalar=modT[:, g1col + dc:g1col + dc + 1], in1=xT[:, dc, tok],
                    op0=ALU.mult, op1=ALU.add),
                    r=[("ps", pb), "modT", ("xT", dc, tt)], w=[("xT", dc, tt)])

            for dc in range(NKC):
                for tt in range(NTT):
                    outproj(dc, tt)
            sch.barrier()

        for g in range(DBG.get("ngroups", 4)):
            do_group(g)


    def ffn(l):
        gcol = 40
        HT = 1024
        hbuf = arena[:, 0:NHC * HT].rearrange("p (c t) -> p c t", c=NHC)
        sg = [arena_f32(NHC * HT * 2 + i * 2048, 128, [512]) for i in range(2)]
        for half in range(DBG.get("halves", 2)):
            for cp in range(DBG.get("ncp", NHC // 2)):
                si = wslot()
                wv = wring[si][:, 0:NKC * 512].rearrange("p (k c) -> p k c", k=NKC)
                srcg = ffn_w_in[l, :, cp * 256:(cp + 1) * 256].rearrange("(k p) c -> p k c", p=128)
                srcu = ffn_w_in[l, :, FFN_H + cp * 256:FFN_H + (cp + 1) * 256].rearrange("(k p) c -> p k c", p=128)
                sch.op("pool", lambda e, wv=wv, srcg=srcg: e.dma_start(out=wv[:, :, 0:256], in_=srcg),
                       w=[("w", si, 0), ("w", si, 1), ("w", si, 2), ("w", si, 3)], dma=True)
                sch.op("pool", lambda e, wv=wv, srcu=srcu: e.dma_start(out=wv[:, :, 256:512], in_=srcu),
                       w=[("w", si, 1)], dma=True)
                for ci in range(2):
                    c = cp * 2 + ci
                    for t2 in range(2):
                        tt = half * 2 + t2
                        tok = slice(tt * TT, (tt + 1) * TT)
                        pg = (2 * (ci * 2 + t2)) % 4
                        pu = pg + 1
                        for k in range(NKC):
                            sch.op("pe", lambda e, wv=wv, k=k, ci=ci, tok=tok, pg=pg: e.matmul(
                                ps[pg][:, :], lhsT=wv[:, k, ci * 128:(ci + 1) * 128], rhs=hh[:, k, tok],
                                start=(k == 0), stop=(k == NKC - 1)),
                                r=[("w", si, 0), ("hh", k, tt)], w=[("ps", pg)])
                        for k in range(NKC):
                            sch.op("pe", lambda e, wv=wv, k=k, ci=ci, tok=tok, pu=pu: e.matmul(
                                ps[pu][:, :], lhsT=wv[:, k, 256 + ci * 128:256 + (ci + 1) * 128], rhs=hh[:, k, tok],
                                start=(k == 0), stop=(k == NKC - 1)),
                                r=[("w", si, 1), ("hh", k, tt)], w=[("ps", pu)])
                        gi = (ci * 2 + t2) % 2
                        sch.op("act", lambda e, pg=pg, gi=gi: e.activation(out=sg[gi], in_=ps[pg][:, :], func=AF.Silu),
                               r=[("ps", pg)], w=[("sg", gi)])
                        sch.op("dve", lambda e, pu=pu, gi=gi, c=c, t2=t2: e.tensor_tensor(
                            out=hbuf[:, c, t2 * TT:(t2 + 1) * TT], in0=ps[pu][:, :], in1=sg[gi], op=ALU.mult),
                            r=[("ps", pu), ("sg", gi)], w=[("hbuf", c, t2)])
            for dc in range(DBG.get("ndc", NKC)):
                si = wslot()
                wv = wring[si][:, 0:NHC * 128].rearrange("p (c d) -> p c d", c=NHC)
                src = ffn_w_out[l, :, dc * 128:(dc + 1) * 128].rearrange("(c p) d -> p c d", p=128)
                sch.op("pool", lambda e, wv=wv, src=src: e.dma_start(out=wv[:, 0:11, :], in_=src[:, 0:11, :]),
                       w=[("w", si, 0), ("w", si, 1), ("w", si, 2), ("w", si, 3)], dma=True)
                sch.op("pool", lambda e, wv=wv, src=src: e.dma_start(out=wv[:, 11:22, :], in_=src[:, 11:22, :]),
                       w=[("w", si, 1)], dma=True)
                for t2 in range(2):
                    tt = half * 2 + t2
                    tok = slice(tt * TT, (tt + 1) * TT)
                    pb = 4 + ((dc * 2 + t2) % 2)
                    for c in range(NHC):
                        sch.op("pe", lambda e, wv=wv, c=c, t2=t2, pb=pb: e.matmul(
                            ps[pb][:, :], lhsT=wv[:, c, :], rhs=hbuf[:, c, t2 * TT:(t2 + 1) * TT],
                            start=(c == 0), stop=(c == NHC - 1)),
                            r=[("w", si, 0 if c < 11 else 1), ("hbuf", c, t2)], w=[("ps", pb)])
                    sch.op("dve", lambda e, pb=pb, dc=dc, tok=tok: e.scalar_tensor_tensor(
                        out=xT[:, dc, tok], in0=ps[pb][:, :], scalar=modT[:, gcol + dc:gcol + dc + 1], in1=xT[:, dc, tok],
                        op0=ALU.mult, op1=ALU.add),
                        r=[("ps", pb), "modT", ("xT", dc, tt)], w=[("xT", dc, tt)])

    for l in layers:
        adaln(l)
        if do_mixer:
            rmsnorm_mod(0)
            sch.barrier()
            if l % 2 == 1:
                rglru(l)
            else:
                nsa(l)
            sch.barrier()
        if do_ffn:
            rmsnorm_mod(1)
            sch.barrier()
            if do_ffn != "norm":
                ffn(l)
            sch.barrier()

    xo = [arena_f32(i * 4096, 128, [D]) for i in range(4)]
    for t in range(S // 128):
        bi = t % 4
        for half in range(2):
            pb = (2 * t + half) % 2
            for j in range(4):
                c = half * 4 + j
                sch.op("pe", lambda e, t=t, c=c, pb=pb, j=j: e.transpose(
                    out=ps[pb][:, j * 128:(j + 1) * 128], in_=xT[:, c, t * 128:(t + 1) * 128], identity=ident[:]),
                    r=[("xT", c, t // 4), "ident"], w=[("ps", pb)])
            eng = "act" if half == 0 else "dve"

            def cp(e, bi=bi, half=half, pb=pb, eng=eng):
                out = xo[bi][:, half * 512:(half + 1) * 512]
                if eng == "act":
                    return e.copy(out=out, in_=ps[pb][:, :])
                return e.tensor_copy(out=out, in_=ps[pb][:, :])
            sch.op(eng, cp, r=[("ps", pb)], w=[("xo", bi, half)])
        i = sch.op("sp", lambda e, t=t, bi=bi: e.dma_start(out=y_d[t * 128:(t + 1) * 128, :], in_=xo[bi]),
                   r=[("xo", bi, 0), ("xo", bi, 1)], dma=True)
        sch.ops[i]["final"] = True

    sch.emit(es)
    es.close()
    return nc


def _structural_constants():
    kk = np.arange(128)[:, None]
    xi = np.arange(1408)[None, :]
    dlt = (xi - 384) - kk
    w01 = ((dlt >= 0) & (dlt < 512)).astype(np.float32)
    c = np.arange(128)[:, None]
    t = np.arange(S)[None, :]
    cm01 = ((c < 127) & (16 * c + 31 <= t)).astype(np.float32)
    n_c, n_s = S // 16 - 1, S // 64
    tok = np.arange(S)
    start = np.arange(n_c) * 16
    cover_c = (tok[None, :] >= start[:, None]) & (tok[None, :] < start[:, None] + 32)
    cover_s = (tok[:, None] // 64) == np.arange(n_s)[None, :]
    sm = cover_c.astype(np.float32) @ cover_s.astype(np.float32) / np.float32(32)
    selaug = np.zeros((128, 33), np.float32)
    selaug[:n_c, :32] = sm
    selaug[:n_c, 32] = 1.0
    q = np.arange(128)[:, None, None]
    tb = np.arange(16)[None, :, None]
    sb_ = np.arange(32)[None, None, :]
    cur = (128 * tb + q) // 64
    forced = (sb_ == 0) | (sb_ == cur) | (sb_ == cur - 1)
    addmask = np.where(forced, 1e30, np.where(sb_ > cur, -1e30, 0.0)).astype(np.float32).reshape(128, 512)
    emat = ((np.arange(S)[None, :] // 64) == np.arange(32)[:, None]).astype(np.float32)
    gsel = np.zeros((12, 12, 64), np.float32)
    for a in range(12):
        gsel[a, a, :] = 1.0
    return {"c_w01": w01, "c_cm01": cm01, "c_selaug": selaug, "c_addmask": addmask, "c_emat": emat,
            "c_gsel": gsel.reshape(12, 768)}


def make_in_maps(inputs):
    f = lambda a: np.ascontiguousarray(np.asarray(a, dtype=np.float32))
    shared = {
        "ada_w": f(inputs["ada_w"]),
        "ada_b": f(inputs["ada_b"]).reshape(DEPTH, 48, 128),
        "norm1_g": f(inputs["norm1_g"]).reshape(DEPTH, NKC, 128),
        "norm2_g": f(inputs["norm2_g"]).reshape(DEPTH, NKC, 128),
        "ffn_w_in": f(inputs["ffn_w_in"]),
        "ffn_w_out": f(inputs["ffn_w_out"]),
        "rg_w_in": f(inputs["rg_w_in"]),
        "rg_conv_w": f(inputs["rg_conv_w"]).reshape(2, 4, RB, RBW),
        "rg_conv_b": f(inputs["rg_conv_b"]).reshape(2, RB, RBW),
        "rg_w_a": f(inputs["rg_w_a"]),
        "rg_b_a": f(inputs["rg_b_a"]),
        "rg_w_x": f(inputs["rg_w_x"]),
        "rg_b_x": f(inputs["rg_b_x"]),
        "rg_lam": f(inputs["rg_lam"]).reshape(2, RB, RBW),
        "rg_w_out": f(inputs["rg_w_out"]),
        "ident_in": np.eye(128, dtype=np.float32),
        "nsa_w_in": f(inputs["nsa_w_in"]),
        "nsa_w_out": f(inputs["nsa_w_out"]),
        "nsa_cmp_pos": f(inputs["nsa_cmp_pos"]),
        "nsa_cmp_w1": f(inputs["nsa_cmp_w1"]),
        "nsa_cmp_w2": f(inputs["nsa_cmp_w2"]),
        "nsa_q_gain": f(inputs["nsa_q_gain"]).reshape(2, 1, 64),
        "nsa_k_gain": f(inputs["nsa_k_gain"]),
    }
    shared.update(_structural_constants())
    x = f(inputs["x"])
    c = f(inputs["c"])
    maps = []
    for b in range(8):
        m = dict(shared)
        m["x"] = x[b]
        m["c"] = c[b].reshape(NKC, 128)
        maps.append(m)
    return maps


_NC_CACHE = {}


def kernel(**inputs):
    if "nc" not in _NC_CACHE:
        _NC_CACHE["nc"] = build_program()
    nc = _NC_CACHE["nc"]
    maps = make_in_maps(inputs)
    res = run_bass_kernel_spmd(nc, maps, core_ids=list(range(8)))
    out = np.stack([np.asarray(r["y"], dtype=np.float32) for r in res.results], axis=0)
    return out
```
